# Optimizing a Trainium2 kernel written in Bass

```python
import math
import jax, jax.numpy as jnp
from jax import lax
import numpy as np

D_MODEL = 2048
BATCH = 4
SEQ = 4096
DEPTH = 2

GRID_W = 64
CTX_LEN = 256
N_EVEN = (DEPTH + 1) // 2
N_ODD = DEPTH // 2
HEAD_DIM = 128
NA_HEADS = D_MODEL // (2 * HEAD_DIM)
NA_WIN_H = 8
NA_WIN_W = 16
DIFF_HEAD_DIM = 128
DIFF_HEADS = D_MODEL // (4 * DIFF_HEAD_DIM)
NA_WIDTH = NA_HEADS * HEAD_DIM
DIFF_WIDTH = DIFF_HEADS * 2 * DIFF_HEAD_DIM
EVEN_PROJ = 3 * NA_WIDTH + 3 * DIFF_WIDTH
MLA_HEADS = D_MODEL // 128
MLA_Q_RANK = 512
MLA_KV_RANK = 512
MLA_NOPE = 128
MLA_ROPE = 64
MLA_V = 128
MLA_DOWN = MLA_Q_RANK + MLA_KV_RANK + MLA_ROPE
D_FF = 128 * ((8 * D_MODEL // 3 + 127) // 128)
CONV_W = 3
Q_BLOCK = 128
ROPE_BASE = 10000.0
EPS = 1e-6

kernel_name = 'hybrid_na_diff_mla_dit_block'


def rms_norm(x, g):
    xf = x.astype(jnp.float32)
    y = xf * lax.rsqrt(jnp.mean(xf * xf, axis=-1, keepdims=True) + EPS)
    return (y * g.astype(jnp.float32)).astype(x.dtype)


def modulate(x, g, shift, scale):
    return rms_norm(x, g) * (1 + scale) + shift


def softmax32(s):
    return jax.nn.softmax(s.astype(jnp.float32), axis=-1)


def rope_1d(x, pos):
    half = x.shape[-1] // 2
    freqs = ROPE_BASE ** (-jnp.arange(half, dtype=jnp.float32) / half)
    ang = pos.astype(jnp.float32)[:, None] * freqs[None, :]
    cos = jnp.concatenate([jnp.cos(ang), jnp.cos(ang)], -1)[None, :, None, :].astype(x.dtype)
    sin = jnp.concatenate([jnp.sin(ang), jnp.sin(ang)], -1)[None, :, None, :].astype(x.dtype)
    rot = jnp.concatenate([-x[..., half:], x[..., :half]], -1)
    return x * cos + rot * sin


def rope_2d(x):
    L = x.shape[1]
    t = jnp.arange(L, dtype=jnp.int32)
    h = x.shape[-1] // 2
    return jnp.concatenate([rope_1d(x[..., :h], t // GRID_W), rope_1d(x[..., h:], t % GRID_W)], -1)


def map_query_blocks(fn, *qs):
    B, L = qs[0].shape[:2]
    nb = L // Q_BLOCK
    xs = tuple(jnp.moveaxis(q.reshape((B, nb, Q_BLOCK) + q.shape[2:]), 1, 0) for q in qs)
    out = lax.map(lambda a: fn(*a), xs)
    out = jnp.moveaxis(out, 0, 1)
    return out.reshape((B, L) + out.shape[3:])


def sdpa(q, k, v):
    s = jnp.einsum('bqhd,bkhd->bhqk', q, k) * (q.shape[-1] ** -0.5)
    return jnp.einsum('bhqk,bkhd->bqhd', softmax32(s).astype(v.dtype), v)


def diff_attend(q1, q2, k1, k2, v, lam):
    scale = q1.shape[-1] ** -0.5
    p1 = softmax32(jnp.einsum('bqhd,bkhd->bhqk', q1, k1) * scale)
    p2 = softmax32(jnp.einsum('bqhd,bkhd->bhqk', q2, k2) * scale)
    p = (p1 - lam * p2).astype(v.dtype)
    return jnp.einsum('bhqk,bkhd->bqhd', p, v)


def neighbourhood_attention(q, k, v, kc, vc, rpb):
    B, L, H, d = q.shape
    rows = L // GRID_W
    wh = min(NA_WIN_H, rows)
    ww = NA_WIN_W
    nk = wh * ww
    scale = d ** -0.5
    r = jnp.arange(rows, dtype=jnp.int32)
    key_rows = jnp.clip(r - wh // 2, 0, rows - wh)[:, None] + jnp.arange(wh, dtype=jnp.int32)[None, :]
    cidx = jnp.arange(GRID_W, dtype=jnp.int32)
    key_cols = jnp.clip(cidx - ww // 2, 0, GRID_W - ww)[:, None] + jnp.arange(ww, dtype=jnp.int32)[None, :]
    dc = key_cols - cidx[:, None] + (NA_WIN_W - 1)
    q_rows = jnp.moveaxis(q.reshape(B, rows, GRID_W, H, d), 1, 0)

    def row_block(args):
        q_r, kr, ri = args
        idx = (kr[None, :, None] * GRID_W + key_cols[:, None, :]).reshape(GRID_W, nk)
        kg = k[:, idx]
        vg = v[:, idx]
        dr = (kr - ri + (NA_WIN_H - 1))[None, :, None]
        bias = rpb[:, dr, dc[:, None, :]].reshape(H, GRID_W, nk)
        s_win = jnp.einsum('bqhd,bqkhd->bhqk', q_r, kg) * scale + bias[None]
        s_ctx = jnp.einsum('bqhd,bkhd->bhqk', q_r, kc) * scale
        p = softmax32(jnp.concatenate([s_win, s_ctx], -1)).astype(v.dtype)
        return (jnp.einsum('bhqk,bqkhd->bqhd', p[..., :nk], vg)
                + jnp.einsum('bhqk,bkhd->bqhd', p[..., nk:], vc))

    out = lax.map(row_block, (q_rows, key_rows, r))
    return jnp.moveaxis(out, 0, 1).reshape(B, L, H, d)


def even_mixer(h, hc, w_in, w_out, qn_a, kn_a, rpb, qn_b, kn_b, diff_lam, subln_b, lam_init, with_ctx):
    def project(t):
        B, L, _ = t.shape
        qa, ka, va, qb, kb, vb = jnp.split(
            t @ w_in, [NA_WIDTH, 2 * NA_WIDTH, 3 * NA_WIDTH, 3 * NA_WIDTH + DIFF_WIDTH, 3 * NA_WIDTH + 2 * DIFF_WIDTH], axis=-1)
        qa = rms_norm(qa.reshape(B, L, NA_HEADS, HEAD_DIM), qn_a)
        ka = rms_norm(ka.reshape(B, L, NA_HEADS, HEAD_DIM), kn_a)
        va = va.reshape(B, L, NA_HEADS, HEAD_DIM)
        qb = rms_norm(qb.reshape(B, L, 2 * DIFF_HEADS, DIFF_HEAD_DIM), qn_b)
        kb = rms_norm(kb.reshape(B, L, 2 * DIFF_HEADS, DIFF_HEAD_DIM), kn_b)
        vb = vb.reshape(B, L, DIFF_HEADS, 2 * DIFF_HEAD_DIM)
        return qa, ka, va, qb, kb, vb

    def halves(t):
        t = t.reshape(t.shape[0], t.shape[1], DIFF_HEADS, 2, DIFF_HEAD_DIM)
        return t[:, :, :, 0], t[:, :, :, 1]

    def merge(oa, ob):
        ob = rms_norm(ob, subln_b) * (1 - lam_init)
        o = jnp.concatenate([oa.reshape(oa.shape[0], oa.shape[1], -1), ob.reshape(ob.shape[0], ob.shape[1], -1)], -1)
        return o @ w_out

    qa, ka, va, qb, kb, vb = project(h)
    qa_c, ka_c, va_c, qb_c, kb_c, vb_c = project(hc)
    qb = rope_2d(qb)
    kb = rope_2d(kb)
    lf = diff_lam.astype(jnp.float32)
    lam = jnp.exp(jnp.sum(lf[0] * lf[1])) - jnp.exp(jnp.sum(lf[2] * lf[3])) + lam_init

    oa = neighbourhood_attention(qa, ka, va, ka_c, va_c, rpb)
    k1, k2 = halves(jnp.concatenate([kb, kb_c], 1))
    v_all = jnp.concatenate([vb, vb_c], 1)
    q1, q2 = halves(qb)
    ob = map_query_blocks(lambda a1, a2: diff_attend(a1, a2, k1, k2, v_all, lam), q1, q2)
    y = merge(oa, ob)
    yc = None
    if with_ctx:
        q1c, q2c = halves(qb_c)
        k1c, k2c = halves(kb_c)
        yc = merge(sdpa(qa_c, ka_c, va_c), diff_attend(q1c, q2c, k1c, k2c, vb_c, lam))
    return y, yc


def odd_mixer(h, hc, w_down, q_a_norm, kv_a_norm, w_uq, w_ukv, qn_nope, qn_rope, kn_nope, kn_rope, w_out, with_ctx):
    scale = (MLA_NOPE + MLA_ROPE) ** -0.5

    def queries(down, rotate):
        B, L, _ = down.shape
        q = (rms_norm(down[..., :MLA_Q_RANK], q_a_norm) @ w_uq).reshape(B, L, MLA_HEADS, MLA_NOPE + MLA_ROPE)
        q_nope = rms_norm(q[..., :MLA_NOPE], qn_nope)
        q_rope = rms_norm(q[..., MLA_NOPE:], qn_rope)
        if rotate:
            q_rope = rope_2d(q_rope)
        return q_nope, q_rope

    def keys_values(down, rotate):
        B, L, _ = down.shape
        kv = (rms_norm(down[..., MLA_Q_RANK:MLA_Q_RANK + MLA_KV_RANK], kv_a_norm) @ w_ukv).reshape(
            B, L, MLA_HEADS, MLA_NOPE + MLA_V)
        k_nope = rms_norm(kv[..., :MLA_NOPE], kn_nope)
        k_rope = rms_norm(down[:, :, None, MLA_Q_RANK + MLA_KV_RANK:], kn_rope)
        if rotate:
            k_rope = rope_2d(k_rope)
        return k_nope, k_rope[:, :, 0], kv[..., MLA_NOPE:]

    def attend(qn, qr, kn, kr, v):
        s = (jnp.einsum('bqhd,bkhd->bhqk', qn, kn) + jnp.einsum('bqhr,bkr->bhqk', qr, kr)) * scale
        return jnp.einsum('bhqk,bkhd->bqhd', softmax32(s).astype(v.dtype), v)

    down = h @ w_down
    down_c = hc @ w_down
    qn, qr = queries(down, True)
    kn, kr, v = keys_values(down, True)
    kn_c, kr_c, v_c = keys_values(down_c, False)
    kn_all = jnp.concatenate([kn, kn_c], 1)
    kr_all = jnp.concatenate([kr, kr_c], 1)
    v_all = jnp.concatenate([v, v_c], 1)
    o = map_query_blocks(lambda a, b: attend(a, b, kn_all, kr_all, v_all), qn, qr)
    y = o.reshape(o.shape[0], o.shape[1], -1) @ w_out
    yc = None
    if with_ctx:
        qn_c, qr_c = queries(down_c, False)
        oc = attend(qn_c, qr_c, kn_c, kr_c, v_c)
        yc = oc.reshape(oc.shape[0], oc.shape[1], -1) @ w_out
    return y, yc


def conv_ffn(h, w_in, conv_w, w_out):
    u = h @ w_in
    up = jnp.pad(u, ((0, 0), (1, 1), (0, 0)))
    u = up[:, :-2] * conv_w[0] + up[:, 1:-1] * conv_w[1] + up[:, 2:] * conv_w[2]
    a, b = jnp.split(u, 2, axis=-1)
    return (jax.nn.silu(a) * b) @ w_out


def setup_inputs(seed: int = 0) -> dict:
    key = jax.random.key(seed)
    ks = jax.random.split(key, 32)
    D = D_MODEL

    def nrm(k, shape, scale):
        return jax.random.normal(k, shape, jnp.float32) * scale

    def gain(k, shape):
        return 1.0 + 0.1 * jax.random.normal(k, shape, jnp.float32)

    return {
        'x': nrm(ks[0], (BATCH, SEQ, D), 1.0),
        'c': nrm(ks[1], (BATCH, D), 1.0),
        'ctx': nrm(ks[2], (BATCH, CTX_LEN, D), 1.0),
        'c_ctx': nrm(ks[3], (D,), 1.0),
        'ada_w': nrm(ks[4], (DEPTH, D, 6 * D), 0.5 * D ** -0.5),
        'ada_b': nrm(ks[5], (DEPTH, 6 * D), 0.02),
        'norm_mix': gain(ks[6], (DEPTH, D)),
        'norm_ffn': gain(ks[7], (DEPTH, D)),
        'ffn_w_in': nrm(ks[8], (DEPTH, D, 2 * D_FF), D ** -0.5),
        'ffn_conv': nrm(ks[9], (DEPTH, CONV_W, 2 * D_FF), CONV_W ** -0.5),
        'ffn_w_out': nrm(ks[10], (DEPTH, D_FF, D), D_FF ** -0.5),
        'even_w_in': nrm(ks[11], (N_EVEN, D, EVEN_PROJ), D ** -0.5),
        'even_w_out': nrm(ks[12], (N_EVEN, NA_WIDTH + DIFF_WIDTH, D), (NA_WIDTH + DIFF_WIDTH) ** -0.5),
        'na_q_norm': gain(ks[13], (N_EVEN, HEAD_DIM)),
        'na_k_norm': gain(ks[14], (N_EVEN, HEAD_DIM)),
        'na_rpb': nrm(ks[15], (N_EVEN, NA_HEADS, 2 * NA_WIN_H - 1, 2 * NA_WIN_W - 1), 0.5),
        'diff_q_norm': gain(ks[16], (N_EVEN, DIFF_HEAD_DIM)),
        'diff_k_norm': gain(ks[17], (N_EVEN, DIFF_HEAD_DIM)),
        'diff_lambda': nrm(ks[18], (N_EVEN, 4, DIFF_HEAD_DIM), 0.1),
        'diff_subln': gain(ks[19], (N_EVEN, 2 * DIFF_HEAD_DIM)),
        'mla_w_down': nrm(ks[20], (N_ODD, D, MLA_DOWN), D ** -0.5),
        'mla_q_a_norm': gain(ks[21], (N_ODD, MLA_Q_RANK)),
        'mla_kv_a_norm': gain(ks[22], (N_ODD, MLA_KV_RANK)),
        'mla_w_uq': nrm(ks[23], (N_ODD, MLA_Q_RANK, MLA_HEADS * (MLA_NOPE + MLA_ROPE)), MLA_Q_RANK ** -0.5),
        'mla_w_ukv': nrm(ks[24], (N_ODD, MLA_KV_RANK, MLA_HEADS * (MLA_NOPE + MLA_V)), MLA_KV_RANK ** -0.5),
        'mla_q_nope_norm': gain(ks[25], (N_ODD, MLA_NOPE)),
        'mla_q_rope_norm': gain(ks[26], (N_ODD, MLA_ROPE)),
        'mla_k_nope_norm': gain(ks[27], (N_ODD, MLA_NOPE)),
        'mla_k_rope_norm': gain(ks[28], (N_ODD, MLA_ROPE)),
        'mla_w_out': nrm(ks[29], (N_ODD, MLA_HEADS * MLA_V, D), (MLA_HEADS * MLA_V) ** -0.5),
    }


def reference(x, c, ctx, c_ctx, ada_w, ada_b, norm_mix, norm_ffn, ffn_w_in, ffn_conv, ffn_w_out,
              even_w_in, even_w_out, na_q_norm, na_k_norm, na_rpb, diff_q_norm, diff_k_norm,
              diff_lambda, diff_subln, mla_w_down, mla_q_a_norm, mla_kv_a_norm, mla_w_uq, mla_w_ukv,
              mla_q_nope_norm, mla_q_rope_norm, mla_k_nope_norm, mla_k_rope_norm, mla_w_out):
    xc = ctx
    s_lat = jax.nn.silu(c)
    s_ctx = jax.nn.silu(c_ctx)
    for l in range(DEPTH):
        with_ctx = l < DEPTH - 1
        mod = (s_lat @ ada_w[l] + ada_b[l])[:, None, :]
        mod_c = s_ctx @ ada_w[l] + ada_b[l]
        sh_m, sc_m, g_m, sh_f, sc_f, g_f = jnp.split(mod, 6, axis=-1)
        shc_m, scc_m, gc_m, shc_f, scc_f, gc_f = jnp.split(mod_c, 6, axis=-1)
        h = modulate(x, norm_mix[l], sh_m, sc_m)
        hc = modulate(xc, norm_mix[l], shc_m, scc_m)
        i = l // 2
        if l % 2 == 0:
            lam_init = 0.8 - 0.6 * math.exp(-0.3 * l)
            y, yc = even_mixer(h, hc, even_w_in[i], even_w_out[i], na_q_norm[i], na_k_norm[i], na_rpb[i],
                               diff_q_norm[i], diff_k_norm[i], diff_lambda[i], diff_subln[i], lam_init, with_ctx)
        else:
            y, yc = odd_mixer(h, hc, mla_w_down[i], mla_q_a_norm[i], mla_kv_a_norm[i], mla_w_uq[i], mla_w_ukv[i],
                              mla_q_nope_norm[i], mla_q_rope_norm[i], mla_k_nope_norm[i], mla_k_rope_norm[i],
                              mla_w_out[i], with_ctx)
        x = x + g_m * y
        x = x + g_f * conv_ffn(modulate(x, norm_ffn[l], sh_f, sc_f), ffn_w_in[l], ffn_conv[l], ffn_w_out[l])
        if with_ctx:
            xc = xc + gc_m * yc
            xc = xc + gc_f * conv_ffn(modulate(xc, norm_ffn[l], shc_f, scc_f), ffn_w_in[l], ffn_conv[l], ffn_w_out[l])
    return x
```

```python
import math
from contextlib import ExitStack

import numpy as np
import concourse.bass as bass
import concourse.mybir as mybir
from concourse.bass_utils import run_bass_kernel_spmd

F32 = mybir.dt.float32
BF16 = mybir.dt.bfloat16
AF = mybir.ActivationFunctionType
ALU = mybir.AluOpType
AX = mybir.AxisListType

D = 2048
KC = 16
SEQ = 4096
CTX = 256
NOWN = 2048
NHALO = 128
NQ = NOWN + NHALO
NQT = NQ // 128
NQC = NQ + CTX
NKV = SEQ + CTX
NKT = NKV // 128
DFF = 5504
FC = DFF // 128
EPS = 1e-6
NEG = -30000.0
NA_KT = 21
LAM_INIT0 = 0.8 - 0.6 * math.exp(-0.3 * 0)


class Buf:
    __slots__ = ("name", "writers", "readers")

    def __init__(self, name=""):
        self.name = name
        self.writers = []
        self.readers = []


class DSem:
    def __init__(self, h, shared=False, step=16):
        self.h = h
        self.count = 0
        self.step = step
        self.shared = shared
        self.cell = [0]
        self.closed = False


class Ins:
    __slots__ = ("eng", "meth", "args", "kw", "deps", "marked", "val", "dsem", "cell", "raw")


class Prog:
    CENG = ("pe", "act", "dve", "pool")

    def __init__(self, nc, stack):
        self.nc = nc
        self.stack = stack
        self.engs = {"pe": nc.tensor, "act": nc.scalar, "dve": nc.vector, "pool": nc.gpsimd, "sp": nc.sync}
        self.csem = {e: stack.enter_context(nc.semaphore("cs_" + e)) for e in self.CENG}
        self.ccount = {e: 0 for e in self.CENG}
        self.bar = stack.enter_context(nc.semaphore("bar"))
        self.barcount = 0
        self.lists = {e: [] for e in self.engs}
        self.known = {e: {} for e in self.engs}
        self.bufs = []
        self.dsems = []
        self.dnext = 0
        self.ninstr = 0

    def buf(self, name=""):
        b = Buf(name)
        self.bufs.append(b)
        return b

    def dsem(self, name, shared=False):
        if shared:
            return DSem(self.stack.enter_context(self.nc.semaphore(name)), True)
        if self.dnext == len(self.dsems):
            self.dsems.append(DSem(self.stack.enter_context(self.nc.semaphore(f"dp{self.dnext}")), False))
        d = self.dsems[self.dnext]
        self.dnext += 1
        return d

    def _rec(self, eng, meth, args, kw, reads, writes, accs, dsem):
        ins = Ins()
        ins.eng, ins.meth, ins.args, ins.kw = eng, meth, args, kw
        ins.marked = False
        ins.val = 0
        ins.dsem = dsem
        ins.cell = None
        ins.raw = []
        deps = []
        for b in reads:
            deps.extend(b.writers)
        for b in writes:
            deps.extend(b.writers)
            deps.extend(b.readers)
        for b in accs:
            deps.extend(b.readers)
        seen = set()
        ins.deps = []
        for p in deps:
            if id(p) in seen:
                continue
            seen.add(id(p))
            if p.eng == "pe" and eng == "pe" and p.dsem is None:
                continue
            p.marked = True
            ins.deps.append(p)
            if p.dsem is not None and p.dsem.shared:
                p.dsem.closed = True
        for b in reads:
            b.readers.append(ins)
        for b in writes:
            b.writers = [ins]
            b.readers = []
        for b in accs:
            b.writers.append(ins)
        if dsem is not None:
            if dsem.shared:
                assert eng == "sp"
                if dsem.closed:
                    ins.raw.append((dsem.h, dsem.count))
                    dsem.cell = [0]
                    dsem.closed = False
                ins.cell = dsem.cell
            dsem.count += dsem.step
            ins.val = dsem.count
            if dsem.shared:
                dsem.cell[0] = dsem.count
            ins.marked = True
        self.lists[eng].append(ins)
        self.ninstr += 1
        return ins

    def op(self, eng, meth, *args, reads=(), writes=(), accs=(), **kw):
        return self._rec(eng, meth, args, kw, reads, writes, accs, None)

    def dma(self, eng, dsem, out, in_, reads=(), writes=(), accs=(), **kw):
        return self._rec(eng, "dma_start", (), dict(out=out, in_=in_, **kw), reads, writes, accs, dsem)

    def _token(self, p):
        if p.dsem is not None:
            return p.dsem.h, (p.cell[0] if p.cell is not None else p.val)
        return self.csem[p.eng], p.val

    def flush(self):
        for e in self.CENG:
            for ins in reversed(self.lists[e]):
                if ins.dsem is None:
                    ins.marked = True
                    break
        for e in self.CENG:
            for ins in self.lists[e]:
                if ins.dsem is None and ins.marked:
                    self.ccount[e] += 1
                    ins.val = self.ccount[e]
        self.barcount += 1
        dma_final = {e: {} for e in self.engs}
        for e in self.engs:
            for ins in self.lists[e]:
                if ins.dsem is not None:
                    dma_final[e][id(ins.dsem)] = (ins.dsem.h, ins.val)
        ndma_eng = sum(1 for e in self.engs if dma_final[e])
        bar_target_add = ndma_eng
        self._bar_total = getattr(self, "_bar_total", 0) + bar_target_add
        bar_total = self._bar_total
        cfinal = dict(self.ccount)
        lists = self.lists

        def make_body(e):
            def body(eng):
                known = self.known[e]
                for ins in lists[e]:
                    waits = {}
                    for h, v in ins.raw:
                        if known.get(id(h), 0) < v:
                            waits[id(h)] = (h, v)
                    for p in ins.deps:
                        h, v = self._token(p)
                        if known.get(id(h), 0) >= v:
                            continue
                        if id(h) not in waits or waits[id(h)][1] < v:
                            waits[id(h)] = (h, v)
                    for h, v in waits.values():
                        eng.wait_ge(h, v)
                        known[id(h)] = v
                    bi = getattr(eng, ins.meth)(*ins.args, **ins.kw)
                    if ins.dsem is not None:
                        bi.then_inc(ins.dsem.h, ins.dsem.step)
                    elif ins.marked:
                        bi.then_inc(self.csem[e], 1)
                if dma_final[e]:
                    for h, v in dma_final[e].values():
                        if known.get(id(h), 0) < v:
                            eng.wait_ge(h, v)
                            known[id(h)] = v
                    eng.sem_inc(self.bar, 1)
                for ce in self.CENG:
                    h, v = self.csem[ce], cfinal[ce]
                    if v > 0 and known.get(id(h), 0) < v:
                        eng.wait_ge(h, v)
                        known[id(h)] = v
                if bar_total > 0 and known.get(id(self.bar), 0) < bar_total:
                    eng.wait_ge(self.bar, bar_total)
                    known[id(self.bar)] = bar_total
            return body

        with self.nc.Block() as block:
            block.tensor(make_body("pe"))
            block.scalar(make_body("act"))
            block.vector(make_body("dve"))
            block.gpsimd(make_body("pool"))
            block.sync(make_body("sp"))
        self.lists = {e: [] for e in self.engs}
        self.dnext = 0
        for b in self.bufs:
            b.writers = []
            b.readers = []


SM = {}
_off = 0
for _n, _sz in (("na_q", 512), ("na_k", 512), ("df_q", 512), ("df_k", 512), ("df_lam", 512), ("df_sub", 256),
                ("m_qa", 512), ("m_kva", 512), ("m_qn", 128), ("m_qr", 64), ("m_kn", 128), ("m_kr", 64),
                ("nmix0", 2048), ("nmix1", 2048), ("nffn0", 2048), ("nffn1", 2048)):
    SM[_n] = (_off, _sz)
    _off += _sz
NSM = _off


def build_program(stage, debug=False):
    nc = bass.Bass("TRN2", target_bir_lowering=False)
    doA = "A" in stage
    doB = "B" in stage
    fused = stage == "AB"

    def din(name, shape, dt=F32):
        return nc.dram_tensor(name, list(shape), dt, kind="ExternalInput").ap()

    def dout(name, shape, dt=F32):
        return nc.dram_tensor(name, list(shape), dt, kind="ExternalOutput").ap()

    def scr(name, shape, dt):
        return nc.dram_tensor(name, list(shape), dt, kind="ExternalOutput" if debug else "Internal").ap()

    def xfer(name, shape, dt):
        if fused:
            return scr(name, shape, dt)
        if stage == "A":
            return dout(name, shape, dt)
        return din(name, shape, dt)

    ident_in = din("ident", [128, 128])
    smalls = din("smalls", [1, NSM])
    NL = 2 if fused else 1
    ffn_w_in = din("ffn_w_in", [NL, D, 2 * DFF])
    ffn_w_out = din("ffn_w_out", [NL, DFF, D])
    cwt_in = din("cwt", [NL, 128, 86 * 3])
    modS = xfer("modS", [2, 2 * 6 * D], F32)
    x1q = xfer("x1q", [NQ, D], F32)
    qlatT = xfer("qlatT", [512, NQ], BF16)
    if doA:
        kvx_own = xfer("kvx_own", [576, NOWN], BF16)
        kvx_ctx = xfer("kvx_ctx", [576, CTX], BF16)
        xl = din("xl", [SEQ, D])
        cl = din("cl", [CTX, D])
        cc_in = din("cc", [128, 32])
        ada_w = din("ada_w", [2, D, 6 * D])
        ada_b = din("ada_b", [1, 2 * 6 * D])
        even_w_in = din("even_w_in", [D, 6144])
        even_w_out = din("even_w_out", [D, D])
        rope0 = din("rope0", [SEQ, 256])
        rope1 = din("rope1", [NQ, 128])
        biasT = din("biasT", [8, 128, 3 * 5 * 128])
        mla_w_down = din("mla_w_down", [D, 1088])
        QaT = scr("QaT", [8, 128, NQC], BF16)
        KaT = scr("KaT", [8, 128, NA_KT * 128], BF16)
        Va = scr("Va", [8, NA_KT * 128, 128], BF16)
        QbT = scr("QbT", [8, 128, NQC], BF16)
        KbT = scr("KbT", [8, 128, NKV], BF16)
        Vb = scr("Vb", [4, NKV, 256], BF16)
        xmid = scr("xmid", [NQ, D], F32)
        xcmid = scr("xcmid", [CTX, D], F32)
        xc1 = scr("xc1", [CTX, D], F32)
    if doB:
        mla_w_uq = din("mla_w_uq", [512, 3072])
        mla_w_ukv = din("mla_w_ukv", [512, 4096])
        mla_w_out = din("mla_w_out", [D, D])
        if not fused:
            kvx_all = din("kvx_all", [576, NKV], BF16)
            rope1 = din("rope1", [NQ, 128])
        knT = scr("knT", [16, 128, NKV], BF16)
        V1 = scr("V1", [16, NKV, 128], BF16)
        qnT = scr("qnT", [16, 128, NQ], BF16)
        qrT = scr("qrT", [16, 64, NQ], BF16)
        xmid1 = scr("xmid1", [NQ, D], F32)
        out = dout("out", [NOWN, D], F32)
    hTf_lat = scr("hTf_lat", [D, NQ + 2], BF16)
    hTf_ctx = scr("hTf_ctx", [D, CTX + 2], BF16)

    top = ExitStack()
    P = Prog(nc, top)
    psb = [top.enter_context(nc.psum_tensor(f"ps{i}", [128, 512], F32)) for i in range(8)]
    psB = [P.buf(f"ps{i}") for i in range(8)]

    _uid = [0]

    class T:
        def __init__(self, st, name, shape, dt):
            _uid[0] += 1
            self.t = st.enter_context(nc.sbuf_tensor(f"sb{_uid[0]}_{name}", list(shape), dt))
            self.b = P.buf(name)

        def __getitem__(self, k):
            return self.t[k]

    ident = T(top, "ident", [128, 128], BF16)
    identf = T(top, "identf", [128, 128], F32)
    ds_misc = P.dsem("ds_misc", shared=True)
    P.dma("sp", ds_misc, identf[:], ident_in[:, :], writes=[identf.b])
    P.op("dve", "tensor_copy", ident[:], identf[:], reads=[identf.b], writes=[ident.b])

    def bcast_load(dst, dsem, row_ap, eng="sp"):
        P.dma(eng, dsem, dst[:], row_ap.partition_broadcast(128), writes=[dst.b])

    def small_row(name):
        o, n = SM[name]
        return smalls[0:1, o:o + n]

    def mod_row(r, l, v):
        o = l * 6 * D + v * D
        return modS[r:r + 1, o:o + D]

    _rr = [0]

    def evac(out_ap, in_ap, reads, writes=(), accs=()):
        _rr[0] ^= 1
        if _rr[0]:
            P.op("act", "activation", out=out_ap, in_=in_ap, func=AF.Copy, reads=reads, writes=writes, accs=accs)
        else:
            P.op("dve", "tensor_copy", out_ap, in_ap, reads=reads, writes=writes, accs=accs)

    def rstd_from_ss(rs, ss, n, width=1):
        P.op("dve", "tensor_scalar", rs[:, 0:width], ss[:, 0:width], 1.0 / n, EPS, ALU.mult, ALU.add,
             reads=[ss.b], writes=[rs.b])
        P.op("act", "activation", out=rs[:, 0:width], in_=rs[:, 0:width], func=AF.Sqrt, reads=[rs.b], writes=[rs.b])
        P.op("dve", "reciprocal", rs[:, 0:width], rs[:, 0:width], reads=[rs.b], writes=[rs.b])

    def transposes(src, nblk, banks, width=128, kp=128):
        for i in range(nblk):
            bk = banks[i // 4]
            col = (i % 4) * 128
            P.op("pe", "matmul", psb[bk][0:width, col:col + 128], src[:, i * width:(i + 1) * width], ident[:],
                 start=True, stop=True, reads=[src.b, ident.b],
                 **({"writes": [psB[bk]]} if i % 4 == 0 else {"accs": [psB[bk]]}))

    if doA:
        with ExitStack() as ph:
            cc = T(ph, "cc", [128, 32], F32)
            sT = T(ph, "sT", [128, 16, 2], BF16)
            adab = T(ph, "adab", [2, 6 * D], F32)
            modsb = T(ph, "modsb", [2, 6 * D], F32)
            wa = [T(ph, f"wa{i}", [128, 16, 512], BF16) for i in range(3)]
            ds_wa = [P.dsem(f"ds_wa{i}") for i in range(3)]
            P.dma("sp", ds_misc, cc[:], cc_in[:, :], writes=[cc.b])
            P.op("act", "activation", out=sT[:].rearrange("p k m -> p (k m)"), in_=cc[:], func=AF.Silu,
                 reads=[cc.b], writes=[sT.b])
            n = 0
            for l in range(2):
                wsrc = ada_w[l].rearrange("(k p) n -> p k n", p=128)
                P.dma("sp", ds_misc, adab[:], ada_b[0:1, l * 6 * D:(l + 1) * 6 * D].partition_broadcast(2), writes=[adab.b])
                for g in range(24):
                    s = n % 3
                    P.dma("pool", ds_wa[s], wa[s][:], wsrc[:, :, g * 512:(g + 1) * 512], writes=[wa[s].b])
                    bk = n % 2
                    for k in range(16):
                        P.op("pe", "matmul", psb[bk][0:2, :], sT[:, k, :], wa[s][:, k, :], start=(k == 0), stop=(k == 15),
                             reads=[sT.b, wa[s].b], **({"writes": [psB[bk]]} if k == 0 else {"accs": [psB[bk]]}))
                    o = g * 512
                    P.op("dve", "tensor_tensor", modsb[0:2, o:o + 512], psb[bk][0:2, :], adab[0:2, o:o + 512], ALU.add,
                         reads=[psB[bk], adab.b], **({"writes": [modsb.b]} if g == 0 else {"accs": [modsb.b]}))
                    n += 1
                P.dma("sp", ds_misc, modS[:, l * 6 * D:(l + 1) * 6 * D], modsb[:], reads=[modsb.b])
            P.flush()

    def modulate_tile(xt, hb, ss, rs, A, Bv):
        P.op("act", "activation", out=hb[:], in_=xt[:], func=AF.Square, accum_out=ss[:, 0:1],
             reads=[xt.b], writes=[hb.b, ss.b])
        rstd_from_ss(rs, ss, D)
        P.op("dve", "scalar_tensor_tensor", out=xt[:], in0=xt[:], scalar=rs[:, 0:1], in1=A[:], op0=ALU.mult, op1=ALU.mult,
             reads=[xt.b, rs.b, A.b], writes=[xt.b])
        P.op("dve", "tensor_tensor", hb[:], xt[:], Bv[:], ALU.add, reads=[xt.b, Bv.b], writes=[hb.b])

    def load_AB(A, Bv, tmp, dsem, row, l, v_shift, v_scale, norm_name):
        bcast_load(A, dsem, mod_row(row, l, v_scale))
        bcast_load(tmp, dsem, small_row(norm_name))
        bcast_load(Bv, dsem, mod_row(row, l, v_shift))
        P.op("dve", "scalar_tensor_tensor", out=A[:], in0=A[:], scalar=1.0, in1=tmp[:], op0=ALU.add, op1=ALU.mult,
             reads=[A.b, tmp.b], writes=[A.b])

    if doA:
        with ExitStack() as ph:
            hT = T(ph, "hT", [128, 16, NKV], BF16)
            with ExitStack() as p1:
                A_l = T(p1, "A_l", [128, D], F32)
                B_l = T(p1, "B_l", [128, D], F32)
                A_c = T(p1, "A_c", [128, D], F32)
                B_c = T(p1, "B_c", [128, D], F32)
                xt = [T(p1, f"xt{i}", [128, D], F32) for i in range(2)]
                hb = [T(p1, f"hb{i}", [128, D], BF16) for i in range(2)]
                ss = [T(p1, f"ss{i}", [128, 1], F32) for i in range(2)]
                rs = [T(p1, f"rs{i}", [128, 1], F32) for i in range(2)]
                ds_x = [P.dsem(f"ds_x{i}") for i in range(2)]
                load_AB(A_l, B_l, xt[0], ds_misc, 0, 0, 0, 1, "nmix0")
                load_AB(A_c, B_c, xt[1], ds_misc, 1, 0, 0, 1, "nmix0")
                for t in range(NKT):
                    s = t % 2
                    src = xl[t * 128:(t + 1) * 128, :] if t < 32 else cl[(t - 32) * 128:(t - 31) * 128, :]
                    A, Bv = (A_l, B_l) if t < 32 else (A_c, B_c)
                    P.dma("sp", ds_x[s], xt[s][:], src, writes=[xt[s].b])
                    modulate_tile(xt[s], hb[s], ss[s], rs[s], A, Bv)
                    for q4 in range(4):
                        bk = 4 * s + q4
                        for i in range(4):
                            k = 4 * q4 + i
                            P.op("pe", "matmul", psb[bk][:, i * 128:(i + 1) * 128], hb[s][:, k * 128:(k + 1) * 128], ident[:],
                                 start=True, stop=True, reads=[hb[s].b, ident.b],
                                 **({"writes": [psB[bk]]} if i == 0 else {"accs": [psB[bk]]}))
                        evac(hT[:, 4 * q4:4 * q4 + 4, t * 128:(t + 1) * 128],
                             psb[bk][:].rearrange("p (a b) -> p a b", a=4), reads=[psB[bk]], accs=[hT.b])
                P.flush()
            with ExitStack() as p2:
                wp = [T(p2, f"wp{i}", [128, 16, 512], BF16) for i in range(2)]
                ds_wp = [P.dsem(f"ds_wp{i}") for i in range(2)]
                gains = {}
                gtmp = T(p2, "gtmp", [128, 128], F32)
                for nm, key, sc in (("qa", "na_q", 128 ** -0.5), ("ka", "na_k", 1.0), ("qb", "df_q", 128 ** -0.5), ("kb", "df_k", 1.0)):
                    g = T(p2, "g_" + nm, [128, 128], F32)
                    o, _ = SM[key]
                    P.dma("sp", ds_misc, g[:], smalls[0:1, o:o + 128].partition_broadcast(128), writes=[g.b])
                    if sc != 1.0:
                        P.op("dve", "tensor_scalar", g[:], g[:], sc, None, ALU.mult, reads=[g.b], writes=[g.b])
                    gains[nm] = g
                cs = [T(p2, f"cs{i}", [128, 256], F32) for i in range(2)]
                ds_cs = [P.dsem(f"ds_cs{i}") for i in range(2)]
                ss4 = [T(p2, f"ss4_{i}", [128, 4], F32) for i in range(2)]
                rs4 = [T(p2, f"rs4_{i}", [128, 4], F32) for i in range(2)]
                junk = T(p2, "junk", [128, 128], BF16)
                xg = [T(p2, f"xg{i}", [128, 512], F32) for i in range(2)]
                t1 = [T(p2, f"t1_{i}", [128, 512], F32) for i in range(2)]
                t2 = [T(p2, f"t2_{i}", [128, 512], F32) for i in range(2)]
                xb = [T(p2, f"xb{i}", [128, 512], BF16) for i in range(2)]
                stg = [T(p2, f"stg{i}", [128, 512], BF16) for i in range(2)]
                ds_stg = [P.dsem(f"ds_stg{i}") for i in range(2)]
                wsrc = even_w_in.rearrange("(k p) n -> p k n", p=128)
                q_tiles = list(range(NQT)) + [32, 33]
                na_tiles = list(range(19)) + [32, 33]
                all_tiles = list(range(NKT))
                u = 0
                for g in range(12):
                    kind = ("qa", "ka", "va", "qb", "kb", "vb")[g // 2]
                    half = g % 2
                    ws = g % 2
                    P.dma("pool", ds_wp[ws], wp[ws][:], wsrc[:, :, g * 512:(g + 1) * 512], writes=[wp[ws].b])
                    tiles = {"qa": q_tiles, "qb": q_tiles, "ka": na_tiles, "va": na_tiles, "kb": all_tiles, "vb": all_tiles}[kind]
                    for ti, t in enumerate(tiles):
                        s = u % 2
                        bk = u % 2
                        for k in range(16):
                            P.op("pe", "matmul", psb[bk][:, :], hT[:, k, t * 128:(t + 1) * 128], wp[ws][:, k, :],
                                 start=(k == 0), stop=(k == 15), reads=[hT.b, wp[ws].b],
                                 **({"writes": [psB[bk]]} if k == 0 else {"accs": [psB[bk]]}))
                        if kind in ("va", "vb"):
                            evac(stg[s][:], psb[bk][:, :], reads=[psB[bk]], writes=[stg[s].b])
                            if kind == "va":
                                dst = Va[4 * half:4 * half + 4, ti * 128:(ti + 1) * 128, :].rearrange("h t d -> t h d")
                                P.dma("sp", ds_stg[s], dst, stg[s][:].rearrange("t (h d) -> t h d", h=4), reads=[stg[s].b])
                            else:
                                dst = Vb[2 * half:2 * half + 2, t * 128:(t + 1) * 128, :].rearrange("h t d -> t h d")
                                P.dma("sp", ds_stg[s], dst, stg[s][:].rearrange("t (h d) -> t h d", h=2), reads=[stg[s].b])
                            u += 1
                            continue
                        for hh in range(4):
                            P.op("act", "activation", out=junk[:], in_=psb[bk][:, hh * 128:(hh + 1) * 128], func=AF.Square,
                                 accum_out=ss4[s][:, hh:hh + 1], reads=[psB[bk]],
                                 **({"writes": [junk.b, ss4[s].b]} if hh == 0 else {"accs": [junk.b, ss4[s].b]}))
                        rstd_from_ss(rs4[s], ss4[s], 128, 4)
                        rope = kind in ("qb", "kb") and t < 32
                        dstT = xg[s] if rope else xb[s]
                        for hh in range(4):
                            P.op("dve", "scalar_tensor_tensor", out=dstT[:, hh * 128:(hh + 1) * 128],
                                 in0=psb[bk][:, hh * 128:(hh + 1) * 128], scalar=rs4[s][:, hh:hh + 1], in1=gains[kind][:],
                                 op0=ALU.mult, op1=ALU.mult, reads=[psB[bk], rs4[s].b, gains[kind].b],
                                 **({"writes": [dstT.b]} if hh == 0 else {"accs": [dstT.b]}))
                        if rope:
                            P.dma("sp", ds_cs[s], cs[s][:], rope0[t * 128:(t + 1) * 128, :], writes=[cs[s].b])
                            x3 = xg[s][:].rearrange("p (h d) -> p h d", h=4)
                            P.op("dve", "tensor_tensor", t1[s][:].rearrange("p (h d) -> p h d", h=4), x3,
                                 cs[s][:, None, 0:128].broadcast_to([128, 4, 128]), ALU.mult,
                                 reads=[xg[s].b, cs[s].b], writes=[t1[s].b])
                            x4 = xg[s][:].rearrange("p (h b e d) -> p h b e d", h=4, b=2, e=2)
                            sn = cs[s][:, 128:256].rearrange("p (b e d) -> p b e d", b=2, e=2)
                            for e in range(2):
                                P.op("dve", "tensor_tensor",
                                     t2[s][:].rearrange("p (h b e d) -> p h b e d", h=4, b=2, e=2)[:, :, :, e, :],
                                     x4[:, :, :, 1 - e, :],
                                     sn[:, None, :, e, :].broadcast_to([128, 4, 2, 32]), ALU.mult,
                                     reads=[xg[s].b, cs[s].b], **({"writes": [t2[s].b]} if e == 0 else {"accs": [t2[s].b]}))
                            P.op("dve", "tensor_tensor", xb[s][:], t1[s][:], t2[s][:], ALU.add,
                                 reads=[t1[s].b, t2[s].b], writes=[xb[s].b])
                        tb = 2 + (u % 2)
                        transposes(xb[s], 4, [tb])
                        evac(stg[s][:], psb[tb][:, :], reads=[psB[tb]], writes=[stg[s].b])
                        sv = stg[s][:].rearrange("d (h t) -> d h t", h=4)
                        if kind == "qa":
                            col = t * 128 if t < 32 else NQ + (t - 32) * 128
                            dst = QaT[4 * half:4 * half + 4, :, col:col + 128]
                        elif kind == "qb":
                            col = t * 128 if t < 32 else NQ + (t - 32) * 128
                            dst = QbT[4 * half:4 * half + 4, :, col:col + 128]
                        elif kind == "ka":
                            dst = KaT[4 * half:4 * half + 4, :, ti * 128:(ti + 1) * 128]
                        else:
                            dst = KbT[4 * half:4 * half + 4, :, t * 128:(t + 1) * 128]
                        P.dma("sp", ds_stg[s], dst.rearrange("h d t -> d h t"), sv, reads=[stg[s].b])
                        u += 1
                P.flush()
        if _UPTO[0] == 2:
            top.close()
            return nc

    def attn_dense(parts, V, kchunks, qcol0, nq, dv, S_banks, pTs, O_banks, fin):
        n = len(kchunks)
        NS = len(S_banks)
        nsub = nq // 128

        def emitS(i):
            bk = S_banks[i % NS]
            kc = kchunks[i]
            for pi, (kT, Kp, qT) in enumerate(parts):
                P.op("pe", "matmul", psb[bk][:, 0:nq], kT[0:Kp, kc * 128:(kc + 1) * 128], qT[0:Kp, qcol0:qcol0 + nq],
                     start=(pi == 0), stop=(pi == len(parts) - 1), reads=[kT.b, qT.b],
                     **({"writes": [psB[bk]]} if pi == 0 else {"accs": [psB[bk]]}))

        def emitE(i):
            bk = S_banks[i % NS]
            pT = pTs[i % NS]
            P.op("act", "activation", out=pT[:, 0:nq], in_=psb[bk][:, 0:nq], func=AF.Exp, reads=[psB[bk]], writes=[pT.b])

        def emitPV(i):
            pT = pTs[i % NS]
            kc = kchunks[i]
            for sub in range(nsub):
                ob = O_banks[sub]
                P.op("pe", "matmul", psb[ob][:, 0:dv + 1], pT[:, sub * 128:(sub + 1) * 128], V[:, kc, 0:dv + 1],
                     start=(i == 0), stop=(i == n - 1), reads=[pT.b, V.b],
                     **({"writes": [psB[ob]]} if i == 0 else {"accs": [psB[ob]]}))

        for i in range(min(NS - 1, n)):
            emitS(i)
        for i in range(n):
            if i + NS - 1 < n:
                emitS(i + NS - 1)
            emitE(i)
            emitPV(i)
        for sub in range(nsub):
            fin(sub, O_banks[sub])


    def post_attn(ph, O_src, ntiles, nlat, w_out_ap, l, x_src_fn, xmid_dst_fn, hT_dst_fn, nmix_unused=None):
        wo = T(ph, "wo", [128, 16, D], BF16)
        G = T(ph, "G", [128, D], F32)
        Af = T(ph, "Af", [128, D], F32)
        Bf = T(ph, "Bf", [128, D], F32)
        xt = [T(ph, f"pxt{i}", [128, D], F32) for i in range(2)]
        oT = T(ph, "oT", [128, 16, 128], BF16)
        h2b = T(ph, "h2b", [128, D], BF16)
        h2T = T(ph, "h2T", [128, 16, 128], BF16)
        tq = [T(ph, f"tq{i}", [128, 512], F32) for i in range(2)]
        ss = [T(ph, f"pss{i}", [128, 1], F32) for i in range(2)]
        rs = [T(ph, f"prs{i}", [128, 1], F32) for i in range(2)]
        ds_w = P.dsem("x")
        ds_x = [P.dsem("x") for _ in range(2)]
        ds_xo = [P.dsem("x") for _ in range(2)]
        ds_h = P.dsem("x")
        wsrc = w_out_ap.rearrange("(k p) n -> p k n", p=128)
        for q in range(4):
            P.dma("pool", ds_w, wo[:, 4 * q:4 * q + 4, :], wsrc[:, 4 * q:4 * q + 4, :],
                  **({"writes": [wo.b]} if q == 0 else {"accs": [wo.b]}))
        nffn = "nffn%d" % l
        for j in range(ntiles):
            s = j % 2
            row = 0 if j < nlat else 1
            if j == 0 or j == nlat:
                bcast_load(G, ds_misc, mod_row(row, l, 2))
                load_AB(Af, Bf, xt[1 - s], ds_misc, row, l, 3, 4, nffn)
            P.dma("sp", ds_x[s], xt[s][:], x_src_fn(j), writes=[xt[s].b])
            for q4 in range(4):
                for i in range(4):
                    k = 4 * q4 + i
                    P.op("pe", "matmul", psb[q4][:, i * 128:(i + 1) * 128], O_src[:, j, k * 128:(k + 1) * 128], ident[:],
                         start=True, stop=True, reads=[O_src.b, ident.b],
                         **({"writes": [psB[q4]]} if i == 0 else {"accs": [psB[q4]]}))
                evac(oT[:, 4 * q4:4 * q4 + 4, :], psb[q4][:].rearrange("p (a b) -> p a b", a=4), reads=[psB[q4]],
                     **({"writes": [oT.b]} if q4 == 0 else {"accs": [oT.b]}))
            for dg in range(4):
                bk = 4 + dg
                for k in range(16):
                    P.op("pe", "matmul", psb[bk][:, :], oT[:, k, :], wo[:, k, dg * 512:(dg + 1) * 512], start=(k == 0), stop=(k == 15),
                         reads=[oT.b, wo.b], **({"writes": [psB[bk]]} if k == 0 else {"accs": [psB[bk]]}))
                z = dg % 2
                P.op("dve", "tensor_tensor", tq[z][:], psb[bk][:, :], G[:, dg * 512:(dg + 1) * 512], ALU.mult,
                     reads=[psB[bk], G.b], writes=[tq[z].b])
                P.op("dve", "tensor_tensor", xt[s][:, dg * 512:(dg + 1) * 512], tq[z][:], xt[s][:, dg * 512:(dg + 1) * 512], ALU.add,
                     reads=[tq[z].b, xt[s].b], writes=[xt[s].b])
            P.dma("sp", ds_xo[s], xmid_dst_fn(j), xt[s][:], reads=[xt[s].b])
            modulate_tile(xt[s], h2b, ss[s], rs[s], Af, Bf)
            for q4 in range(4):
                for i in range(4):
                    k = 4 * q4 + i
                    P.op("pe", "matmul", psb[q4][:, i * 128:(i + 1) * 128], h2b[:, k * 128:(k + 1) * 128], ident[:],
                         start=True, stop=True, reads=[h2b.b, ident.b],
                         **({"writes": [psB[q4]]} if i == 0 else {"accs": [psB[q4]]}))
                evac(h2T[:, 4 * q4:4 * q4 + 4, :], psb[q4][:].rearrange("p (a b) -> p a b", a=4), reads=[psB[q4]],
                     **({"writes": [h2T.b]} if q4 == 0 else {"accs": [h2T.b]}))
            P.dma("sp", ds_h, hT_dst_fn(j), h2T[:], reads=[h2T.b])

    def ffn(ph, l, supers):
        gT = T(ph, "gT", [128, FC, 512], BF16)
        hTs = [T(ph, f"hTs{i}", [128, 16, 2, 258], BF16) for i in range(2)]
        wi = [T(ph, f"wi{i}", [128, 16, 2, 128], BF16) for i in range(3)]
        wo = [T(ph, f"fwo{i}", [128, FC, 256], BF16) for i in range(2)]
        Gs = [T(ph, f"Gf{i}", [128, D], F32) for i in range(2)]
        cw = T(ph, "cw", [128, 86 * 3], F32)
        ta = [T(ph, f"ta{i}", [128, 256], F32) for i in range(2)]
        tb = [T(ph, f"tb{i}", [128, 256], F32) for i in range(2)]
        sa = [T(ph, f"sa{i}", [128, 256], F32) for i in range(2)]
        xr = [T(ph, f"xr{i}", [128, 256], F32) for i in range(2)]
        ot = [T(ph, f"ot{i}", [128, 256], F32) for i in range(2)]
        ds_h = [[P.dsem("x") for _ in range(2)] for _ in range(2)]
        ds_wi = [P.dsem("x") for _ in range(3)]
        ds_wo = [P.dsem("x") for _ in range(2)]
        ds_xr = [P.dsem("x") for _ in range(2)]
        ds_ot = [P.dsem("x") for _ in range(2)]
        li = l if fused else 0
        P.dma("sp", ds_misc, cw[:], cwt_in[li], writes=[cw.b])
        rows = sorted({u["row"] for sp_ in supers for u in sp_})
        for r in rows:
            bcast_load(Gs[r], ds_misc, mod_row(r, l, 5))
        w_in_v = ffn_w_in[li].rearrange("(k p) n -> p k n", p=128)
        w_out_v = ffn_w_out[li].rearrange("(c p) n -> p c n", p=128)
        nwi = 0
        nwo = 0
        nz = 0
        nd = 0
        for si, units in enumerate(supers):
            sl = si % 2
            for ui, u in enumerate(units):
                n = u["n"]
                P.dma("sp", ds_h[sl][ui], hTs[sl][:, :, ui, 0:n + 2],
                      u["src"].rearrange("(k p) t -> p k t", p=128)[:, :, u["c0"]:u["c0"] + n + 2],
                      **({"writes": [hTs[sl].b]} if ui == 0 else {"accs": [hTs[sl].b]}))
            for c in range(FC):
                ws = nwi % 3
                nwi += 1
                for half in range(2):
                    P.dma("pool", ds_wi[ws], wi[ws][:, :, half, :], w_in_v[:, :, half * DFF + c * 128:half * DFF + (c + 1) * 128],
                          **({"writes": [wi[ws].b]} if half == 0 else {"accs": [wi[ws].b]}))
                for ui, u in enumerate(units):
                    n = u["n"]
                    z = nz % 2
                    nz += 1
                    ba, bb = 2 * z, 2 * z + 1
                    for half, bk in ((0, ba), (1, bb)):
                        for k in range(16):
                            P.op("pe", "matmul", psb[bk][:, 0:n + 2], wi[ws][:, k, half, :], hTs[sl][:, k, ui, 0:n + 2],
                                 start=(k == 0), stop=(k == 15), reads=[wi[ws].b, hTs[sl].b],
                                 **({"writes": [psB[bk]]} if k == 0 else {"accs": [psB[bk]]}))
                    for (tt, bk, ci) in ((ta[z], ba, c), (tb[z], bb, FC + c)):
                        P.op("act", "activation", out=tt[:, 0:n], in_=psb[bk][:, 1:n + 1], func=AF.Copy,
                             scale=cw[:, 3 * ci + 1:3 * ci + 2], reads=[psB[bk], cw.b], writes=[tt.b])
                        P.op("dve", "scalar_tensor_tensor", out=tt[:, 0:n], in0=psb[bk][:, 0:n], scalar=cw[:, 3 * ci:3 * ci + 1],
                             in1=tt[:, 0:n], op0=ALU.mult, op1=ALU.add, reads=[psB[bk], cw.b, tt.b], writes=[tt.b])
                        P.op("dve", "scalar_tensor_tensor", out=tt[:, 0:n], in0=psb[bk][:, 2:n + 2], scalar=cw[:, 3 * ci + 2:3 * ci + 3],
                             in1=tt[:, 0:n], op0=ALU.mult, op1=ALU.add, reads=[psB[bk], cw.b, tt.b], writes=[tt.b])
                    P.op("act", "activation", out=sa[z][:, 0:n], in_=ta[z][:, 0:n], func=AF.Silu, reads=[ta[z].b], writes=[sa[z].b])
                    P.op("dve", "tensor_tensor", gT[:, c, ui * 256:ui * 256 + n], sa[z][:, 0:n], tb[z][:, 0:n], ALU.mult,
                         reads=[sa[z].b, tb[z].b], **({"writes": [gT.b]} if (c == 0 and ui == 0) else {"accs": [gT.b]}))
            for dg in range(8):
                ws = nwo % 2
                nwo += 1
                P.dma("pool", ds_wo[ws], wo[ws][:], w_out_v[:, :, dg * 256:(dg + 1) * 256], writes=[wo[ws].b])
                for ui, u in enumerate(units):
                    for sub in range(u["n"] // 128):
                        z = nd % 2
                        nd += 1
                        bk = 4 + z
                        tc0 = ui * 256 + sub * 128
                        for c in range(FC):
                            P.op("pe", "matmul", psb[bk][:, 0:256], gT[:, c, tc0:tc0 + 128], wo[ws][:, c, :],
                                 start=(c == 0), stop=(c == FC - 1), reads=[gT.b, wo[ws].b],
                                 **({"writes": [psB[bk]]} if c == 0 else {"accs": [psB[bk]]}))
                        r0 = sub * 128
                        P.dma("sp", ds_xr[z], xr[z][:], u["res"][r0:r0 + 128, dg * 256:(dg + 1) * 256], writes=[xr[z].b])
                        P.op("dve", "tensor_tensor", ot[z][:], psb[bk][:, 0:256], Gs[u["row"]][:, dg * 256:(dg + 1) * 256], ALU.mult,
                             reads=[psB[bk], Gs[u["row"]].b], writes=[ot[z].b])
                        P.op("dve", "tensor_tensor", ot[z][:], ot[z][:], xr[z][:], ALU.add,
                             reads=[ot[z].b, xr[z].b], writes=[ot[z].b])
                        P.dma("sp", ds_ot[z], u["dst"][r0:r0 + 128, dg * 256:(dg + 1) * 256], ot[z][:], reads=[ot[z].b])


    if doA:
        with ExitStack() as ph:
            O_all = T(ph, "O_all", [128, 19, D], BF16)
            with ExitStack() as p3:
                kT = [T(p3, f"kT{i}", [128, NA_KT * 128], BF16) for i in range(2)]
                vA = [T(p3, f"vA{i}", [128, NA_KT, 129], BF16) for i in range(2)]
                qT = [T(p3, f"qT{i}", [128, NQC], BF16) for i in range(2)]
                bT = [T(p3, f"bT{i}", [128, 3, 5, 128], F32) for i in range(2)]
                dsl = [[P.dsem("x") for _ in range(4)] for _ in range(2)]
                sb = [T(p3, f"sb{i}", [128, 640], F32) for i in range(2)]
                pT = [T(p3, f"pT{i}", [128, 896], BF16) for i in range(2)]
                rinv = [T(p3, f"rinv{i}", [128, 1], F32) for i in range(2)]
                for i in range(2):
                    P.op("dve", "memset", vA[i][:, :, 128:129], 1.0, writes=[vA[i].b])
                n = 0
                for h in range(8):
                    s = h % 2
                    P.dma("sp", dsl[s][0], kT[s][:], KaT[h], writes=[kT[s].b])
                    P.dma("sp", dsl[s][1], vA[s][:, :, 0:128], Va[h].rearrange("(kt p) d -> p kt d", p=128), writes=[vA[s].b])
                    P.dma("sp", dsl[s][2], qT[s][:], QaT[h], writes=[qT[s].b])
                    P.dma("sp", dsl[s][3], bT[s][:].rearrange("p a b c -> p (a b c)"), biasT[h], writes=[bT[s].b])
                    for j in range(19):
                        lat = j < NQT
                        if lat:
                            st = min(max(2 * j - 4, 0), 54)
                            kts = [st // 2 + i for i in range(5)] + [19, 20]
                            var = min(j, 2)
                            qc = j * 128
                        else:
                            kts = [19, 20]
                            qc = NQ + (j - NQT) * 128
                        z = n % 2
                        bA, bB, bO = 2 * z, 2 * z + 1, 4 + z
                        nk = len(kts)
                        for i, kt in enumerate(kts):
                            bk, col = (bA, i * 128) if i < 4 else (bB, (i - 4) * 128)
                            P.op("pe", "matmul", psb[bk][:, col:col + 128], kT[s][:, kt * 128:(kt + 1) * 128], qT[s][:, qc:qc + 128],
                                 start=True, stop=True, reads=[kT[s].b, qT[s].b],
                                 **({"writes": [psB[bk]]} if i in (0, 4) else {"accs": [psB[bk]]}))
                        if lat:
                            P.op("dve", "tensor_tensor", sb[z][:, 0:512], psb[bA][:, :],
                                 bT[s][:, var, 0:4, :].rearrange("p a b -> p (a b)"), ALU.add,
                                 reads=[psB[bA], bT[s].b], writes=[sb[z].b])
                            P.op("dve", "tensor_tensor", sb[z][:, 512:640], psb[bB][:, 0:128], bT[s][:, var, 4, :], ALU.add,
                                 reads=[psB[bB], bT[s].b], accs=[sb[z].b])
                            P.op("act", "activation", out=pT[z][:, 0:640], in_=sb[z][:, 0:640], func=AF.Exp,
                                 reads=[sb[z].b], writes=[pT[z].b])
                            P.op("act", "activation", out=pT[z][:, 640:896], in_=psb[bB][:, 128:384], func=AF.Exp,
                                 reads=[psB[bB]], accs=[pT[z].b])
                        else:
                            P.op("act", "activation", out=pT[z][:, 0:256], in_=psb[bA][:, 0:256], func=AF.Exp,
                                 reads=[psB[bA]], writes=[pT[z].b])
                        for i, kt in enumerate(kts):
                            P.op("pe", "matmul", psb[bO][:, 0:129], pT[z][:, i * 128:(i + 1) * 128], vA[s][:, kt, :],
                                 start=(i == 0), stop=(i == nk - 1), reads=[pT[z].b, vA[s].b],
                                 **({"writes": [psB[bO]]} if i == 0 else {"accs": [psB[bO]]}))
                        P.op("dve", "reciprocal", rinv[z][:], psb[bO][:, 128:129], reads=[psB[bO]], writes=[rinv[z].b])
                        P.op("dve", "tensor_scalar", O_all[:, j, h * 128:(h + 1) * 128], psb[bO][:, 0:128], rinv[z][:, 0:1], None, ALU.mult,
                             reads=[psB[bO], rinv[z].b], accs=[O_all.b])
                        n += 1
                P.flush()
            with ExitStack() as p3:
                kT = [T(p3, f"dkT{i}", [128, 2, NKV], BF16) for i in range(2)]
                vB = [T(p3, f"vB{i}", [128, NKT, 257], BF16) for i in range(2)]
                qT = [T(p3, f"dqT{i}", [128, 2, NQC], BF16) for i in range(2)]
                dsl = [[P.dsem("x") for _ in range(3)] for _ in range(2)]
                pTs = [T(p3, f"dpT{i}", [128, 512], BF16) for i in range(3)]
                o1 = [T(p3, f"o1_{i}", [128, 256], F32) for i in range(4)]
                od = [T(p3, f"od_{i}", [128, 256], F32) for i in range(2)]
                rinv = [T(p3, f"drinv{i}", [128, 1], F32) for i in range(2)]
                ssd = [T(p3, f"ssd{i}", [128, 1], F32) for i in range(2)]
                rsd = [T(p3, f"rsd{i}", [128, 1], F32) for i in range(2)]
                junkd = T(p3, "junkd", [128, 256], BF16)
                lamt = T(p3, "lamt", [128, 512], F32)
                ltmp = T(p3, "ltmp", [128, 128], F32)
                e12 = T(p3, "e12", [128, 2], F32)
                nlam = T(p3, "nlam", [128, 1], F32)
                subg = T(p3, "subg", [128, 256], F32)
                for i in range(2):
                    P.op("dve", "memset", vB[i][:, :, 256:257], 1.0, writes=[vB[i].b])
                bcast_load(lamt, ds_misc, small_row("df_lam"))
                bcast_load(subg, ds_misc, small_row("df_sub"))
                for i in range(2):
                    P.op("dve", "tensor_tensor", ltmp[:], lamt[:, 256 * i:256 * i + 128], lamt[:, 256 * i + 128:256 * i + 256], ALU.mult,
                         reads=[lamt.b], writes=[ltmp.b])
                    P.op("dve", "reduce_sum", e12[:, i:i + 1], ltmp[:], AX.X, reads=[ltmp.b],
                         **({"writes": [e12.b]} if i == 0 else {"accs": [e12.b]}))
                P.op("act", "activation", out=e12[:], in_=e12[:], func=AF.Exp, reads=[e12.b], writes=[e12.b])
                P.op("dve", "tensor_tensor", nlam[:], e12[:, 1:2], e12[:, 0:1], ALU.subtract, reads=[e12.b], writes=[nlam.b])
                P.op("dve", "tensor_scalar", nlam[:], nlam[:], -LAM_INIT0, None, ALU.add, reads=[nlam.b], writes=[nlam.b])
                P.op("dve", "tensor_scalar", subg[:], subg[:], 1.0 - LAM_INIT0, None, ALU.mult, reads=[subg.b], writes=[subg.b])
                fcount = [0]
                for h in range(4):
                    s = h % 2
                    P.dma("sp", dsl[s][0], kT[s][:], KbT[2 * h:2 * h + 2].rearrange("a d t -> d a t"), writes=[kT[s].b])
                    P.dma("sp", dsl[s][1], vB[s][:, :, 0:256], Vb[h].rearrange("(kt p) d -> p kt d", p=128), writes=[vB[s].b])
                    P.dma("sp", dsl[s][2], qT[s][:], QbT[2 * h:2 * h + 2].rearrange("a d t -> d a t"), writes=[qT[s].b])
                    blocks = [(0, 512, "lat"), (512, 512, "lat"), (1024, 512, "lat"), (1536, 512, "lat"), (2048, 128, "lat"),
                              (NQ, 256, "ctx")]
                    for (q0, nq, kind) in blocks:
                        kch = list(range(NKT)) if kind == "lat" else [32, 33]
                        for sidx in range(2):
                            kTs = T.__new__(T)
                            kTs.t = kT[s].t[:, sidx, :]
                            kTs.b = kT[s].b
                            qTs = T.__new__(T)
                            qTs.t = qT[s].t[:, sidx, :]
                            qTs.b = qT[s].b

                            def fin(sub, ob, sidx=sidx, q0=q0, h=h, kind=kind):
                                z = fcount[0] % 2
                                fcount[0] += 1
                                tile = (q0 // 128 + sub) if kind == "lat" else (NQT + sub)
                                P.op("dve", "reciprocal", rinv[z][:], psb[ob][:, 256:257], reads=[psB[ob]], writes=[rinv[z].b])
                                if sidx == 0:
                                    P.op("dve", "tensor_scalar", o1[sub][:], psb[ob][:, 0:256], rinv[z][:, 0:1], None, ALU.mult,
                                         reads=[psB[ob], rinv[z].b], writes=[o1[sub].b])
                                    return
                                P.op("dve", "tensor_tensor", rinv[z][:], rinv[z][:], nlam[:], ALU.mult,
                                     reads=[rinv[z].b, nlam.b], writes=[rinv[z].b])
                                P.op("dve", "scalar_tensor_tensor", out=od[z][:], in0=psb[ob][:, 0:256], scalar=rinv[z][:, 0:1],
                                     in1=o1[sub][:], op0=ALU.mult, op1=ALU.add,
                                     reads=[psB[ob], rinv[z].b, o1[sub].b], writes=[od[z].b])
                                P.op("act", "activation", out=junkd[:], in_=od[z][:], func=AF.Square, accum_out=ssd[z][:, 0:1],
                                     reads=[od[z].b], writes=[junkd.b, ssd[z].b])
                                rstd_from_ss(rsd[z], ssd[z], 256)
                                P.op("dve", "scalar_tensor_tensor", out=O_all[:, tile, 1024 + 256 * h:1024 + 256 * (h + 1)],
                                     in0=od[z][:], scalar=rsd[z][:, 0:1], in1=subg[:], op0=ALU.mult, op1=ALU.mult,
                                     reads=[od[z].b, rsd[z].b, subg.b], accs=[O_all.b])

                            attn_dense([(kTs, 128, qTs)], vB[s], kch, q0, nq, 256, [0, 1, 2], pTs, [3, 4, 5, 6], fin)
                P.flush()
            if _UPTO[0] == 3:
                top.close()
                return nc
            with ExitStack() as p4:
                zt = T(p4, "zt", [128, 16, 1], BF16)
                P.op("dve", "memset", zt[:], 0.0, writes=[zt.b])
                hl = hTf_lat.rearrange("(k p) t -> p k t", p=128)
                hc = hTf_ctx.rearrange("(k p) t -> p k t", p=128)
                for dstv, col in ((hl, 0), (hl, NQ + 1), (hc, 0), (hc, CTX + 1)):
                    P.dma("sp", ds_misc, dstv[:, :, col:col + 1], zt[:], reads=[zt.b], allow_slow_non_contiguous=True)
                post_attn(p4, O_all, 19, NQT, even_w_out, 0,
                          lambda j: xl[j * 128:(j + 1) * 128, :] if j < NQT else cl[(j - NQT) * 128:(j - NQT + 1) * 128, :],
                          lambda j: xmid[j * 128:(j + 1) * 128, :] if j < NQT else xcmid[(j - NQT) * 128:(j - NQT + 1) * 128, :],
                          lambda j: (hl[:, :, 1 + j * 128:1 + (j + 1) * 128] if j < NQT
                                     else hc[:, :, 1 + (j - NQT) * 128:1 + (j - NQT + 1) * 128]))
                P.flush()
        if _UPTO[0] == 4:
            top.close()
            return nc
        with ExitStack() as p5:
            def unit(src, c0, n, res, dst, row):
                return {"src": src, "c0": c0, "n": n, "res": res, "dst": dst, "row": row}
            lat_units = [unit(hTf_lat, 256 * u, 256, xmid[256 * u:256 * (u + 1), :], x1q[256 * u:256 * (u + 1), :], 0) for u in range(8)]
            halo_unit = unit(hTf_lat, NOWN, 128, xmid[NOWN:NQ, :], x1q[NOWN:NQ, :], 0)
            ctx_unit = unit(hTf_ctx, 0, 256, xcmid[:, :], xc1[:, :], 1)
            supers = [lat_units[2 * i:2 * i + 2] for i in range(4)] + [[halo_unit, ctx_unit]]
            ffn(p5, 0, supers)
            P.flush()
        if _UPTO[0] == 5:
            top.close()
            return nc
        with ExitStack() as p6:
            wd = T(p6, "wd", [128, 16, 1088], BF16)
            A1 = T(p6, "A1", [128, D], F32)
            B1 = T(p6, "B1", [128, D], F32)
            xt = [T(p6, f"dxt{i}", [128, D], F32) for i in range(2)]
            hb = T(p6, "dhb", [128, D], BF16)
            hT1 = T(p6, "hT1", [128, 16, 128], BF16)
            gq = T(p6, "gq", [128, 512], F32)
            gkv = T(p6, "gkv", [128, 512], F32)
            gkr = T(p6, "gkr", [128, 64], F32)
            cs1 = [T(p6, f"cs1_{i}", [128, 128], F32) for i in range(2)]
            ss = [T(p6, f"dss{i}", [128, 1], F32) for i in range(2)]
            rs = [T(p6, f"drs{i}", [128, 1], F32) for i in range(2)]
            ss3 = [T(p6, f"dss3{i}", [128, 3], F32) for i in range(2)]
            rs3 = [T(p6, f"drs3{i}", [128, 3], F32) for i in range(2)]
            junk6 = T(p6, "junk6", [128, 512], BF16)
            nb = [T(p6, f"nb{i}", [128, 512], BF16) for i in range(2)]
            krf = T(p6, "krf", [128, 64], F32)
            kt1 = T(p6, "kt1", [128, 64], F32)
            kt2 = T(p6, "kt2", [128, 64], F32)
            krb = T(p6, "krb", [128, 64], BF16)
            stg6 = [T(p6, f"stg6_{i}", [128, 4, 128], BF16) for i in range(2)]
            stgr = T(p6, "stgr", [64, 128], BF16)
            ds_w = P.dsem("x")
            ds_x = [P.dsem("x") for _ in range(2)]
            ds_c = [P.dsem("x") for _ in range(2)]
            ds_s = [P.dsem("x") for _ in range(2)]
            ds_r = P.dsem("x")
            P.dma("pool", ds_w, wd[:], mla_w_down.rearrange("(k p) n -> p k n", p=128), writes=[wd.b])
            bcast_load(gq, ds_misc, small_row("m_qa"))
            bcast_load(gkv, ds_misc, small_row("m_kva"))
            bcast_load(gkr, ds_misc, small_row("m_kr"))
            nst = 0
            for j in range(19):
                s = j % 2
                lat = j < NQT
                if j == 0 or j == NQT:
                    load_AB(A1, B1, xt[1 - s], ds_misc, 0 if lat else 1, 1, 0, 1, "nmix1")
                src = x1q[j * 128:(j + 1) * 128, :] if lat else xc1[(j - NQT) * 128:(j - NQT + 1) * 128, :]
                P.dma("sp", ds_x[s], xt[s][:], src, writes=[xt[s].b])
                modulate_tile(xt[s], hb, ss[s], rs[s], A1, B1)
                for q4 in range(4):
                    for i in range(4):
                        k = 4 * q4 + i
                        P.op("pe", "matmul", psb[q4][:, i * 128:(i + 1) * 128], hb[:, k * 128:(k + 1) * 128], ident[:],
                             start=True, stop=True, reads=[hb.b, ident.b],
                             **({"writes": [psB[q4]]} if i == 0 else {"accs": [psB[q4]]}))
                    evac(hT1[:, 4 * q4:4 * q4 + 4, :], psb[q4][:].rearrange("p (a b) -> p a b", a=4), reads=[psB[q4]],
                         **({"writes": [hT1.b]} if q4 == 0 else {"accs": [hT1.b]}))
                for gi, (c0, w) in enumerate(((0, 512), (512, 512), (1024, 64))):
                    bk = 4 + gi
                    for k in range(16):
                        P.op("pe", "matmul", psb[bk][:, 0:w], hT1[:, k, :], wd[:, k, c0:c0 + w], start=(k == 0), stop=(k == 15),
                             reads=[hT1.b, wd.b], **({"writes": [psB[bk]]} if k == 0 else {"accs": [psB[bk]]}))
                for gi, w in ((0, 512), (1, 512), (2, 64)):
                    P.op("act", "activation", out=junk6[:, 0:w], in_=psb[4 + gi][:, 0:w], func=AF.Square,
                         accum_out=ss3[s][:, gi:gi + 1], reads=[psB[4 + gi]],
                         **({"writes": [junk6.b, ss3[s].b]} if gi == 0 else {"accs": [junk6.b, ss3[s].b]}))
                P.op("dve", "tensor_scalar", rs3[s][:, 0:2], ss3[s][:, 0:2], 1.0 / 512, EPS, ALU.mult, ALU.add,
                     reads=[ss3[s].b], writes=[rs3[s].b])
                P.op("dve", "tensor_scalar", rs3[s][:, 2:3], ss3[s][:, 2:3], 1.0 / 64, EPS, ALU.mult, ALU.add,
                     reads=[ss3[s].b], accs=[rs3[s].b])
                P.op("act", "activation", out=rs3[s][:], in_=rs3[s][:], func=AF.Sqrt, reads=[rs3[s].b], writes=[rs3[s].b])
                P.op("dve", "reciprocal", rs3[s][:], rs3[s][:], reads=[rs3[s].b], writes=[rs3[s].b])
                jobs = []
                if lat:
                    jobs.append((0, gq, qlatT.rearrange("(c p) t -> p c t", p=128)[:, :, j * 128:(j + 1) * 128]))
                if j < 16:
                    jobs.append((1, gkv, kvx_own[0:512, :].rearrange("(c p) t -> p c t", p=128)[:, :, j * 128:(j + 1) * 128]))
                if not lat:
                    jj = j - NQT
                    jobs.append((1, gkv, kvx_ctx[0:512, :].rearrange("(c p) t -> p c t", p=128)[:, :, jj * 128:(jj + 1) * 128]))
                for (gi, gain, dst) in jobs:
                    z = nst % 2
                    nst += 1
                    P.op("dve", "scalar_tensor_tensor", out=nb[z][:], in0=psb[4 + gi][:, :], scalar=rs3[s][:, gi:gi + 1], in1=gain[:],
                         op0=ALU.mult, op1=ALU.mult, reads=[psB[4 + gi], rs3[s].b, gain.b], writes=[nb[z].b])
                    tbk = 2 + z
                    transposes(nb[z], 4, [tbk])
                    evac(stg6[z][:].rearrange("p a b -> p (a b)"), psb[tbk][:, :], reads=[psB[tbk]], writes=[stg6[z].b])
                    P.dma("sp", ds_s[z], dst, stg6[z][:], reads=[stg6[z].b])
                if j < 16 or not lat:
                    P.op("dve", "scalar_tensor_tensor", out=krf[:], in0=psb[6][:, 0:64], scalar=rs3[s][:, 2:3], in1=gkr[:],
                         op0=ALU.mult, op1=ALU.mult, reads=[psB[6], rs3[s].b, gkr.b], writes=[krf.b])
                    if lat:
                        P.dma("sp", ds_c[s], cs1[s][:], rope1[j * 128:(j + 1) * 128, :], writes=[cs1[s].b])
                        P.op("dve", "tensor_tensor", kt1[:], krf[:], cs1[s][:, 0:64], ALU.mult, reads=[krf.b, cs1[s].b], writes=[kt1.b])
                        x4 = krf[:].rearrange("p (b e d) -> p b e d", b=2, e=2)
                        sn = cs1[s][:, 64:128].rearrange("p (b e d) -> p b e d", b=2, e=2)
                        o4 = kt2[:].rearrange("p (b e d) -> p b e d", b=2, e=2)
                        for e in range(2):
                            P.op("dve", "tensor_tensor", o4[:, :, e, :], x4[:, :, 1 - e, :], sn[:, :, e, :], ALU.mult,
                                 reads=[krf.b, cs1[s].b], **({"writes": [kt2.b]} if e == 0 else {"accs": [kt2.b]}))
                        P.op("dve", "tensor_tensor", krb[:], kt1[:], kt2[:], ALU.add, reads=[kt1.b, kt2.b], writes=[krb.b])
                    else:
                        P.op("dve", "tensor_copy", krb[:], krf[:], reads=[krf.b], writes=[krb.b])
                    P.op("pe", "matmul", psb[7][0:64, 0:128], krb[:, 0:64], ident[:], start=True, stop=True,
                         reads=[krb.b, ident.b], writes=[psB[7]])
                    evac(stgr[:], psb[7][0:64, 0:128], reads=[psB[7]], writes=[stgr.b])
                    if lat:
                        dstr = kvx_own[512:576, j * 128:(j + 1) * 128]
                    else:
                        dstr = kvx_ctx[512:576, (j - NQT) * 128:(j - NQT + 1) * 128]
                    P.dma("sp", ds_r, dstr, stgr[:], reads=[stgr.b])
            P.flush()
    if doB:
        if fused:
            kvx_pair = scr("kvx_pair", [5, 256, NOWN], BF16)
            cc = DSem(top.enter_context(nc.semaphore("cc_sem")), False, 1)
            for i in range(5):
                rows = 128 if i < 4 else 64
                P._rec("pool", "collective_compute", ("AllGather", ALU.bypass),
                       dict(replica_groups=[[0, 1], [2, 3], [4, 5], [6, 7]], ins=[kvx_own[128 * i:128 * i + rows, :]],
                            outs=[kvx_pair[i, 0:2 * rows, :]]), (), (), (), cc)
            P.flush()

            def kv_lat_src(kt):
                if kt < 32:
                    r, c = kt // 16, kt % 16
                    return kvx_pair[0:4, r * 128:(r + 1) * 128, c * 128:(c + 1) * 128].rearrange("c p t -> p c t")
                return kvx_ctx[0:512, :].rearrange("(c p) t -> p c t", p=128)[:, :, (kt - 32) * 128:(kt - 31) * 128]
            kr_srcs = [(0, NOWN, kvx_pair[4, 0:64, :]), (NOWN, 2 * NOWN, kvx_pair[4, 64:128, :]),
                       (2 * NOWN, NKV, kvx_ctx[512:576, :])]
        else:
            kv_lat_src = lambda kt: kvx_all[0:512, :].rearrange("(c p) t -> p c t", p=128)[:, :, kt * 128:(kt + 1) * 128]
            kr_srcs = [(0, NKV, kvx_all[512:576, :])]
        SCL = (128 + 64) ** -0.5
        with ExitStack() as p7:
            wkv = T(p7, "wkv", [128, 4, 4096], BF16)
            wq = T(p7, "wq", [128, 4, 3072], BF16)
            gkn = T(p7, "gkn", [128, 128], F32)
            gqn = T(p7, "gqn", [128, 128], F32)
            gqr = T(p7, "gqr", [128, 64], F32)
            lt = [T(p7, f"lt{i}", [128, 4, 128], BF16) for i in range(2)]
            ss2 = [T(p7, f"ss2_{i}", [128, 4], F32) for i in range(2)]
            rs2 = [T(p7, f"rs2_{i}", [128, 4], F32) for i in range(2)]
            junk7 = T(p7, "junk7", [128, 128], BF16)
            xb7 = [T(p7, f"xb7_{i}", [128, 256], BF16) for i in range(2)]
            st7 = [T(p7, f"st7_{i}", [128, 2, 128], BF16) for i in range(2)]
            vs7 = [T(p7, f"vs7_{i}", [128, 2, 128], BF16) for i in range(2)]
            qrf = T(p7, "qrf", [128, 128], F32)
            qt1 = T(p7, "qt1", [128, 128], F32)
            qt2 = T(p7, "qt2", [128, 128], F32)
            qrb = [T(p7, f"qrb{i}", [128, 128], BF16) for i in range(2)]
            sr7 = [T(p7, f"sr7_{i}", [128, 128], BF16) for i in range(2)]
            cs1 = [T(p7, f"cs7_{i}", [128, 128], F32) for i in range(2)]
            ds_w = [P.dsem("x") for _ in range(2)]
            ds_l = [P.dsem("x") for _ in range(2)]
            ds_k = [P.dsem("x") for _ in range(2)]
            ds_v = [P.dsem("x") for _ in range(2)]
            ds_q = [P.dsem("x") for _ in range(2)]
            ds_c = [P.dsem("x") for _ in range(2)]
            P.dma("pool", ds_w[0], wkv[:], mla_w_ukv.rearrange("(k p) n -> p k n", p=128), writes=[wkv.b])
            P.dma("pool", ds_w[1], wq[:], mla_w_uq.rearrange("(k p) n -> p k n", p=128), writes=[wq.b])
            bcast_load(gkn, ds_misc, small_row("m_kn"))
            bcast_load(gqn, ds_misc, small_row("m_qn"))
            bcast_load(gqr, ds_misc, small_row("m_qr"))
            P.op("dve", "tensor_scalar", gqn[:], gqn[:], SCL, None, ALU.mult, reads=[gqn.b], writes=[gqn.b])
            P.op("dve", "tensor_scalar", gqr[:], gqr[:], SCL, None, ALU.mult, reads=[gqr.b], writes=[gqr.b])
            u = 0
            for kt in range(NKT):
                ls = kt % 2
                P.dma("sp", ds_l[ls], lt[ls][:], kv_lat_src(kt), writes=[lt[ls].b])
                for g in range(8):
                    z = u % 2
                    bk = z
                    for k in range(4):
                        P.op("pe", "matmul", psb[bk][:, :], lt[ls][:, k, :], wkv[:, k, g * 512:(g + 1) * 512], start=(k == 0), stop=(k == 3),
                             reads=[lt[ls].b, wkv.b], **({"writes": [psB[bk]]} if k == 0 else {"accs": [psB[bk]]}))
                    for hh in range(2):
                        P.op("act", "activation", out=junk7[:], in_=psb[bk][:, hh * 256:hh * 256 + 128], func=AF.Square,
                             accum_out=ss2[z][:, hh:hh + 1], reads=[psB[bk]],
                             **({"writes": [junk7.b, ss2[z].b]} if hh == 0 else {"accs": [junk7.b, ss2[z].b]}))
                    rstd_from_ss(rs2[z], ss2[z], 128, 2)
                    for hh in range(2):
                        P.op("dve", "scalar_tensor_tensor", out=xb7[z][:, hh * 128:(hh + 1) * 128],
                             in0=psb[bk][:, hh * 256:hh * 256 + 128], scalar=rs2[z][:, hh:hh + 1], in1=gkn[:],
                             op0=ALU.mult, op1=ALU.mult, reads=[psB[bk], rs2[z].b, gkn.b],
                             **({"writes": [xb7[z].b]} if hh == 0 else {"accs": [xb7[z].b]}))
                    evac(vs7[z][:], psb[bk][:].rearrange("p (h a d) -> p h a d", h=2, a=2)[:, :, 1, :], reads=[psB[bk]], writes=[vs7[z].b])
                    P.dma("sp", ds_v[z], V1[2 * g:2 * g + 2, kt * 128:(kt + 1) * 128, :].rearrange("h t d -> t h d"), vs7[z][:],
                          reads=[vs7[z].b])
                    tbk = 2 + z
                    transposes(xb7[z], 2, [tbk])
                    evac(st7[z][:].rearrange("p a b -> p (a b)"), psb[tbk][:, 0:256], reads=[psB[tbk]], writes=[st7[z].b])
                    P.dma("sp", ds_k[z], knT[2 * g:2 * g + 2, :, kt * 128:(kt + 1) * 128].rearrange("h d t -> d h t"), st7[z][:],
                          reads=[st7[z].b])
                    u += 1
            for j in range(NQT):
                ls = j % 2
                P.dma("sp", ds_l[ls], lt[ls][:], qlatT.rearrange("(c p) t -> p c t", p=128)[:, :, j * 128:(j + 1) * 128],
                      writes=[lt[ls].b])
                P.dma("sp", ds_c[ls], cs1[ls][:], rope1[j * 128:(j + 1) * 128, :], writes=[cs1[ls].b])
                for g in range(8):
                    z = u % 2
                    bk = z
                    for k in range(4):
                        P.op("pe", "matmul", psb[bk][:, 0:384], lt[ls][:, k, :], wq[:, k, g * 384:(g + 1) * 384], start=(k == 0), stop=(k == 3),
                             reads=[lt[ls].b, wq.b], **({"writes": [psB[bk]]} if k == 0 else {"accs": [psB[bk]]}))
                    for hh in range(2):
                        P.op("act", "activation", out=junk7[:], in_=psb[bk][:, hh * 192:hh * 192 + 128], func=AF.Square,
                             accum_out=ss2[z][:, hh:hh + 1], reads=[psB[bk]],
                             **({"writes": [junk7.b, ss2[z].b]} if hh == 0 else {"accs": [junk7.b, ss2[z].b]}))
                        P.op("act", "activation", out=junk7[:, 0:64], in_=psb[bk][:, hh * 192 + 128:hh * 192 + 192], func=AF.Square,
                             accum_out=ss2[z][:, 2 + hh:3 + hh], reads=[psB[bk]], accs=[junk7.b, ss2[z].b])
                    P.op("dve", "tensor_scalar", rs2[z][:, 0:2], ss2[z][:, 0:2], 1.0 / 128, EPS, ALU.mult, ALU.add,
                         reads=[ss2[z].b], writes=[rs2[z].b])
                    P.op("dve", "tensor_scalar", rs2[z][:, 2:4], ss2[z][:, 2:4], 1.0 / 64, EPS, ALU.mult, ALU.add,
                         reads=[ss2[z].b], accs=[rs2[z].b])
                    P.op("act", "activation", out=rs2[z][:], in_=rs2[z][:], func=AF.Sqrt, reads=[rs2[z].b], writes=[rs2[z].b])
                    P.op("dve", "reciprocal", rs2[z][:], rs2[z][:], reads=[rs2[z].b], writes=[rs2[z].b])
                    for hh in range(2):
                        P.op("dve", "scalar_tensor_tensor", out=xb7[z][:, hh * 128:(hh + 1) * 128],
                             in0=psb[bk][:, hh * 192:hh * 192 + 128], scalar=rs2[z][:, hh:hh + 1], in1=gqn[:],
                             op0=ALU.mult, op1=ALU.mult, reads=[psB[bk], rs2[z].b, gqn.b],
                             **({"writes": [xb7[z].b]} if hh == 0 else {"accs": [xb7[z].b]}))
                        P.op("dve", "scalar_tensor_tensor", out=qrf[:, hh * 64:(hh + 1) * 64],
                             in0=psb[bk][:, hh * 192 + 128:hh * 192 + 192], scalar=rs2[z][:, 2 + hh:3 + hh], in1=gqr[:],
                             op0=ALU.mult, op1=ALU.mult, reads=[psB[bk], rs2[z].b, gqr.b],
                             **({"writes": [qrf.b]} if hh == 0 else {"accs": [qrf.b]}))
                    P.op("dve", "tensor_tensor", qt1[:].rearrange("p (h d) -> p h d", h=2), qrf[:].rearrange("p (h d) -> p h d", h=2),
                         cs1[ls][:, None, 0:64].broadcast_to([128, 2, 64]), ALU.mult, reads=[qrf.b, cs1[ls].b], writes=[qt1.b])
                    x5 = qrf[:].rearrange("p (h b e d) -> p h b e d", h=2, b=2, e=2)
                    o5 = qt2[:].rearrange("p (h b e d) -> p h b e d", h=2, b=2, e=2)
                    sn = cs1[ls][:, 64:128].rearrange("p (b e d) -> p b e d", b=2, e=2)
                    for e in range(2):
                        P.op("dve", "tensor_tensor", o5[:, :, :, e, :], x5[:, :, :, 1 - e, :],
                             sn[:, None, :, e, :].broadcast_to([128, 2, 2, 16]), ALU.mult,
                             reads=[qrf.b, cs1[ls].b], **({"writes": [qt2.b]} if e == 0 else {"accs": [qt2.b]}))
                    P.op("dve", "tensor_tensor", qrb[z][:], qt1[:], qt2[:], ALU.add, reads=[qt1.b, qt2.b], writes=[qrb[z].b])
                    tbk = 2 + z
                    transposes(xb7[z], 2, [tbk])
                    evac(st7[z][:].rearrange("p a b -> p (a b)"), psb[tbk][:, 0:256], reads=[psB[tbk]], writes=[st7[z].b])
                    P.dma("sp", ds_k[z], qnT[2 * g:2 * g + 2, :, j * 128:(j + 1) * 128].rearrange("h d t -> d h t"), st7[z][:],
                          reads=[st7[z].b])
                    rbk = 4 + z
                    P.op("pe", "matmul", psb[rbk][:, 0:128], qrb[z][:], ident[:], start=True, stop=True,
                         reads=[qrb[z].b, ident.b], writes=[psB[rbk]])
                    evac(sr7[z][:], psb[rbk][:, 0:128], reads=[psB[rbk]], writes=[sr7[z].b])
                    P.dma("sp", ds_q[z], qrT[2 * g:2 * g + 2].rearrange("h d t -> (h d) t")[:, j * 128:(j + 1) * 128], sr7[z][:],
                          reads=[sr7[z].b])
                    u += 1
            P.flush()
        with ExitStack() as ph:
            O1 = T(ph, "O1", [128, NQT, D], BF16)
            with ExitStack() as p8:
                kn = [T(p8, f"kn{i}", [128, NKV], BF16) for i in range(2)]
                v1 = [T(p8, f"v1_{i}", [128, NKT, 129], BF16) for i in range(2)]
                qn = [T(p8, f"qn{i}", [128, NQ], BF16) for i in range(2)]
                qr = [T(p8, f"qr{i}", [64, NQ], BF16) for i in range(2)]
                kr = T(p8, "kr", [64, NKV], BF16)
                pTs = [T(p8, f"mpT{i}", [128, 512], BF16) for i in range(3)]
                rinv = [T(p8, f"mrinv{i}", [128, 1], F32) for i in range(2)]
                dsl = [[P.dsem("x") for _ in range(4)] for _ in range(2)]
                for i, (c0, c1, src) in enumerate(kr_srcs):
                    P.dma("sp", ds_misc, kr[:, c0:c1], src, **({"writes": [kr.b]} if i == 0 else {"accs": [kr.b]}))
                for i in range(2):
                    P.op("dve", "memset", v1[i][:, :, 128:129], 1.0, writes=[v1[i].b])
                fcount = [0]
                for h in range(16):
                    s = h % 2
                    P.dma("sp", dsl[s][0], kn[s][:], knT[h], writes=[kn[s].b])
                    P.dma("sp", dsl[s][1], v1[s][:, :, 0:128], V1[h].rearrange("(kt p) d -> p kt d", p=128), writes=[v1[s].b])
                    P.dma("sp", dsl[s][2], qn[s][:], qnT[h], writes=[qn[s].b])
                    P.dma("sp", dsl[s][3], qr[s][:], qrT[h], writes=[qr[s].b])
                    for (q0, nq) in ((0, 512), (512, 512), (1024, 512), (1536, 512), (2048, 128)):
                        def fin(sub, ob, q0=q0, h=h):
                            z = fcount[0] % 2
                            fcount[0] += 1
                            tile = q0 // 128 + sub
                            P.op("dve", "reciprocal", rinv[z][:], psb[ob][:, 128:129], reads=[psB[ob]], writes=[rinv[z].b])
                            P.op("dve", "tensor_scalar", O1[:, tile, h * 128:(h + 1) * 128], psb[ob][:, 0:128], rinv[z][:, 0:1], None,
                                 ALU.mult, reads=[psB[ob], rinv[z].b], accs=[O1.b])
                        attn_dense([(kn[s], 128, qn[s]), (kr, 64, qr[s])], v1[s], list(range(NKT)), q0, nq, 128,
                                   [0, 1, 2], pTs, [3, 4, 5, 6], fin)
                P.flush()
            with ExitStack() as p9:
                zt = T(p9, "zt9", [128, 16, 1], BF16)
                P.op("dve", "memset", zt[:], 0.0, writes=[zt.b])
                hl = hTf_lat.rearrange("(k p) t -> p k t", p=128)
                P.dma("sp", ds_misc, hl[:, :, 0:1], zt[:], reads=[zt.b], allow_slow_non_contiguous=True)
                post_attn(p9, O1, NQT, NQT, mla_w_out, 1,
                          lambda j: x1q[j * 128:(j + 1) * 128, :],
                          lambda j: xmid1[j * 128:(j + 1) * 128, :],
                          lambda j: hl[:, :, 1 + j * 128:1 + (j + 1) * 128])
                P.flush()
        with ExitStack() as p10:
            units = [{"src": hTf_lat, "c0": 256 * u, "n": 256, "res": xmid1[256 * u:256 * (u + 1), :],
                      "dst": out[256 * u:256 * (u + 1), :], "row": 0} for u in range(8)]
            ffn(p10, 1, [units[2 * i:2 * i + 2] for i in range(4)])
            P.flush()
    top.close()
    return nc


_UPTO = [99]


def _perm(s):
    L = np.arange(SEQ)
    return L if s == 0 else (SEQ - 1 - L)


def _rope_tables(tok, half):
    freqs = (np.float32(10000.0) ** (-np.arange(half, dtype=np.float32) / np.float32(half))).astype(np.float32)
    r = (tok // 64).astype(np.float32)[:, None] * freqs[None, :]
    c = (tok % 64).astype(np.float32)[:, None] * freqs[None, :]
    cr, sr, ccs, sc = np.cos(r), np.sin(r), np.cos(c), np.sin(c)
    cos = np.concatenate([cr, cr, ccs, ccs], 1)
    sin = np.concatenate([-sr, sr, -sc, sc], 1)
    return np.concatenate([cos, sin], 1).astype(np.float32)


def _bias_tables(rpb, s):
    perm = _perm(s)
    out = np.empty((8, 128, 3, 5, 128), np.float32)
    p = np.arange(128)
    for v in range(3):
        tq = perm[128 * v + p]
        r, c = tq // 64, tq % 64
        rs0 = np.clip(r - 4, 0, 56)
        cs0 = np.clip(c - 8, 0, 48)
        for i in range(5):
            tk = perm[128 * i + p]
            kr, kc = tk // 64, tk % 64
            okr = (kr[:, None] >= rs0[None, :]) & (kr[:, None] < rs0[None, :] + 8)
            okc = (kc[:, None] >= cs0[None, :]) & (kc[:, None] < cs0[None, :] + 16)
            dr = np.clip(kr[:, None] - r[None, :] + 7, 0, 14)
            dc = np.clip(kc[:, None] - c[None, :] + 15, 0, 30)
            ok = okr & okc
            for h in range(8):
                out[h, :, v, i, :] = np.where(ok, rpb[h][dr, dc], np.float32(NEG))
    return out.reshape(8, 128, 3 * 5 * 128)


def _smalls(inp):
    v = np.zeros((1, NSM), np.float32)

    def put(name, arr):
        o, n = SM[name]
        a = np.asarray(arr, np.float32).reshape(-1)
        v[0, o:o + n] = np.tile(a, n // a.size)
    put("na_q", inp["na_q_norm"][0]); put("na_k", inp["na_k_norm"][0])
    put("df_q", inp["diff_q_norm"][0]); put("df_k", inp["diff_k_norm"][0])
    put("df_lam", inp["diff_lambda"][0]); put("df_sub", inp["diff_subln"][0])
    put("m_qa", inp["mla_q_a_norm"][0]); put("m_kva", inp["mla_kv_a_norm"][0])
    put("m_qn", inp["mla_q_nope_norm"][0]); put("m_qr", inp["mla_q_rope_norm"][0])
    put("m_kn", inp["mla_k_nope_norm"][0]); put("m_kr", inp["mla_k_rope_norm"][0])
    put("nmix0", inp["norm_mix"][0]); put("nmix1", inp["norm_mix"][1])
    put("nffn0", inp["norm_ffn"][0]); put("nffn1", inp["norm_ffn"][1])
    return v


def prep_shared(inp):
    sh = {
        "ident": np.eye(128, dtype=np.float32),
        "smalls": _smalls(inp),
        "ffn_w_in": np.ascontiguousarray(inp["ffn_w_in"], np.float32),
        "ffn_w_out": np.ascontiguousarray(inp["ffn_w_out"], np.float32),
        "ada_w": np.ascontiguousarray(inp["ada_w"], np.float32),
        "ada_b": np.ascontiguousarray(inp["ada_b"], np.float32).reshape(1, -1),
        "even_w_in": np.ascontiguousarray(inp["even_w_in"][0], np.float32),
        "even_w_out": np.ascontiguousarray(inp["even_w_out"][0], np.float32),
        "mla_w_down": np.ascontiguousarray(inp["mla_w_down"][0], np.float32),
        "mla_w_uq": np.ascontiguousarray(inp["mla_w_uq"][0], np.float32),
        "mla_w_ukv": np.ascontiguousarray(inp["mla_w_ukv"][0], np.float32),
        "mla_w_out": np.ascontiguousarray(inp["mla_w_out"][0], np.float32),
    }
    per_s = []
    for s in range(2):
        perm = _perm(s)
        conv = np.asarray(inp["ffn_conv"], np.float32)
        if s == 1:
            conv = conv[:, ::-1, :]
        cwt = conv.reshape(2, 3, 86, 128).transpose(0, 3, 2, 1).reshape(2, 128, 86 * 3)
        per_s.append({
            "cwt": np.ascontiguousarray(cwt),
            "rope0": _rope_tables(perm, 32),
            "rope1": _rope_tables(perm[:NQ], 16),
            "biasT": _bias_tables(np.asarray(inp["na_rpb"][0], np.float32), s),
        })
    return sh, per_s


def prep_core(inp, sh, per_s, core):
    b, s = core // 2, core % 2
    perm = _perm(s)
    x = np.asarray(inp["x"], np.float32)
    ctx = np.asarray(inp["ctx"], np.float32)
    cc = np.empty((128, 16, 2), np.float32)
    cc[:, :, 0] = np.asarray(inp["c"], np.float32)[b].reshape(16, 128).T
    cc[:, :, 1] = np.asarray(inp["c_ctx"], np.float32).reshape(16, 128).T
    m = dict(sh)
    m.update(per_s[s])
    m["xl"] = np.ascontiguousarray(x[b][perm])
    m["cl"] = np.ascontiguousarray(ctx[b] if s == 0 else ctx[b][::-1])
    m["cc"] = cc.reshape(128, 32)
    return m


A_INPUTS = ("ident", "smalls", "ffn_w_in", "ffn_w_out", "cwt", "xl", "cl", "cc", "ada_w", "ada_b", "even_w_in", "even_w_out",
            "rope0", "rope1", "biasT", "mla_w_down")
B_INPUTS = ("ident", "smalls", "ffn_w_in", "ffn_w_out", "cwt", "mla_w_uq", "mla_w_ukv", "mla_w_out", "rope1")


def _stage_inputs(m, names, layer):
    d = {}
    for k in names:
        v = m[k]
        if k in ("ffn_w_in", "ffn_w_out", "cwt"):
            v = v[layer:layer + 1]
        d[k] = v
    return d


def run_stage_A(inputs, cores, sh=None, per_s=None):
    if sh is None:
        sh, per_s = prep_shared(inputs)
    nc = build_program("A")
    in_maps = [_stage_inputs(prep_core(inputs, sh, per_s, c), A_INPUTS, 0) for c in cores]
    res = run_bass_kernel_spmd(nc, in_maps, core_ids=list(range(len(cores))))
    return res.results


def run_stage_B(inputs, cores, resA, sh=None, per_s=None):
    if sh is None:
        sh, per_s = prep_shared(inputs)
    nc = build_program("B")
    in_maps = []
    for c in cores:
        m = dict(sh)
        m.update(per_s[c % 2])
        d = _stage_inputs(m, B_INPUTS, 1)
        ra, rp = resA[c], resA[c ^ 1]
        d["modS"] = ra["modS"]
        d["x1q"] = ra["x1q"]
        d["qlatT"] = ra["qlatT"]
        d["kvx_all"] = np.ascontiguousarray(np.concatenate([ra["kvx_own"], rp["kvx_own"], ra["kvx_ctx"]], axis=1))
        in_maps.append(d)
    res = run_bass_kernel_spmd(nc, in_maps, core_ids=list(range(len(cores))))
    return res.results


FUSED = True
AB_INPUTS = A_INPUTS + ("mla_w_uq", "mla_w_ukv", "mla_w_out")


def kernel_fused(inputs):
    sh, per_s = prep_shared(inputs)
    cores = list(range(8))
    nc = build_program("AB")
    in_maps = []
    for c in cores:
        m = prep_core(inputs, sh, per_s, c)
        in_maps.append({k: m[k] for k in AB_INPUTS})
    res = run_bass_kernel_spmd(nc, in_maps, core_ids=cores)
    out = np.empty((4, SEQ, D), np.float32)
    for c in cores:
        b, s = c // 2, c % 2
        out[b][_perm(s)[:NOWN]] = res.results[c]["out"]
    return out


def kernel(**inputs):
    inputs = {k: np.asarray(v) for k, v in inputs.items()}
    if FUSED:
        return kernel_fused(inputs)
    sh, per_s = prep_shared(inputs)
    cores = list(range(8))
    ra = run_stage_A(inputs, cores, sh, per_s)
    keep = ("modS", "x1q", "qlatT", "kvx_own", "kvx_ctx")
    resA = {c: {k: ra[c][k] for k in keep} for c in cores}
    del ra
    rb = run_stage_B(inputs, cores, resA, sh, per_s)
    out = np.empty((4, SEQ, D), np.float32)
    for c in cores:
        b, s = c // 2, c % 2
        out[b][_perm(s)[:NOWN]] = rb[c]["out"]
    return out
```

```python
import math
from contextlib import ExitStack

import numpy as np
import concourse.bass as bass
import concourse.mybir as mybir
from concourse.bass_utils import run_bass_kernel_spmd

F32 = mybir.dt.float32
BF16 = mybir.dt.bfloat16
AF = mybir.ActivationFunctionType
ALU = mybir.AluOpType
AX = mybir.AxisListType

D = 2048
KC = 16
SEQ = 4096
CTX = 256
NOWN = 2048
NHALO = 128
NQ = NOWN + NHALO
NQT = NQ // 128
NQC = NQ + CTX
NKV = SEQ + CTX
NKT = NKV // 128
DFF = 5504
FC = DFF // 128
EPS = 1e-6
NEG = -30000.0
NA_KT = 21
LAM_INIT0 = 0.8 - 0.6 * math.exp(-0.3 * 0)


class Buf:
    __slots__ = ("name", "writers", "readers")

    def __init__(self, name=""):
        self.name = name
        self.writers = []
        self.readers = []


class DSem:
    def __init__(self, h, shared=False, step=16):
        self.h = h
        self.count = 0
        self.step = step
        self.shared = shared
        self.cell = [0]
        self.closed = False


class Ins:
    __slots__ = ("eng", "meth", "args", "kw", "deps", "marked", "val", "dsem", "cell", "raw")


class Prog:
    CENG = ("pe", "act", "dve", "pool")

    def __init__(self, nc, stack):
        self.nc = nc
        self.stack = stack
        self.engs = {"pe": nc.tensor, "act": nc.scalar, "dve": nc.vector, "pool": nc.gpsimd, "sp": nc.sync}
        self.csem = {e: stack.enter_context(nc.semaphore("cs_" + e)) for e in self.CENG}
        self.ccount = {e: 0 for e in self.CENG}
        self.bar = stack.enter_context(nc.semaphore("bar"))
        self.barcount = 0
        self.lists = {e: [] for e in self.engs}
        self.known = {e: {} for e in self.engs}
        self.bufs = []
        self.dsems = []
        self.dnext = 0
        self.ninstr = 0

    def buf(self, name=""):
        b = Buf(name)
        self.bufs.append(b)
        return b

    def dsem(self, name, shared=False):
        if shared:
            return DSem(self.stack.enter_context(self.nc.semaphore(name)), True)
        if self.dnext == len(self.dsems):
            self.dsems.append(DSem(self.stack.enter_context(self.nc.semaphore(f"dp{self.dnext}")), False))
        d = self.dsems[self.dnext]
        self.dnext += 1
        return d

    def _rec(self, eng, meth, args, kw, reads, writes, accs, dsem):
        ins = Ins()
        ins.eng, ins.meth, ins.args, ins.kw = eng, meth, args, kw
        ins.marked = False
        ins.val = 0
        ins.dsem = dsem
        ins.cell = None
        ins.raw = []
        deps = []
        for b in reads:
            deps.extend(b.writers)
        for b in writes:
            deps.extend(b.writers)
            deps.extend(b.readers)
        for b in accs:
            deps.extend(b.readers)
        seen = set()
        ins.deps = []
        for p in deps:
            if id(p) in seen:
                continue
            seen.add(id(p))
            if p.eng == "pe" and eng == "pe" and p.dsem is None:
                continue
            p.marked = True
            ins.deps.append(p)
            if p.dsem is not None and p.dsem.shared:
                p.dsem.closed = True
        for b in reads:
            b.readers.append(ins)
        for b in writes:
            b.writers = [ins]
            b.readers = []
        for b in accs:
            b.writers.append(ins)
        if dsem is not None:
            if dsem.shared:
                assert eng == "sp"
                if dsem.closed:
                    ins.raw.append((dsem.h, dsem.count))
                    dsem.cell = [0]
                    dsem.closed = False
                ins.cell = dsem.cell
            dsem.count += dsem.step
            ins.val = dsem.count
            if dsem.shared:
                dsem.cell[0] = dsem.count
            ins.marked = True
        self.lists[eng].append(ins)
        self.ninstr += 1
        return ins

    def op(self, eng, meth, *args, reads=(), writes=(), accs=(), **kw):
        return self._rec(eng, meth, args, kw, reads, writes, accs, None)

    def dma(self, eng, dsem, out, in_, reads=(), writes=(), accs=(), **kw):
        return self._rec(eng, "dma_start", (), dict(out=out, in_=in_, **kw), reads, writes, accs, dsem)

    def _token(self, p):
        if p.dsem is not None:
            return p.dsem.h, (p.cell[0] if p.cell is not None else p.val)
        return self.csem[p.eng], p.val

    def flush(self):
        for e in self.CENG:
            for ins in reversed(self.lists[e]):
                if ins.dsem is None:
                    ins.marked = True
                    break
        for e in self.CENG:
            for ins in self.lists[e]:
                if ins.dsem is None and ins.marked:
                    self.ccount[e] += 1
                    ins.val = self.ccount[e]
        self.barcount += 1
        dma_final = {e: {} for e in self.engs}
        for e in self.engs:
            for ins in self.lists[e]:
                if ins.dsem is not None:
                    dma_final[e][id(ins.dsem)] = (ins.dsem.h, ins.val)
        ndma_eng = sum(1 for e in self.engs if dma_final[e])
        bar_target_add = ndma_eng
        self._bar_total = getattr(self, "_bar_total", 0) + bar_target_add
        bar_total = self._bar_total
        cfinal = dict(self.ccount)
        lists = self.lists

        def make_body(e):
            def body(eng):
                known = self.known[e]
                for ins in lists[e]:
                    waits = {}
                    for h, v in ins.raw:
                        if known.get(id(h), 0) < v:
                            waits[id(h)] = (h, v)
                    for p in ins.deps:
                        h, v = self._token(p)
                        if known.get(id(h), 0) >= v:
                            continue
                        if id(h) not in waits or waits[id(h)][1] < v:
                            waits[id(h)] = (h, v)
                    for h, v in waits.values():
                        eng.wait_ge(h, v)
                        known[id(h)] = v
                    bi = getattr(eng, ins.meth)(*ins.args, **ins.kw)
                    if ins.dsem is not None:
                        bi.then_inc(ins.dsem.h, ins.dsem.step)
                    elif ins.marked:
                        bi.then_inc(self.csem[e], 1)
                if dma_final[e]:
                    for h, v in dma_final[e].values():
                        if known.get(id(h), 0) < v:
                            eng.wait_ge(h, v)
                            known[id(h)] = v
                    eng.sem_inc(self.bar, 1)
                for ce in self.CENG:
                    h, v = self.csem[ce], cfinal[ce]
                    if v > 0 and known.get(id(h), 0) < v:
                        eng.wait_ge(h, v)
                        known[id(h)] = v
                if bar_total > 0 and known.get(id(self.bar), 0) < bar_total:
                    eng.wait_ge(self.bar, bar_total)
                    known[id(self.bar)] = bar_total
            return body

        with self.nc.Block() as block:
            block.tensor(make_body("pe"))
            block.scalar(make_body("act"))
            block.vector(make_body("dve"))
            block.gpsimd(make_body("pool"))
            block.sync(make_body("sp"))
        self.lists = {e: [] for e in self.engs}
        self.dnext = 0
        for b in self.bufs:
            b.writers = []
            b.readers = []


SM = {}
_off = 0
for _n, _sz in (("na_q", 512), ("na_k", 512), ("df_q", 512), ("df_k", 512), ("df_lam", 512), ("df_sub", 256),
                ("m_qa", 512), ("m_kva", 512), ("m_qn", 128), ("m_qr", 64), ("m_kn", 128), ("m_kr", 64),
                ("nmix0", 2048), ("nmix1", 2048), ("nffn0", 2048), ("nffn1", 2048)):
    SM[_n] = (_off, _sz)
    _off += _sz
NSM = _off


def build_program(stage, debug=False):
    nc = bass.Bass("TRN2", target_bir_lowering=False)
    doA = "A" in stage
    doB = "B" in stage
    fused = stage == "AB"

    def din(name, shape, dt=F32):
        return nc.dram_tensor(name, list(shape), dt, kind="ExternalInput").ap()

    def dout(name, shape, dt=F32):
        return nc.dram_tensor(name, list(shape), dt, kind="ExternalOutput").ap()

    def scr(name, shape, dt):
        return nc.dram_tensor(name, list(shape), dt, kind="ExternalOutput" if debug else "Internal").ap()

    def xfer(name, shape, dt):
        if fused:
            return scr(name, shape, dt)
        if stage == "A":
            return dout(name, shape, dt)
        return din(name, shape, dt)

    ident_in = din("ident", [128, 128])
    smalls = din("smalls", [1, NSM])
    NL = 2 if fused else 1
    ffn_w_in = din("ffn_w_in", [NL, D, 2 * DFF])
    ffn_w_out = din("ffn_w_out", [NL, DFF, D])
    cwt_in = din("cwt", [NL, 128, 86 * 3])
    modS = xfer("modS", [2, 2 * 6 * D], F32)
    x1q = xfer("x1q", [NQ, D], F32)
    qlatT = xfer("qlatT", [512, NQ], BF16)
    if doA:
        kvx_own = xfer("kvx_own", [576, NOWN], BF16)
        kvx_ctx = xfer("kvx_ctx", [576, CTX], BF16)
        xl = din("xl", [SEQ, D])
        cl = din("cl", [CTX, D])
        cc_in = din("cc", [128, 32])
        ada_w = din("ada_w", [2, D, 6 * D])
        ada_b = din("ada_b", [1, 2 * 6 * D])
        even_w_in = din("even_w_in", [D, 6144])
        even_w_out = din("even_w_out", [D, D])
        rope0 = din("rope0", [SEQ, 256])
        rope1 = din("rope1", [NQ, 128])
        biasT = din("biasT", [8, 128, 3 * 5 * 128])
        mla_w_down = din("mla_w_down", [D, 1088])
        QaT = scr("QaT", [8, 128, NQC], BF16)
        KaT = scr("KaT", [8, 128, NA_KT * 128], BF16)
        Va = scr("Va", [8, NA_KT * 128, 128], BF16)
        QbT = scr("QbT", [8, 128, NQC], BF16)
        KbT = scr("KbT", [8, 128, NKV], BF16)
        Vb = scr("Vb", [4, NKV, 256], BF16)
        xmid = scr("xmid", [NQ, D], F32)
        xcmid = scr("xcmid", [CTX, D], F32)
        xc1 = scr("xc1", [CTX, D], F32)
    if doB:
        mla_w_uq = din("mla_w_uq", [512, 3072])
        mla_w_ukv = din("mla_w_ukv", [512, 4096])
        mla_w_out = din("mla_w_out", [D, D])
        if not fused:
            kvx_all = din("kvx_all", [576, NKV], BF16)
            rope1 = din("rope1", [NQ, 128])
        knT = scr("knT", [16, 128, NKV], BF16)
        V1 = scr("V1", [16, NKV, 128], BF16)
        qnT = scr("qnT", [16, 128, NQ], BF16)
        qrT = scr("qrT", [16, 64, NQ], BF16)
        xmid1 = scr("xmid1", [NQ, D], F32)
        out = dout("out", [NOWN, D], F32)
    hTf_lat = scr("hTf_lat", [D, NQ + 2], BF16)
    hTf_ctx = scr("hTf_ctx", [D, CTX + 2], BF16)

    top = ExitStack()
    P = Prog(nc, top)
    psb = [top.enter_context(nc.psum_tensor(f"ps{i}", [128, 512], F32)) for i in range(8)]
    psB = [P.buf(f"ps{i}") for i in range(8)]

    _uid = [0]

    class T:
        def __init__(self, st, name, shape, dt):
            _uid[0] += 1
            self.t = st.enter_context(nc.sbuf_tensor(f"sb{_uid[0]}_{name}", list(shape), dt))
            self.b = P.buf(name)

        def __getitem__(self, k):
            return self.t[k]

    ident = T(top, "ident", [128, 128], BF16)
    identf = T(top, "identf", [128, 128], F32)
    ds_misc = P.dsem("ds_misc", shared=True)
    P.dma("sp", ds_misc, identf[:], ident_in[:, :], writes=[identf.b])
    P.op("dve", "tensor_copy", ident[:], identf[:], reads=[identf.b], writes=[ident.b])

    def bcast_load(dst, dsem, row_ap, eng="sp"):
        P.dma(eng, dsem, dst[:], row_ap.partition_broadcast(128), writes=[dst.b])

    def small_row(name):
        o, n = SM[name]
        return smalls[0:1, o:o + n]

    def mod_row(r, l, v):
        o = l * 6 * D + v * D
        return modS[r:r + 1, o:o + D]

    _rr = [0]

    def evac(out_ap, in_ap, reads, writes=(), accs=()):
        _rr[0] ^= 1
        if _rr[0]:
            P.op("act", "activation", out=out_ap, in_=in_ap, func=AF.Copy, reads=reads, writes=writes, accs=accs)
        else:
            P.op("dve", "tensor_copy", out_ap, in_ap, reads=reads, writes=writes, accs=accs)

    def rstd_from_ss(rs, ss, n, width=1):
        P.op("dve", "tensor_scalar", rs[:, 0:width], ss[:, 0:width], 1.0 / n, EPS, ALU.mult, ALU.add,
             reads=[ss.b], writes=[rs.b])
        P.op("act", "activation", out=rs[:, 0:width], in_=rs[:, 0:width], func=AF.Sqrt, reads=[rs.b], writes=[rs.b])
        P.op("dve", "reciprocal", rs[:, 0:width], rs[:, 0:width], reads=[rs.b], writes=[rs.b])

    def transposes(src, nblk, banks, width=128, kp=128):
        for i in range(nblk):
            bk = banks[i // 4]
            col = (i % 4) * 128
            P.op("pe", "matmul", psb[bk][0:width, col:col + 128], src[:, i * width:(i + 1) * width], ident[:],
                 start=True, stop=True, reads=[src.b, ident.b],
                 **({"writes": [psB[bk]]} if i % 4 == 0 else {"accs": [psB[bk]]}))

    if doA:
        with ExitStack() as ph:
            cc = T(ph, "cc", [128, 32], F32)
            sT = T(ph, "sT", [128, 16, 2], BF16)
            adab = T(ph, "adab", [2, 6 * D], F32)
            modsb = T(ph, "modsb", [2, 6 * D], F32)
            wa = [T(ph, f"wa{i}", [128, 16, 512], BF16) for i in range(3)]
            ds_wa = [P.dsem(f"ds_wa{i}") for i in range(3)]
            P.dma("sp", ds_misc, cc[:], cc_in[:, :], writes=[cc.b])
            P.op("act", "activation", out=sT[:].rearrange("p k m -> p (k m)"), in_=cc[:], func=AF.Silu,
                 reads=[cc.b], writes=[sT.b])
            n = 0
            for l in range(2):
                wsrc = ada_w[l].rearrange("(k p) n -> p k n", p=128)
                P.dma("sp", ds_misc, adab[:], ada_b[0:1, l * 6 * D:(l + 1) * 6 * D].partition_broadcast(2), writes=[adab.b])
                for g in range(24):
                    s = n % 3
                    P.dma("pool", ds_wa[s], wa[s][:], wsrc[:, :, g * 512:(g + 1) * 512], writes=[wa[s].b])
                    bk = n % 2
                    for k in range(16):
                        P.op("pe", "matmul", psb[bk][0:2, :], sT[:, k, :], wa[s][:, k, :], start=(k == 0), stop=(k == 15),
                             reads=[sT.b, wa[s].b], **({"writes": [psB[bk]]} if k == 0 else {"accs": [psB[bk]]}))
                    o = g * 512
                    P.op("dve", "tensor_tensor", modsb[0:2, o:o + 512], psb[bk][0:2, :], adab[0:2, o:o + 512], ALU.add,
                         reads=[psB[bk], adab.b], **({"writes": [modsb.b]} if g == 0 else {"accs": [modsb.b]}))
                    n += 1
                P.dma("sp", ds_misc, modS[:, l * 6 * D:(l + 1) * 6 * D], modsb[:], reads=[modsb.b])
            P.flush()

    def modulate_tile(xt, hb, ss, rs, A, Bv):
        P.op("act", "activation", out=hb[:], in_=xt[:], func=AF.Square, accum_out=ss[:, 0:1],
             reads=[xt.b], writes=[hb.b, ss.b])
        rstd_from_ss(rs, ss, D)
        P.op("dve", "scalar_tensor_tensor", out=xt[:], in0=xt[:], scalar=rs[:, 0:1], in1=A[:], op0=ALU.mult, op1=ALU.mult,
             reads=[xt.b, rs.b, A.b], writes=[xt.b])
        P.op("dve", "tensor_tensor", hb[:], xt[:], Bv[:], ALU.add, reads=[xt.b, Bv.b], writes=[hb.b])

    def load_AB(A, Bv, tmp, dsem, row, l, v_shift, v_scale, norm_name):
        bcast_load(A, dsem, mod_row(row, l, v_scale))
        bcast_load(tmp, dsem, small_row(norm_name))
        bcast_load(Bv, dsem, mod_row(row, l, v_shift))
        P.op("dve", "scalar_tensor_tensor", out=A[:], in0=A[:], scalar=1.0, in1=tmp[:], op0=ALU.add, op1=ALU.mult,
             reads=[A.b, tmp.b], writes=[A.b])

    if doA:
        with ExitStack() as ph:
            hT = T(ph, "hT", [128, 16, NKV], BF16)
            with ExitStack() as p1:
                A_l = T(p1, "A_l", [128, D], F32)
                B_l = T(p1, "B_l", [128, D], F32)
                A_c = T(p1, "A_c", [128, D], F32)
                B_c = T(p1, "B_c", [128, D], F32)
                xt = [T(p1, f"xt{i}", [128, D], F32) for i in range(2)]
                hb = [T(p1, f"hb{i}", [128, D], BF16) for i in range(2)]
                ss = [T(p1, f"ss{i}", [128, 1], F32) for i in range(2)]
                rs = [T(p1, f"rs{i}", [128, 1], F32) for i in range(2)]
                ds_x = [P.dsem(f"ds_x{i}") for i in range(2)]
                load_AB(A_l, B_l, xt[0], ds_misc, 0, 0, 0, 1, "nmix0")
                load_AB(A_c, B_c, xt[1], ds_misc, 1, 0, 0, 1, "nmix0")
                for t in range(NKT):
                    s = t % 2
                    src = xl[t * 128:(t + 1) * 128, :] if t < 32 else cl[(t - 32) * 128:(t - 31) * 128, :]
                    A, Bv = (A_l, B_l) if t < 32 else (A_c, B_c)
                    P.dma("sp", ds_x[s], xt[s][:], src, writes=[xt[s].b])
                    modulate_tile(xt[s], hb[s], ss[s], rs[s], A, Bv)
                    for q4 in range(4):
                        bk = 4 * s + q4
                        for i in range(4):
                            k = 4 * q4 + i
                            P.op("pe", "matmul", psb[bk][:, i * 128:(i + 1) * 128], hb[s][:, k * 128:(k + 1) * 128], ident[:],
                                 start=True, stop=True, reads=[hb[s].b, ident.b],
                                 **({"writes": [psB[bk]]} if i == 0 else {"accs": [psB[bk]]}))
                        evac(hT[:, 4 * q4:4 * q4 + 4, t * 128:(t + 1) * 128],
                             psb[bk][:].rearrange("p (a b) -> p a b", a=4), reads=[psB[bk]], accs=[hT.b])
                P.flush()
            with ExitStack() as p2:
                wp = [T(p2, f"wp{i}", [128, 16, 512], BF16) for i in range(2)]
                ds_wp = [P.dsem(f"ds_wp{i}") for i in range(2)]
                gains = {}
                gtmp = T(p2, "gtmp", [128, 128], F32)
                for nm, key, sc in (("qa", "na_q", 128 ** -0.5), ("ka", "na_k", 1.0), ("qb", "df_q", 128 ** -0.5), ("kb", "df_k", 1.0)):
                    g = T(p2, "g_" + nm, [128, 128], F32)
                    o, _ = SM[key]
                    P.dma("sp", ds_misc, g[:], smalls[0:1, o:o + 128].partition_broadcast(128), writes=[g.b])
                    if sc != 1.0:
                        P.op("dve", "tensor_scalar", g[:], g[:], sc, None, ALU.mult, reads=[g.b], writes=[g.b])
                    gains[nm] = g
                cs = [T(p2, f"cs{i}", [128, 256], F32) for i in range(3)]
                ds_cs = [P.dsem(f"ds_cs{i}") for i in range(3)]
                ss4 = [T(p2, f"ss4_{i}", [128, 4], F32) for i in range(3)]
                rs4 = [T(p2, f"rs4_{i}", [128, 4], F32) for i in range(3)]
                junk = T(p2, "junk", [128, 128], BF16)
                xg = [T(p2, f"xg{i}", [128, 512], F32) for i in range(3)]
                t1 = [T(p2, f"t1_{i}", [128, 512], F32) for i in range(3)]
                t2 = [T(p2, f"t2_{i}", [128, 512], F32) for i in range(3)]
                xb = [T(p2, f"xb{i}", [128, 512], BF16) for i in range(3)]
                stg = [T(p2, f"stg{i}", [128, 512], BF16) for i in range(3)]
                ds_stg = [P.dsem(f"ds_stg{i}") for i in range(3)]
                wsrc = even_w_in.rearrange("(k p) n -> p k n", p=128)
                q_tiles = list(range(NQT)) + [32, 33]
                na_tiles = list(range(19)) + [32, 33]
                all_tiles = list(range(NKT))
                u = 0
                pending = []
                for g in range(12):
                    kind = ("qa", "ka", "va", "qb", "kb", "vb")[g // 2]
                    half = g % 2
                    ws = g % 2
                    P.dma("pool", ds_wp[ws], wp[ws][:], wsrc[:, :, g * 512:(g + 1) * 512], writes=[wp[ws].b])
                    tiles = {"qa": q_tiles, "qb": q_tiles, "ka": na_tiles, "va": na_tiles, "kb": all_tiles, "vb": all_tiles}[kind]
                    for ti, t in enumerate(tiles):
                        s = u % 3
                        bk = u % 3
                        for k in range(16):
                            P.op("pe", "matmul", psb[bk][:, :], hT[:, k, t * 128:(t + 1) * 128], wp[ws][:, k, :],
                                 start=(k == 0), stop=(k == 15), reads=[hT.b, wp[ws].b],
                                 **({"writes": [psB[bk]]} if k == 0 else {"accs": [psB[bk]]}))
                        while pending:
                            pending.pop(0)()
                        if kind in ("va", "vb"):
                            evac(stg[s][:], psb[bk][:, :], reads=[psB[bk]], writes=[stg[s].b])
                            if kind == "va":
                                dst = Va[4 * half:4 * half + 4, ti * 128:(ti + 1) * 128, :].rearrange("h t d -> t h d")
                                P.dma("sp", ds_stg[s], dst, stg[s][:].rearrange("t (h d) -> t h d", h=4), reads=[stg[s].b])
                            else:
                                dst = Vb[2 * half:2 * half + 2, t * 128:(t + 1) * 128, :].rearrange("h t d -> t h d")
                                P.dma("sp", ds_stg[s], dst, stg[s][:].rearrange("t (h d) -> t h d", h=2), reads=[stg[s].b])
                            u += 1
                            continue
                        for hh in range(4):
                            P.op("act", "activation", out=junk[:], in_=psb[bk][:, hh * 128:(hh + 1) * 128], func=AF.Square,
                                 accum_out=ss4[s][:, hh:hh + 1], reads=[psB[bk]],
                                 **({"writes": [junk.b, ss4[s].b]} if hh == 0 else {"accs": [junk.b, ss4[s].b]}))
                        rstd_from_ss(rs4[s], ss4[s], 128, 4)
                        rope = kind in ("qb", "kb") and t < 32
                        dstT = xg[s] if rope else xb[s]
                        for hh in range(4):
                            P.op("dve", "scalar_tensor_tensor", out=dstT[:, hh * 128:(hh + 1) * 128],
                                 in0=psb[bk][:, hh * 128:(hh + 1) * 128], scalar=rs4[s][:, hh:hh + 1], in1=gains[kind][:],
                                 op0=ALU.mult, op1=ALU.mult, reads=[psB[bk], rs4[s].b, gains[kind].b],
                                 **({"writes": [dstT.b]} if hh == 0 else {"accs": [dstT.b]}))
                        if rope:
                            P.dma("sp", ds_cs[s], cs[s][:], rope0[t * 128:(t + 1) * 128, :], writes=[cs[s].b])
                            x3 = xg[s][:].rearrange("p (h d) -> p h d", h=4)
                            P.op("dve", "tensor_tensor", t1[s][:].rearrange("p (h d) -> p h d", h=4), x3,
                                 cs[s][:, None, 0:128].broadcast_to([128, 4, 128]), ALU.mult,
                                 reads=[xg[s].b, cs[s].b], writes=[t1[s].b])
                            x4 = xg[s][:].rearrange("p (h b e d) -> p h b e d", h=4, b=2, e=2)
                            sn = cs[s][:, 128:256].rearrange("p (b e d) -> p b e d", b=2, e=2)
                            for e in range(2):
                                P.op("dve", "tensor_tensor",
                                     t2[s][:].rearrange("p (h b e d) -> p h b e d", h=4, b=2, e=2)[:, :, :, e, :],
                                     x4[:, :, :, 1 - e, :],
                                     sn[:, None, :, e, :].broadcast_to([128, 4, 2, 32]), ALU.mult,
                                     reads=[xg[s].b, cs[s].b], **({"writes": [t2[s].b]} if e == 0 else {"accs": [t2[s].b]}))
                            P.op("dve", "tensor_tensor", xb[s][:], t1[s][:], t2[s][:], ALU.add,
                                 reads=[t1[s].b, t2[s].b], writes=[xb[s].b])
                        def tail(s=s, u=u, kind=kind, half=half, t=t, ti=ti):
                            tb = 4 + (u % 3)
                            transposes(xb[s], 4, [tb])
                            evac(stg[s][:], psb[tb][:, :], reads=[psB[tb]], writes=[stg[s].b])
                            sv = stg[s][:].rearrange("d (h t) -> d h t", h=4)
                            if kind == "qa":
                                col = t * 128 if t < 32 else NQ + (t - 32) * 128
                                dst = QaT[4 * half:4 * half + 4, :, col:col + 128]
                            elif kind == "qb":
                                col = t * 128 if t < 32 else NQ + (t - 32) * 128
                                dst = QbT[4 * half:4 * half + 4, :, col:col + 128]
                            elif kind == "ka":
                                dst = KaT[4 * half:4 * half + 4, :, ti * 128:(ti + 1) * 128]
                            else:
                                dst = KbT[4 * half:4 * half + 4, :, t * 128:(t + 1) * 128]
                            P.dma("sp", ds_stg[s], dst.rearrange("h d t -> d h t"), sv, reads=[stg[s].b])
                        pending.append(tail)
                        u += 1
                while pending:
                    pending.pop(0)()
                P.flush()
        if _UPTO[0] == 2:
            top.close()
            return nc

    def attn_dense(parts, V, kchunks, qcol0, nq, dv, S_banks, pTs, O_banks, fin):
        n = len(kchunks)
        NS = len(S_banks)
        nsub = nq // 128

        def emitS(i):
            bk = S_banks[i % NS]
            kc = kchunks[i]
            for pi, (kT, Kp, qT) in enumerate(parts):
                P.op("pe", "matmul", psb[bk][:, 0:nq], kT[0:Kp, kc * 128:(kc + 1) * 128], qT[0:Kp, qcol0:qcol0 + nq],
                     start=(pi == 0), stop=(pi == len(parts) - 1), reads=[kT.b, qT.b],
                     **({"writes": [psB[bk]]} if pi == 0 else {"accs": [psB[bk]]}))

        def emitE(i):
            bk = S_banks[i % NS]
            pT = pTs[i % NS]
            P.op("act", "activation", out=pT[:, 0:nq], in_=psb[bk][:, 0:nq], func=AF.Exp, reads=[psB[bk]], writes=[pT.b])

        def emitPV(i):
            pT = pTs[i % NS]
            kc = kchunks[i]
            for sub in range(nsub):
                ob = O_banks[sub]
                P.op("pe", "matmul", psb[ob][:, 0:dv + 1], pT[:, sub * 128:(sub + 1) * 128], V[:, kc, 0:dv + 1],
                     start=(i == 0), stop=(i == n - 1), reads=[pT.b, V.b],
                     **({"writes": [psB[ob]]} if i == 0 else {"accs": [psB[ob]]}))

        for i in range(min(NS - 1, n)):
            emitS(i)
        for i in range(n):
            if i + NS - 1 < n:
                emitS(i + NS - 1)
            emitE(i)
            emitPV(i)
        for sub in range(nsub):
            fin(sub, O_banks[sub])


    def post_attn(ph, O_src, ntiles, nlat, w_out_ap, l, x_src_fn, xmid_dst_fn, hT_dst_fn, nmix_unused=None):
        wo = T(ph, "wo", [128, 16, D], BF16)
        G = T(ph, "G", [128, D], F32)
        Af = T(ph, "Af", [128, D], F32)
        Bf = T(ph, "Bf", [128, D], F32)
        xt = [T(ph, f"pxt{i}", [128, D], F32) for i in range(2)]
        oT = T(ph, "oT", [128, 16, 128], BF16)
        h2b = T(ph, "h2b", [128, D], BF16)
        h2T = T(ph, "h2T", [128, 16, 128], BF16)
        tq = [T(ph, f"tq{i}", [128, 512], F32) for i in range(2)]
        ss = [T(ph, f"pss{i}", [128, 1], F32) for i in range(2)]
        rs = [T(ph, f"prs{i}", [128, 1], F32) for i in range(2)]
        ds_w = P.dsem("x")
        ds_x = [P.dsem("x") for _ in range(2)]
        ds_xo = [P.dsem("x") for _ in range(2)]
        ds_h = P.dsem("x")
        wsrc = w_out_ap.rearrange("(k p) n -> p k n", p=128)
        for q in range(4):
            P.dma("pool", ds_w, wo[:, 4 * q:4 * q + 4, :], wsrc[:, 4 * q:4 * q + 4, :],
                  **({"writes": [wo.b]} if q == 0 else {"accs": [wo.b]}))
        nffn = "nffn%d" % l
        pending = []
        for j in range(ntiles):
            s = j % 2
            row = 0 if j < nlat else 1
            if j == 0 or j == nlat:
                bcast_load(G, ds_misc, mod_row(row, l, 2))
                load_AB(Af, Bf, xt[1 - s], ds_misc, row, l, 3, 4, nffn)
            P.dma("sp", ds_x[s], xt[s][:], x_src_fn(j), writes=[xt[s].b])
            for q4 in range(4):
                for i in range(4):
                    k = 4 * q4 + i
                    P.op("pe", "matmul", psb[q4][:, i * 128:(i + 1) * 128], O_src[:, j, k * 128:(k + 1) * 128], ident[:],
                         start=True, stop=True, reads=[O_src.b, ident.b],
                         **({"writes": [psB[q4]]} if i == 0 else {"accs": [psB[q4]]}))
                evac(oT[:, 4 * q4:4 * q4 + 4, :], psb[q4][:].rearrange("p (a b) -> p a b", a=4), reads=[psB[q4]],
                     **({"writes": [oT.b]} if q4 == 0 else {"accs": [oT.b]}))
            for dg in range(4):
                bk = 4 + dg
                for k in range(16):
                    P.op("pe", "matmul", psb[bk][:, :], oT[:, k, :], wo[:, k, dg * 512:(dg + 1) * 512], start=(k == 0), stop=(k == 15),
                         reads=[oT.b, wo.b], **({"writes": [psB[bk]]} if k == 0 else {"accs": [psB[bk]]}))
                z = dg % 2
                P.op("dve", "tensor_tensor", tq[z][:], psb[bk][:, :], G[:, dg * 512:(dg + 1) * 512], ALU.mult,
                     reads=[psB[bk], G.b], writes=[tq[z].b])
                P.op("dve", "tensor_tensor", xt[s][:, dg * 512:(dg + 1) * 512], tq[z][:], xt[s][:, dg * 512:(dg + 1) * 512], ALU.add,
                     reads=[tq[z].b, xt[s].b], writes=[xt[s].b])
            while pending:
                pending.pop(0)()
            P.dma("sp", ds_xo[s], xmid_dst_fn(j), xt[s][:], reads=[xt[s].b])
            modulate_tile(xt[s], h2b, ss[s], rs[s], Af, Bf)

            def tail(j=j):
                for q4 in range(4):
                    for i in range(4):
                        k = 4 * q4 + i
                        P.op("pe", "matmul", psb[q4][:, i * 128:(i + 1) * 128], h2b[:, k * 128:(k + 1) * 128], ident[:],
                             start=True, stop=True, reads=[h2b.b, ident.b],
                             **({"writes": [psB[q4]]} if i == 0 else {"accs": [psB[q4]]}))
                    evac(h2T[:, 4 * q4:4 * q4 + 4, :], psb[q4][:].rearrange("p (a b) -> p a b", a=4), reads=[psB[q4]],
                         **({"writes": [h2T.b]} if q4 == 0 else {"accs": [h2T.b]}))
                P.dma("sp", ds_h, hT_dst_fn(j), h2T[:], reads=[h2T.b])
            pending.append(tail)
        while pending:
            pending.pop(0)()

    def ffn(ph, l, supers):
        gT = T(ph, "gT", [128, FC, 512], BF16)
        hTs = [T(ph, f"hTs{i}", [128, 16, 2, 258], BF16) for i in range(2)]
        wi = [T(ph, f"wi{i}", [128, 16, 2, 128], BF16) for i in range(3)]
        wo = [T(ph, f"fwo{i}", [128, FC, 256], BF16) for i in range(2)]
        Gs = [T(ph, f"Gf{i}", [128, D], F32) for i in range(2)]
        cw = T(ph, "cw", [128, 86 * 3], F32)
        ta = [T(ph, f"ta{i}", [128, 256], F32) for i in range(2)]
        tb = [T(ph, f"tb{i}", [128, 256], F32) for i in range(2)]
        sa = [T(ph, f"sa{i}", [128, 256], F32) for i in range(2)]
        xr = [T(ph, f"xr{i}", [128, 256], F32) for i in range(2)]
        ot = [T(ph, f"ot{i}", [128, 256], F32) for i in range(2)]
        ds_h = [[P.dsem("x") for _ in range(2)] for _ in range(2)]
        ds_wi = [P.dsem("x") for _ in range(3)]
        ds_wo = [P.dsem("x") for _ in range(2)]
        ds_xr = [P.dsem("x") for _ in range(2)]
        ds_ot = [P.dsem("x") for _ in range(2)]
        li = l if fused else 0
        P.dma("sp", ds_misc, cw[:], cwt_in[li], writes=[cw.b])
        rows = sorted({u["row"] for sp_ in supers for u in sp_})
        for r in rows:
            bcast_load(Gs[r], ds_misc, mod_row(r, l, 5))
        w_in_v = ffn_w_in[li].rearrange("(k p) n -> p k n", p=128)
        w_out_v = ffn_w_out[li].rearrange("(c p) n -> p c n", p=128)
        nwi = 0
        nwo = 0
        nz = 0
        nd = 0
        for si, units in enumerate(supers):
            sl = si % 2
            for ui, u in enumerate(units):
                n = u["n"]
                P.dma("sp", ds_h[sl][ui], hTs[sl][:, :, ui, 0:n + 2],
                      u["src"].rearrange("(k p) t -> p k t", p=128)[:, :, u["c0"]:u["c0"] + n + 2],
                      **({"writes": [hTs[sl].b]} if ui == 0 else {"accs": [hTs[sl].b]}))
            for c in range(FC):
                ws = nwi % 3
                nwi += 1
                for half in range(2):
                    P.dma("pool", ds_wi[ws], wi[ws][:, :, half, :], w_in_v[:, :, half * DFF + c * 128:half * DFF + (c + 1) * 128],
                          **({"writes": [wi[ws].b]} if half == 0 else {"accs": [wi[ws].b]}))
                for ui, u in enumerate(units):
                    n = u["n"]
                    z = nz % 2
                    nz += 1
                    ba, bb = 2 * z, 2 * z + 1
                    for half, bk in ((0, ba), (1, bb)):
                        for k in range(16):
                            P.op("pe", "matmul", psb[bk][:, 0:n + 2], wi[ws][:, k, half, :], hTs[sl][:, k, ui, 0:n + 2],
                                 start=(k == 0), stop=(k == 15), reads=[wi[ws].b, hTs[sl].b],
                                 **({"writes": [psB[bk]]} if k == 0 else {"accs": [psB[bk]]}))
                    for (tt, bk, ci) in ((ta[z], ba, c), (tb[z], bb, FC + c)):
                        P.op("act", "activation", out=tt[:, 0:n], in_=psb[bk][:, 1:n + 1], func=AF.Copy,
                             scale=cw[:, 3 * ci + 1:3 * ci + 2], reads=[psB[bk], cw.b], writes=[tt.b])
                        P.op("dve", "scalar_tensor_tensor", out=tt[:, 0:n], in0=psb[bk][:, 0:n], scalar=cw[:, 3 * ci:3 * ci + 1],
                             in1=tt[:, 0:n], op0=ALU.mult, op1=ALU.add, reads=[psB[bk], cw.b, tt.b], writes=[tt.b])
                        P.op("dve", "scalar_tensor_tensor", out=tt[:, 0:n], in0=psb[bk][:, 2:n + 2], scalar=cw[:, 3 * ci + 2:3 * ci + 3],
                             in1=tt[:, 0:n], op0=ALU.mult, op1=ALU.add, reads=[psB[bk], cw.b, tt.b], writes=[tt.b])
                    P.op("act", "activation", out=sa[z][:, 0:n], in_=ta[z][:, 0:n], func=AF.Silu, reads=[ta[z].b], writes=[sa[z].b])
                    P.op("dve", "tensor_tensor", gT[:, c, ui * 256:ui * 256 + n], sa[z][:, 0:n], tb[z][:, 0:n], ALU.mult,
                         reads=[sa[z].b, tb[z].b], **({"writes": [gT.b]} if (c == 0 and ui == 0) else {"accs": [gT.b]}))
            for dg in range(8):
                ws = nwo % 2
                nwo += 1
                P.dma("pool", ds_wo[ws], wo[ws][:], w_out_v[:, :, dg * 256:(dg + 1) * 256], writes=[wo[ws].b])
                for ui, u in enumerate(units):
                    for sub in range(u["n"] // 128):
                        z = nd % 2
                        nd += 1
                        bk = 4 + z
                        tc0 = ui * 256 + sub * 128
                        for c in range(FC):
                            P.op("pe", "matmul", psb[bk][:, 0:256], gT[:, c, tc0:tc0 + 128], wo[ws][:, c, :],
                                 start=(c == 0), stop=(c == FC - 1), reads=[gT.b, wo[ws].b],
                                 **({"writes": [psB[bk]]} if c == 0 else {"accs": [psB[bk]]}))
                        r0 = sub * 128
                        P.dma("sp", ds_xr[z], xr[z][:], u["res"][r0:r0 + 128, dg * 256:(dg + 1) * 256], writes=[xr[z].b])
                        P.op("dve", "tensor_tensor", ot[z][:], psb[bk][:, 0:256], Gs[u["row"]][:, dg * 256:(dg + 1) * 256], ALU.mult,
                             reads=[psB[bk], Gs[u["row"]].b], writes=[ot[z].b])
                        P.op("dve", "tensor_tensor", ot[z][:], ot[z][:], xr[z][:], ALU.add,
                             reads=[ot[z].b, xr[z].b], writes=[ot[z].b])
                        P.dma("sp", ds_ot[z], u["dst"][r0:r0 + 128, dg * 256:(dg + 1) * 256], ot[z][:], reads=[ot[z].b])


    if doA:
        with ExitStack() as ph:
            O_all = T(ph, "O_all", [128, 19, D], BF16)
            with ExitStack() as p3:
                kT = [T(p3, f"kT{i}", [128, NA_KT * 128], BF16) for i in range(2)]
                vA = [T(p3, f"vA{i}", [128, NA_KT, 129], BF16) for i in range(2)]
                qT = [T(p3, f"qT{i}", [128, NQC], BF16) for i in range(2)]
                bT = [T(p3, f"bT{i}", [128, 3, 5, 128], F32) for i in range(2)]
                dsl = [[P.dsem("x") for _ in range(4)] for _ in range(2)]
                sb = [T(p3, f"sb{i}", [128, 640], F32) for i in range(2)]
                pT = [T(p3, f"pT{i}", [128, 896], BF16) for i in range(2)]
                rinv = [T(p3, f"rinv{i}", [128, 1], F32) for i in range(2)]
                for i in range(2):
                    P.op("dve", "memset", vA[i][:, :, 128:129], 1.0, writes=[vA[i].b])
                n = 0
                for h in range(8):
                    s = h % 2
                    P.dma("sp", dsl[s][0], kT[s][:], KaT[h], writes=[kT[s].b])
                    P.dma("sp", dsl[s][1], vA[s][:, :, 0:128], Va[h].rearrange("(kt p) d -> p kt d", p=128), writes=[vA[s].b])
                    P.dma("sp", dsl[s][2], qT[s][:], QaT[h], writes=[qT[s].b])
                    P.dma("sp", dsl[s][3], bT[s][:].rearrange("p a b c -> p (a b c)"), biasT[h], writes=[bT[s].b])
                    for j in range(19):
                        lat = j < NQT
                        if lat:
                            st = min(max(2 * j - 4, 0), 54)
                            kts = [st // 2 + i for i in range(5)] + [19, 20]
                            var = min(j, 2)
                            qc = j * 128
                        else:
                            kts = [19, 20]
                            qc = NQ + (j - NQT) * 128
                        z = n % 2
                        bA, bB, bO = 2 * z, 2 * z + 1, 4 + z
                        nk = len(kts)
                        for i, kt in enumerate(kts):
                            bk, col = (bA, i * 128) if i < 4 else (bB, (i - 4) * 128)
                            P.op("pe", "matmul", psb[bk][:, col:col + 128], kT[s][:, kt * 128:(kt + 1) * 128], qT[s][:, qc:qc + 128],
                                 start=True, stop=True, reads=[kT[s].b, qT[s].b],
                                 **({"writes": [psB[bk]]} if i in (0, 4) else {"accs": [psB[bk]]}))
                        if lat:
                            P.op("dve", "tensor_tensor", sb[z][:, 0:512], psb[bA][:, :],
                                 bT[s][:, var, 0:4, :].rearrange("p a b -> p (a b)"), ALU.add,
                                 reads=[psB[bA], bT[s].b], writes=[sb[z].b])
                            P.op("dve", "tensor_tensor", sb[z][:, 512:640], psb[bB][:, 0:128], bT[s][:, var, 4, :], ALU.add,
                                 reads=[psB[bB], bT[s].b], accs=[sb[z].b])
                            P.op("act", "activation", out=pT[z][:, 0:640], in_=sb[z][:, 0:640], func=AF.Exp,
                                 reads=[sb[z].b], writes=[pT[z].b])
                            P.op("act", "activation", out=pT[z][:, 640:896], in_=psb[bB][:, 128:384], func=AF.Exp,
                                 reads=[psB[bB]], accs=[pT[z].b])
                        else:
                            P.op("act", "activation", out=pT[z][:, 0:256], in_=psb[bA][:, 0:256], func=AF.Exp,
                                 reads=[psB[bA]], writes=[pT[z].b])
                        for i, kt in enumerate(kts):
                            P.op("pe", "matmul", psb[bO][:, 0:129], pT[z][:, i * 128:(i + 1) * 128], vA[s][:, kt, :],
                                 start=(i == 0), stop=(i == nk - 1), reads=[pT[z].b, vA[s].b],
                                 **({"writes": [psB[bO]]} if i == 0 else {"accs": [psB[bO]]}))
                        P.op("dve", "reciprocal", rinv[z][:], psb[bO][:, 128:129], reads=[psB[bO]], writes=[rinv[z].b])
                        P.op("dve", "tensor_scalar", O_all[:, j, h * 128:(h + 1) * 128], psb[bO][:, 0:128], rinv[z][:, 0:1], None, ALU.mult,
                             reads=[psB[bO], rinv[z].b], accs=[O_all.b])
                        n += 1
                P.flush()
            with ExitStack() as p3:
                kT = [T(p3, f"dkT{i}", [128, 2, NKV], BF16) for i in range(2)]
                vB = [T(p3, f"vB{i}", [128, NKT, 257], BF16) for i in range(2)]
                qT = [T(p3, f"dqT{i}", [128, 2, NQC], BF16) for i in range(2)]
                dsl = [[P.dsem("x") for _ in range(3)] for _ in range(2)]
                pTs = [T(p3, f"dpT{i}", [128, 512], BF16) for i in range(3)]
                o1 = [T(p3, f"o1_{i}", [128, 256], F32) for i in range(4)]
                od = [T(p3, f"od_{i}", [128, 256], F32) for i in range(2)]
                rinv = [T(p3, f"drinv{i}", [128, 1], F32) for i in range(2)]
                ssd = [T(p3, f"ssd{i}", [128, 1], F32) for i in range(2)]
                rsd = [T(p3, f"rsd{i}", [128, 1], F32) for i in range(2)]
                junkd = T(p3, "junkd", [128, 256], BF16)
                lamt = T(p3, "lamt", [128, 512], F32)
                ltmp = T(p3, "ltmp", [128, 128], F32)
                e12 = T(p3, "e12", [128, 2], F32)
                nlam = T(p3, "nlam", [128, 1], F32)
                subg = T(p3, "subg", [128, 256], F32)
                for i in range(2):
                    P.op("dve", "memset", vB[i][:, :, 256:257], 1.0, writes=[vB[i].b])
                bcast_load(lamt, ds_misc, small_row("df_lam"))
                bcast_load(subg, ds_misc, small_row("df_sub"))
                for i in range(2):
                    P.op("dve", "tensor_tensor", ltmp[:], lamt[:, 256 * i:256 * i + 128], lamt[:, 256 * i + 128:256 * i + 256], ALU.mult,
                         reads=[lamt.b], writes=[ltmp.b])
                    P.op("dve", "reduce_sum", e12[:, i:i + 1], ltmp[:], AX.X, reads=[ltmp.b],
                         **({"writes": [e12.b]} if i == 0 else {"accs": [e12.b]}))
                P.op("act", "activation", out=e12[:], in_=e12[:], func=AF.Exp, reads=[e12.b], writes=[e12.b])
                P.op("dve", "tensor_tensor", nlam[:], e12[:, 1:2], e12[:, 0:1], ALU.subtract, reads=[e12.b], writes=[nlam.b])
                P.op("dve", "tensor_scalar", nlam[:], nlam[:], -LAM_INIT0, None, ALU.add, reads=[nlam.b], writes=[nlam.b])
                P.op("dve", "tensor_scalar", subg[:], subg[:], 1.0 - LAM_INIT0, None, ALU.mult, reads=[subg.b], writes=[subg.b])
                fcount = [0]
                for h in range(4):
                    s = h % 2
                    P.dma("sp", dsl[s][0], kT[s][:], KbT[2 * h:2 * h + 2].rearrange("a d t -> d a t"), writes=[kT[s].b])
                    P.dma("sp", dsl[s][1], vB[s][:, :, 0:256], Vb[h].rearrange("(kt p) d -> p kt d", p=128), writes=[vB[s].b])
                    P.dma("sp", dsl[s][2], qT[s][:], QbT[2 * h:2 * h + 2].rearrange("a d t -> d a t"), writes=[qT[s].b])
                    blocks = [(0, 512, "lat"), (512, 512, "lat"), (1024, 512, "lat"), (1536, 512, "lat"), (2048, 128, "lat"),
                              (NQ, 256, "ctx")]
                    for (q0, nq, kind) in blocks:
                        kch = list(range(NKT)) if kind == "lat" else [32, 33]
                        for sidx in range(2):
                            kTs = T.__new__(T)
                            kTs.t = kT[s].t[:, sidx, :]
                            kTs.b = kT[s].b
                            qTs = T.__new__(T)
                            qTs.t = qT[s].t[:, sidx, :]
                            qTs.b = qT[s].b

                            def fin(sub, ob, sidx=sidx, q0=q0, h=h, kind=kind):
                                z = fcount[0] % 2
                                fcount[0] += 1
                                tile = (q0 // 128 + sub) if kind == "lat" else (NQT + sub)
                                P.op("dve", "reciprocal", rinv[z][:], psb[ob][:, 256:257], reads=[psB[ob]], writes=[rinv[z].b])
                                if sidx == 0:
                                    P.op("dve", "tensor_scalar", o1[sub][:], psb[ob][:, 0:256], rinv[z][:, 0:1], None, ALU.mult,
                                         reads=[psB[ob], rinv[z].b], writes=[o1[sub].b])
                                    return
                                P.op("dve", "tensor_tensor", rinv[z][:], rinv[z][:], nlam[:], ALU.mult,
                                     reads=[rinv[z].b, nlam.b], writes=[rinv[z].b])
                                P.op("dve", "scalar_tensor_tensor", out=od[z][:], in0=psb[ob][:, 0:256], scalar=rinv[z][:, 0:1],
                                     in1=o1[sub][:], op0=ALU.mult, op1=ALU.add,
                                     reads=[psB[ob], rinv[z].b, o1[sub].b], writes=[od[z].b])
                                P.op("act", "activation", out=junkd[:], in_=od[z][:], func=AF.Square, accum_out=ssd[z][:, 0:1],
                                     reads=[od[z].b], writes=[junkd.b, ssd[z].b])
                                rstd_from_ss(rsd[z], ssd[z], 256)
                                P.op("dve", "scalar_tensor_tensor", out=O_all[:, tile, 1024 + 256 * h:1024 + 256 * (h + 1)],
                                     in0=od[z][:], scalar=rsd[z][:, 0:1], in1=subg[:], op0=ALU.mult, op1=ALU.mult,
                                     reads=[od[z].b, rsd[z].b, subg.b], accs=[O_all.b])

                            attn_dense([(kTs, 128, qTs)], vB[s], kch, q0, nq, 256, [0, 1, 2], pTs, [3, 4, 5, 6], fin)
                P.flush()
            if _UPTO[0] == 3:
                top.close()
                return nc
            with ExitStack() as p4:
                zt = T(p4, "zt", [128, 16, 1], BF16)
                P.op("dve", "memset", zt[:], 0.0, writes=[zt.b])
                hl = hTf_lat.rearrange("(k p) t -> p k t", p=128)
                hc = hTf_ctx.rearrange("(k p) t -> p k t", p=128)
                for dstv, col in ((hl, 0), (hl, NQ + 1), (hc, 0), (hc, CTX + 1)):
                    P.dma("sp", ds_misc, dstv[:, :, col:col + 1], zt[:], reads=[zt.b], allow_slow_non_contiguous=True)
                post_attn(p4, O_all, 19, NQT, even_w_out, 0,
                          lambda j: xl[j * 128:(j + 1) * 128, :] if j < NQT else cl[(j - NQT) * 128:(j - NQT + 1) * 128, :],
                          lambda j: xmid[j * 128:(j + 1) * 128, :] if j < NQT else xcmid[(j - NQT) * 128:(j - NQT + 1) * 128, :],
                          lambda j: (hl[:, :, 1 + j * 128:1 + (j + 1) * 128] if j < NQT
                                     else hc[:, :, 1 + (j - NQT) * 128:1 + (j - NQT + 1) * 128]))
                P.flush()
        if _UPTO[0] == 4:
            top.close()
            return nc
        with ExitStack() as p5:
            def unit(src, c0, n, res, dst, row):
                return {"src": src, "c0": c0, "n": n, "res": res, "dst": dst, "row": row}
            lat_units = [unit(hTf_lat, 256 * u, 256, xmid[256 * u:256 * (u + 1), :], x1q[256 * u:256 * (u + 1), :], 0) for u in range(8)]
            halo_unit = unit(hTf_lat, NOWN, 128, xmid[NOWN:NQ, :], x1q[NOWN:NQ, :], 0)
            ctx_unit = unit(hTf_ctx, 0, 256, xcmid[:, :], xc1[:, :], 1)
            supers = [lat_units[2 * i:2 * i + 2] for i in range(4)] + [[halo_unit, ctx_unit]]
            ffn(p5, 0, supers)
            P.flush()
        if _UPTO[0] == 5:
            top.close()
            return nc
        with ExitStack() as p6:
            wd = T(p6, "wd", [128, 16, 1088], BF16)
            A1 = T(p6, "A1", [128, D], F32)
            B1 = T(p6, "B1", [128, D], F32)
            xt = [T(p6, f"dxt{i}", [128, D], F32) for i in range(2)]
            hbs = [T(p6, f"dhb{i}", [128, D], BF16) for i in range(2)]
            hT1s = [T(p6, f"hT1_{i}", [128, 16, 128], BF16) for i in range(2)]
            gq = T(p6, "gq", [128, 512], F32)
            gkv = T(p6, "gkv", [128, 512], F32)
            gkr = T(p6, "gkr", [128, 64], F32)
            cs1 = [T(p6, f"cs1_{i}", [128, 128], F32) for i in range(2)]
            ss = [T(p6, f"dss{i}", [128, 1], F32) for i in range(2)]
            rs = [T(p6, f"drs{i}", [128, 1], F32) for i in range(2)]
            ss3 = [T(p6, f"dss3{i}", [128, 3], F32) for i in range(2)]
            rs3 = [T(p6, f"drs3{i}", [128, 3], F32) for i in range(2)]
            junk6 = T(p6, "junk6", [128, 512], BF16)
            nb = [T(p6, f"nb{i}", [128, 512], BF16) for i in range(2)]
            krfs = [T(p6, f"krf{i}", [128, 64], F32) for i in range(2)]
            kt1s = [T(p6, f"kt1_{i}", [128, 64], F32) for i in range(2)]
            kt2s = [T(p6, f"kt2_{i}", [128, 64], F32) for i in range(2)]
            krbs = [T(p6, f"krb{i}", [128, 64], BF16) for i in range(2)]
            stg6 = [T(p6, f"stg6_{i}", [128, 4, 128], BF16) for i in range(2)]
            stgrs = [T(p6, f"stgr{i}", [64, 128], BF16) for i in range(2)]
            ds_w = P.dsem("x")
            ds_x = [P.dsem("x") for _ in range(2)]
            ds_c = [P.dsem("x") for _ in range(2)]
            ds_s = [P.dsem("x") for _ in range(2)]
            ds_rs = [P.dsem("x") for _ in range(2)]
            P.dma("pool", ds_w, wd[:], mla_w_down.rearrange("(k p) n -> p k n", p=128), writes=[wd.b])
            bcast_load(gq, ds_misc, small_row("m_qa"))
            bcast_load(gkv, ds_misc, small_row("m_kva"))
            bcast_load(gkr, ds_misc, small_row("m_kr"))
            nst = 0
            for j in range(19):
                s = j % 2
                lat = j < NQT
                hb, hT1, krf, kt1, kt2, krb, stgr, ds_r = hbs[s], hT1s[s], krfs[s], kt1s[s], kt2s[s], krbs[s], stgrs[s], ds_rs[s]
                if j == 0 or j == NQT:
                    load_AB(A1, B1, xt[1 - s], ds_misc, 0 if lat else 1, 1, 0, 1, "nmix1")
                src = x1q[j * 128:(j + 1) * 128, :] if lat else xc1[(j - NQT) * 128:(j - NQT + 1) * 128, :]
                P.dma("sp", ds_x[s], xt[s][:], src, writes=[xt[s].b])
                modulate_tile(xt[s], hb, ss[s], rs[s], A1, B1)
                for q4 in range(4):
                    for i in range(4):
                        k = 4 * q4 + i
                        P.op("pe", "matmul", psb[q4][:, i * 128:(i + 1) * 128], hb[:, k * 128:(k + 1) * 128], ident[:],
                             start=True, stop=True, reads=[hb.b, ident.b],
                             **({"writes": [psB[q4]]} if i == 0 else {"accs": [psB[q4]]}))
                    evac(hT1[:, 4 * q4:4 * q4 + 4, :], psb[q4][:].rearrange("p (a b) -> p a b", a=4), reads=[psB[q4]],
                         **({"writes": [hT1.b]} if q4 == 0 else {"accs": [hT1.b]}))
                for gi, (c0, w) in enumerate(((0, 512), (512, 512), (1024, 64))):
                    bk = 4 + gi
                    for k in range(16):
                        P.op("pe", "matmul", psb[bk][:, 0:w], hT1[:, k, :], wd[:, k, c0:c0 + w], start=(k == 0), stop=(k == 15),
                             reads=[hT1.b, wd.b], **({"writes": [psB[bk]]} if k == 0 else {"accs": [psB[bk]]}))
                for gi, w in ((0, 512), (1, 512), (2, 64)):
                    P.op("act", "activation", out=junk6[:, 0:w], in_=psb[4 + gi][:, 0:w], func=AF.Square,
                         accum_out=ss3[s][:, gi:gi + 1], reads=[psB[4 + gi]],
                         **({"writes": [junk6.b, ss3[s].b]} if gi == 0 else {"accs": [junk6.b, ss3[s].b]}))
                P.op("dve", "tensor_scalar", rs3[s][:, 0:2], ss3[s][:, 0:2], 1.0 / 512, EPS, ALU.mult, ALU.add,
                     reads=[ss3[s].b], writes=[rs3[s].b])
                P.op("dve", "tensor_scalar", rs3[s][:, 2:3], ss3[s][:, 2:3], 1.0 / 64, EPS, ALU.mult, ALU.add,
                     reads=[ss3[s].b], accs=[rs3[s].b])
                P.op("act", "activation", out=rs3[s][:], in_=rs3[s][:], func=AF.Sqrt, reads=[rs3[s].b], writes=[rs3[s].b])
                P.op("dve", "reciprocal", rs3[s][:], rs3[s][:], reads=[rs3[s].b], writes=[rs3[s].b])
                jobs = []
                if lat:
                    jobs.append((0, gq, qlatT.rearrange("(c p) t -> p c t", p=128)[:, :, j * 128:(j + 1) * 128]))
                if j < 16:
                    jobs.append((1, gkv, kvx_own[0:512, :].rearrange("(c p) t -> p c t", p=128)[:, :, j * 128:(j + 1) * 128]))
                if not lat:
                    jj = j - NQT
                    jobs.append((1, gkv, kvx_ctx[0:512, :].rearrange("(c p) t -> p c t", p=128)[:, :, jj * 128:(jj + 1) * 128]))
                for (gi, gain, dst) in jobs:
                    z = nst % 2
                    nst += 1
                    P.op("dve", "scalar_tensor_tensor", out=nb[z][:], in0=psb[4 + gi][:, :], scalar=rs3[s][:, gi:gi + 1], in1=gain[:],
                         op0=ALU.mult, op1=ALU.mult, reads=[psB[4 + gi], rs3[s].b, gain.b], writes=[nb[z].b])
                    tbk = 2 + z
                    transposes(nb[z], 4, [tbk])
                    evac(stg6[z][:].rearrange("p a b -> p (a b)"), psb[tbk][:, :], reads=[psB[tbk]], writes=[stg6[z].b])
                    P.dma("sp", ds_s[z], dst, stg6[z][:], reads=[stg6[z].b])
                if j < 16 or not lat:
                    P.op("dve", "scalar_tensor_tensor", out=krf[:], in0=psb[6][:, 0:64], scalar=rs3[s][:, 2:3], in1=gkr[:],
                         op0=ALU.mult, op1=ALU.mult, reads=[psB[6], rs3[s].b, gkr.b], writes=[krf.b])
                    if lat:
                        P.dma("sp", ds_c[s], cs1[s][:], rope1[j * 128:(j + 1) * 128, :], writes=[cs1[s].b])
                        P.op("dve", "tensor_tensor", kt1[:], krf[:], cs1[s][:, 0:64], ALU.mult, reads=[krf.b, cs1[s].b], writes=[kt1.b])
                        x4 = krf[:].rearrange("p (b e d) -> p b e d", b=2, e=2)
                        sn = cs1[s][:, 64:128].rearrange("p (b e d) -> p b e d", b=2, e=2)
                        o4 = kt2[:].rearrange("p (b e d) -> p b e d", b=2, e=2)
                        for e in range(2):
                            P.op("dve", "tensor_tensor", o4[:, :, e, :], x4[:, :, 1 - e, :], sn[:, :, e, :], ALU.mult,
                                 reads=[krf.b, cs1[s].b], **({"writes": [kt2.b]} if e == 0 else {"accs": [kt2.b]}))
                        P.op("dve", "tensor_tensor", krb[:], kt1[:], kt2[:], ALU.add, reads=[kt1.b, kt2.b], writes=[krb.b])
                    else:
                        P.op("dve", "tensor_copy", krb[:], krf[:], reads=[krf.b], writes=[krb.b])
                    P.op("pe", "matmul", psb[7][0:64, 0:128], krb[:, 0:64], ident[:], start=True, stop=True,
                         reads=[krb.b, ident.b], writes=[psB[7]])
                    evac(stgr[:], psb[7][0:64, 0:128], reads=[psB[7]], writes=[stgr.b])
                    if lat:
                        dstr = kvx_own[512:576, j * 128:(j + 1) * 128]
                    else:
                        dstr = kvx_ctx[512:576, (j - NQT) * 128:(j - NQT + 1) * 128]
                    P.dma("sp", ds_r, dstr, stgr[:], reads=[stgr.b])
            P.flush()
    if doB:
        if fused:
            kvx_pair = scr("kvx_pair", [5, 256, NOWN], BF16)
            cc = DSem(top.enter_context(nc.semaphore("cc_sem")), False, 1)
            for i in range(5):
                rows = 128 if i < 4 else 64
                P._rec("pool", "collective_compute", ("AllGather", ALU.bypass),
                       dict(replica_groups=[[0, 1], [2, 3], [4, 5], [6, 7]], ins=[kvx_own[128 * i:128 * i + rows, :]],
                            outs=[kvx_pair[i, 0:2 * rows, :]]), (), (), (), cc)
            P.flush()

            def kv_lat_src(kt):
                if kt < 32:
                    r, c = kt // 16, kt % 16
                    return kvx_pair[0:4, r * 128:(r + 1) * 128, c * 128:(c + 1) * 128].rearrange("c p t -> p c t")
                return kvx_ctx[0:512, :].rearrange("(c p) t -> p c t", p=128)[:, :, (kt - 32) * 128:(kt - 31) * 128]
            kr_srcs = [(0, NOWN, kvx_pair[4, 0:64, :]), (NOWN, 2 * NOWN, kvx_pair[4, 64:128, :]),
                       (2 * NOWN, NKV, kvx_ctx[512:576, :])]
        else:
            kv_lat_src = lambda kt: kvx_all[0:512, :].rearrange("(c p) t -> p c t", p=128)[:, :, kt * 128:(kt + 1) * 128]
            kr_srcs = [(0, NKV, kvx_all[512:576, :])]
        SCL = (128 + 64) ** -0.5
        with ExitStack() as p7:
            wkv = T(p7, "wkv", [128, 4, 4096], BF16)
            wq = T(p7, "wq", [128, 4, 3072], BF16)
            gkn = T(p7, "gkn", [128, 128], F32)
            gqn = T(p7, "gqn", [128, 128], F32)
            gqr = T(p7, "gqr", [128, 64], F32)
            lt = [T(p7, f"lt{i}", [128, 4, 128], BF16) for i in range(2)]
            ss2 = [T(p7, f"ss2_{i}", [128, 4], F32) for i in range(4)]
            rs2 = [T(p7, f"rs2_{i}", [128, 4], F32) for i in range(4)]
            junk7 = T(p7, "junk7", [128, 128], BF16)
            xb7 = [T(p7, f"xb7_{i}", [128, 256], BF16) for i in range(4)]
            st7 = [T(p7, f"st7_{i}", [128, 2, 128], BF16) for i in range(4)]
            vs7 = [T(p7, f"vs7_{i}", [128, 2, 128], BF16) for i in range(4)]
            qrf = T(p7, "qrf", [128, 128], F32)
            qt1 = T(p7, "qt1", [128, 128], F32)
            qt2 = T(p7, "qt2", [128, 128], F32)
            qrb = [T(p7, f"qrb{i}", [128, 128], BF16) for i in range(4)]
            sr7 = [T(p7, f"sr7_{i}", [128, 128], BF16) for i in range(4)]
            cs1 = [T(p7, f"cs7_{i}", [128, 128], F32) for i in range(2)]
            ds_w = [P.dsem("x") for _ in range(2)]
            ds_l = [P.dsem("x") for _ in range(2)]
            ds_k = [P.dsem("x") for _ in range(4)]
            ds_v = [P.dsem("x") for _ in range(4)]
            ds_q = [P.dsem("x") for _ in range(4)]
            ds_c = [P.dsem("x") for _ in range(2)]
            P.dma("pool", ds_w[0], wkv[:], mla_w_ukv.rearrange("(k p) n -> p k n", p=128), writes=[wkv.b])
            P.dma("pool", ds_w[1], wq[:], mla_w_uq.rearrange("(k p) n -> p k n", p=128), writes=[wq.b])
            bcast_load(gkn, ds_misc, small_row("m_kn"))
            bcast_load(gqn, ds_misc, small_row("m_qn"))
            bcast_load(gqr, ds_misc, small_row("m_qr"))
            P.op("dve", "tensor_scalar", gqn[:], gqn[:], SCL, None, ALU.mult, reads=[gqn.b], writes=[gqn.b])
            P.op("dve", "tensor_scalar", gqr[:], gqr[:], SCL, None, ALU.mult, reads=[gqr.b], writes=[gqr.b])
            u = 0
            pending = []
            for kt in range(NKT):
                ls = kt % 2
                P.dma("sp", ds_l[ls], lt[ls][:], kv_lat_src(kt), writes=[lt[ls].b])
                for g in range(8):
                    z = u % 4
                    bk = z
                    for k in range(4):
                        P.op("pe", "matmul", psb[bk][:, :], lt[ls][:, k, :], wkv[:, k, g * 512:(g + 1) * 512], start=(k == 0), stop=(k == 3),
                             reads=[lt[ls].b, wkv.b], **({"writes": [psB[bk]]} if k == 0 else {"accs": [psB[bk]]}))
                    while len(pending) > 1:
                        pending.pop(0)()
                    for hh in range(2):
                        P.op("act", "activation", out=junk7[:], in_=psb[bk][:, hh * 256:hh * 256 + 128], func=AF.Square,
                             accum_out=ss2[z][:, hh:hh + 1], reads=[psB[bk]],
                             **({"writes": [junk7.b, ss2[z].b]} if hh == 0 else {"accs": [junk7.b, ss2[z].b]}))
                    rstd_from_ss(rs2[z], ss2[z], 128, 2)
                    for hh in range(2):
                        P.op("dve", "scalar_tensor_tensor", out=xb7[z][:, hh * 128:(hh + 1) * 128],
                             in0=psb[bk][:, hh * 256:hh * 256 + 128], scalar=rs2[z][:, hh:hh + 1], in1=gkn[:],
                             op0=ALU.mult, op1=ALU.mult, reads=[psB[bk], rs2[z].b, gkn.b],
                             **({"writes": [xb7[z].b]} if hh == 0 else {"accs": [xb7[z].b]}))
                    evac(vs7[z][:], psb[bk][:].rearrange("p (h a d) -> p h a d", h=2, a=2)[:, :, 1, :], reads=[psB[bk]], writes=[vs7[z].b])
                    P.dma("sp", ds_v[z], V1[2 * g:2 * g + 2, kt * 128:(kt + 1) * 128, :].rearrange("h t d -> t h d"), vs7[z][:],
                          reads=[vs7[z].b])
                    def tail(z=z, g=g, kt=kt):
                        tbk = 4 + z
                        transposes(xb7[z], 2, [tbk])
                        evac(st7[z][:].rearrange("p a b -> p (a b)"), psb[tbk][:, 0:256], reads=[psB[tbk]], writes=[st7[z].b])
                        P.dma("sp", ds_k[z], knT[2 * g:2 * g + 2, :, kt * 128:(kt + 1) * 128].rearrange("h d t -> d h t"), st7[z][:],
                              reads=[st7[z].b])
                    pending.append(tail)
                    u += 1
            for j in range(NQT):
                ls = j % 2
                P.dma("sp", ds_l[ls], lt[ls][:], qlatT.rearrange("(c p) t -> p c t", p=128)[:, :, j * 128:(j + 1) * 128],
                      writes=[lt[ls].b])
                P.dma("sp", ds_c[ls], cs1[ls][:], rope1[j * 128:(j + 1) * 128, :], writes=[cs1[ls].b])
                for g in range(8):
                    z = u % 4
                    bk = z
                    for k in range(4):
                        P.op("pe", "matmul", psb[bk][:, 0:384], lt[ls][:, k, :], wq[:, k, g * 384:(g + 1) * 384], start=(k == 0), stop=(k == 3),
                             reads=[lt[ls].b, wq.b], **({"writes": [psB[bk]]} if k == 0 else {"accs": [psB[bk]]}))
                    while len(pending) > 1:
                        pending.pop(0)()
                    for hh in range(2):
                        P.op("act", "activation", out=junk7[:], in_=psb[bk][:, hh * 192:hh * 192 + 128], func=AF.Square,
                             accum_out=ss2[z][:, hh:hh + 1], reads=[psB[bk]],
                             **({"writes": [junk7.b, ss2[z].b]} if hh == 0 else {"accs": [junk7.b, ss2[z].b]}))
                        P.op("act", "activation", out=junk7[:, 0:64], in_=psb[bk][:, hh * 192 + 128:hh * 192 + 192], func=AF.Square,
                             accum_out=ss2[z][:, 2 + hh:3 + hh], reads=[psB[bk]], accs=[junk7.b, ss2[z].b])
                    P.op("dve", "tensor_scalar", rs2[z][:, 0:2], ss2[z][:, 0:2], 1.0 / 128, EPS, ALU.mult, ALU.add,
                         reads=[ss2[z].b], writes=[rs2[z].b])
                    P.op("dve", "tensor_scalar", rs2[z][:, 2:4], ss2[z][:, 2:4], 1.0 / 64, EPS, ALU.mult, ALU.add,
                         reads=[ss2[z].b], accs=[rs2[z].b])
                    P.op("act", "activation", out=rs2[z][:], in_=rs2[z][:], func=AF.Sqrt, reads=[rs2[z].b], writes=[rs2[z].b])
                    P.op("dve", "reciprocal", rs2[z][:], rs2[z][:], reads=[rs2[z].b], writes=[rs2[z].b])
                    for hh in range(2):
                        P.op("dve", "scalar_tensor_tensor", out=xb7[z][:, hh * 128:(hh + 1) * 128],
                             in0=psb[bk][:, hh * 192:hh * 192 + 128], scalar=rs2[z][:, hh:hh + 1], in1=gqn[:],
                             op0=ALU.mult, op1=ALU.mult, reads=[psB[bk], rs2[z].b, gqn.b],
                             **({"writes": [xb7[z].b]} if hh == 0 else {"accs": [xb7[z].b]}))
                        P.op("dve", "scalar_tensor_tensor", out=qrf[:, hh * 64:(hh + 1) * 64],
                             in0=psb[bk][:, hh * 192 + 128:hh * 192 + 192], scalar=rs2[z][:, 2 + hh:3 + hh], in1=gqr[:],
                             op0=ALU.mult, op1=ALU.mult, reads=[psB[bk], rs2[z].b, gqr.b],
                             **({"writes": [qrf.b]} if hh == 0 else {"accs": [qrf.b]}))
                    P.op("dve", "tensor_tensor", qt1[:].rearrange("p (h d) -> p h d", h=2), qrf[:].rearrange("p (h d) -> p h d", h=2),
                         cs1[ls][:, None, 0:64].broadcast_to([128, 2, 64]), ALU.mult, reads=[qrf.b, cs1[ls].b], writes=[qt1.b])
                    x5 = qrf[:].rearrange("p (h b e d) -> p h b e d", h=2, b=2, e=2)
                    o5 = qt2[:].rearrange("p (h b e d) -> p h b e d", h=2, b=2, e=2)
                    sn = cs1[ls][:, 64:128].rearrange("p (b e d) -> p b e d", b=2, e=2)
                    for e in range(2):
                        P.op("dve", "tensor_tensor", o5[:, :, :, e, :], x5[:, :, :, 1 - e, :],
                             sn[:, None, :, e, :].broadcast_to([128, 2, 2, 16]), ALU.mult,
                             reads=[qrf.b, cs1[ls].b], **({"writes": [qt2.b]} if e == 0 else {"accs": [qt2.b]}))
                    P.op("dve", "tensor_tensor", qrb[z][:], qt1[:], qt2[:], ALU.add, reads=[qt1.b, qt2.b], writes=[qrb[z].b])
                    def tail(z=z, g=g, j=j):
                        tbk = 4 + z
                        transposes(xb7[z], 2, [tbk])
                        evac(st7[z][:].rearrange("p a b -> p (a b)"), psb[tbk][:, 0:256], reads=[psB[tbk]], writes=[st7[z].b])
                        P.dma("sp", ds_k[z], qnT[2 * g:2 * g + 2, :, j * 128:(j + 1) * 128].rearrange("h d t -> d h t"), st7[z][:],
                              reads=[st7[z].b])
                        P.op("pe", "matmul", psb[tbk][:, 256:384], qrb[z][:], ident[:], start=True, stop=True,
                             reads=[qrb[z].b, ident.b], accs=[psB[tbk]])
                        evac(sr7[z][:], psb[tbk][:, 256:384], reads=[psB[tbk]], writes=[sr7[z].b])
                        P.dma("sp", ds_q[z], qrT[2 * g:2 * g + 2].rearrange("h d t -> (h d) t")[:, j * 128:(j + 1) * 128], sr7[z][:],
                              reads=[sr7[z].b])
                    pending.append(tail)
                    u += 1
            while pending:
                pending.pop(0)()
            P.flush()
        with ExitStack() as ph:
            O1 = T(ph, "O1", [128, NQT, D], BF16)
            with ExitStack() as p8:
                kn = [T(p8, f"kn{i}", [128, NKV], BF16) for i in range(2)]
                v1 = [T(p8, f"v1_{i}", [128, NKT, 129], BF16) for i in range(2)]
                qn = [T(p8, f"qn{i}", [128, NQ], BF16) for i in range(2)]
                qr = [T(p8, f"qr{i}", [64, NQ], BF16) for i in range(2)]
                kr = T(p8, "kr", [64, NKV], BF16)
                pTs = [T(p8, f"mpT{i}", [128, 512], BF16) for i in range(3)]
                rinv = [T(p8, f"mrinv{i}", [128, 1], F32) for i in range(2)]
                dsl = [[P.dsem("x") for _ in range(4)] for _ in range(2)]
                for i, (c0, c1, src) in enumerate(kr_srcs):
                    P.dma("sp", ds_misc, kr[:, c0:c1], src, **({"writes": [kr.b]} if i == 0 else {"accs": [kr.b]}))
                for i in range(2):
                    P.op("dve", "memset", v1[i][:, :, 128:129], 1.0, writes=[v1[i].b])
                fcount = [0]
                for h in range(16):
                    s = h % 2
                    P.dma("sp", dsl[s][0], kn[s][:], knT[h], writes=[kn[s].b])
                    P.dma("sp", dsl[s][1], v1[s][:, :, 0:128], V1[h].rearrange("(kt p) d -> p kt d", p=128), writes=[v1[s].b])
                    P.dma("sp", dsl[s][2], qn[s][:], qnT[h], writes=[qn[s].b])
                    P.dma("sp", dsl[s][3], qr[s][:], qrT[h], writes=[qr[s].b])
                    for (q0, nq) in ((0, 512), (512, 512), (1024, 512), (1536, 512), (2048, 128)):
                        def fin(sub, ob, q0=q0, h=h):
                            z = fcount[0] % 2
                            fcount[0] += 1
                            tile = q0 // 128 + sub
                            P.op("dve", "reciprocal", rinv[z][:], psb[ob][:, 128:129], reads=[psB[ob]], writes=[rinv[z].b])
                            P.op("dve", "tensor_scalar", O1[:, tile, h * 128:(h + 1) * 128], psb[ob][:, 0:128], rinv[z][:, 0:1], None,
                                 ALU.mult, reads=[psB[ob], rinv[z].b], accs=[O1.b])
                        attn_dense([(kn[s], 128, qn[s]), (kr, 64, qr[s])], v1[s], list(range(NKT)), q0, nq, 128,
                                   [0, 1, 2], pTs, [3, 4, 5, 6], fin)
                P.flush()
            with ExitStack() as p9:
                zt = T(p9, "zt9", [128, 16, 1], BF16)
                P.op("dve", "memset", zt[:], 0.0, writes=[zt.b])
                hl = hTf_lat.rearrange("(k p) t -> p k t", p=128)
                P.dma("sp", ds_misc, hl[:, :, 0:1], zt[:], reads=[zt.b], allow_slow_non_contiguous=True)
                post_attn(p9, O1, NQT, NQT, mla_w_out, 1,
                          lambda j: x1q[j * 128:(j + 1) * 128, :],
                          lambda j: xmid1[j * 128:(j + 1) * 128, :],
                          lambda j: hl[:, :, 1 + j * 128:1 + (j + 1) * 128])
                P.flush()
        with ExitStack() as p10:
            units = [{"src": hTf_lat, "c0": 256 * u, "n": 256, "res": xmid1[256 * u:256 * (u + 1), :],
                      "dst": out[256 * u:256 * (u + 1), :], "row": 0} for u in range(8)]
            ffn(p10, 1, [units[2 * i:2 * i + 2] for i in range(4)])
            P.flush()
    top.close()
    return nc


_UPTO = [99]


def _perm(s):
    L = np.arange(SEQ)
    return L if s == 0 else (SEQ - 1 - L)


def _rope_tables(tok, half):
    freqs = (np.float32(10000.0) ** (-np.arange(half, dtype=np.float32) / np.float32(half))).astype(np.float32)
    r = (tok // 64).astype(np.float32)[:, None] * freqs[None, :]
    c = (tok % 64).astype(np.float32)[:, None] * freqs[None, :]
    cr, sr, ccs, sc = np.cos(r), np.sin(r), np.cos(c), np.sin(c)
    cos = np.concatenate([cr, cr, ccs, ccs], 1)
    sin = np.concatenate([-sr, sr, -sc, sc], 1)
    return np.concatenate([cos, sin], 1).astype(np.float32)


def _bias_tables(rpb, s):
    perm = _perm(s)
    out = np.empty((8, 128, 3, 5, 128), np.float32)
    p = np.arange(128)
    for v in range(3):
        tq = perm[128 * v + p]
        r, c = tq // 64, tq % 64
        rs0 = np.clip(r - 4, 0, 56)
        cs0 = np.clip(c - 8, 0, 48)
        for i in range(5):
            tk = perm[128 * i + p]
            kr, kc = tk // 64, tk % 64
            okr = (kr[:, None] >= rs0[None, :]) & (kr[:, None] < rs0[None, :] + 8)
            okc = (kc[:, None] >= cs0[None, :]) & (kc[:, None] < cs0[None, :] + 16)
            dr = np.clip(kr[:, None] - r[None, :] + 7, 0, 14)
            dc = np.clip(kc[:, None] - c[None, :] + 15, 0, 30)
            ok = okr & okc
            for h in range(8):
                out[h, :, v, i, :] = np.where(ok, rpb[h][dr, dc], np.float32(NEG))
    return out.reshape(8, 128, 3 * 5 * 128)


def _smalls(inp):
    v = np.zeros((1, NSM), np.float32)

    def put(name, arr):
        o, n = SM[name]
        a = np.asarray(arr, np.float32).reshape(-1)
        v[0, o:o + n] = np.tile(a, n // a.size)
    put("na_q", inp["na_q_norm"][0]); put("na_k", inp["na_k_norm"][0])
    put("df_q", inp["diff_q_norm"][0]); put("df_k", inp["diff_k_norm"][0])
    put("df_lam", inp["diff_lambda"][0]); put("df_sub", inp["diff_subln"][0])
    put("m_qa", inp["mla_q_a_norm"][0]); put("m_kva", inp["mla_kv_a_norm"][0])
    put("m_qn", inp["mla_q_nope_norm"][0]); put("m_qr", inp["mla_q_rope_norm"][0])
    put("m_kn", inp["mla_k_nope_norm"][0]); put("m_kr", inp["mla_k_rope_norm"][0])
    put("nmix0", inp["norm_mix"][0]); put("nmix1", inp["norm_mix"][1])
    put("nffn0", inp["norm_ffn"][0]); put("nffn1", inp["norm_ffn"][1])
    return v


def prep_shared(inp):
    sh = {
        "ident": np.eye(128, dtype=np.float32),
        "smalls": _smalls(inp),
        "ffn_w_in": np.ascontiguousarray(inp["ffn_w_in"], np.float32),
        "ffn_w_out": np.ascontiguousarray(inp["ffn_w_out"], np.float32),
        "ada_w": np.ascontiguousarray(inp["ada_w"], np.float32),
        "ada_b": np.ascontiguousarray(inp["ada_b"], np.float32).reshape(1, -1),
        "even_w_in": np.ascontiguousarray(inp["even_w_in"][0], np.float32),
        "even_w_out": np.ascontiguousarray(inp["even_w_out"][0], np.float32),
        "mla_w_down": np.ascontiguousarray(inp["mla_w_down"][0], np.float32),
        "mla_w_uq": np.ascontiguousarray(inp["mla_w_uq"][0], np.float32),
        "mla_w_ukv": np.ascontiguousarray(inp["mla_w_ukv"][0], np.float32),
        "mla_w_out": np.ascontiguousarray(inp["mla_w_out"][0], np.float32),
    }
    per_s = []
    for s in range(2):
        perm = _perm(s)
        conv = np.asarray(inp["ffn_conv"], np.float32)
        if s == 1:
            conv = conv[:, ::-1, :]
        cwt = conv.reshape(2, 3, 86, 128).transpose(0, 3, 2, 1).reshape(2, 128, 86 * 3)
        per_s.append({
            "cwt": np.ascontiguousarray(cwt),
            "rope0": _rope_tables(perm, 32),
            "rope1": _rope_tables(perm[:NQ], 16),
            "biasT": _bias_tables(np.asarray(inp["na_rpb"][0], np.float32), s),
        })
    return sh, per_s


def prep_core(inp, sh, per_s, core):
    b, s = core // 2, core % 2
    perm = _perm(s)
    x = np.asarray(inp["x"], np.float32)
    ctx = np.asarray(inp["ctx"], np.float32)
    cc = np.empty((128, 16, 2), np.float32)
    cc[:, :, 0] = np.asarray(inp["c"], np.float32)[b].reshape(16, 128).T
    cc[:, :, 1] = np.asarray(inp["c_ctx"], np.float32).reshape(16, 128).T
    m = dict(sh)
    m.update(per_s[s])
    m["xl"] = np.ascontiguousarray(x[b][perm])
    m["cl"] = np.ascontiguousarray(ctx[b] if s == 0 else ctx[b][::-1])
    m["cc"] = cc.reshape(128, 32)
    return m


A_INPUTS = ("ident", "smalls", "ffn_w_in", "ffn_w_out", "cwt", "xl", "cl", "cc", "ada_w", "ada_b", "even_w_in", "even_w_out",
            "rope0", "rope1", "biasT", "mla_w_down")
B_INPUTS = ("ident", "smalls", "ffn_w_in", "ffn_w_out", "cwt", "mla_w_uq", "mla_w_ukv", "mla_w_out", "rope1")


def _stage_inputs(m, names, layer):
    d = {}
    for k in names:
        v = m[k]
        if k in ("ffn_w_in", "ffn_w_out", "cwt"):
            v = v[layer:layer + 1]
        d[k] = v
    return d


def run_stage_A(inputs, cores, sh=None, per_s=None):
    if sh is None:
        sh, per_s = prep_shared(inputs)
    nc = build_program("A")
    in_maps = [_stage_inputs(prep_core(inputs, sh, per_s, c), A_INPUTS, 0) for c in cores]
    res = run_bass_kernel_spmd(nc, in_maps, core_ids=list(range(len(cores))))
    return res.results


def run_stage_B(inputs, cores, resA, sh=None, per_s=None):
    if sh is None:
        sh, per_s = prep_shared(inputs)
    nc = build_program("B")
    in_maps = []
    for c in cores:
        m = dict(sh)
        m.update(per_s[c % 2])
        d = _stage_inputs(m, B_INPUTS, 1)
        ra, rp = resA[c], resA[c ^ 1]
        d["modS"] = ra["modS"]
        d["x1q"] = ra["x1q"]
        d["qlatT"] = ra["qlatT"]
        d["kvx_all"] = np.ascontiguousarray(np.concatenate([ra["kvx_own"], rp["kvx_own"], ra["kvx_ctx"]], axis=1))
        in_maps.append(d)
    res = run_bass_kernel_spmd(nc, in_maps, core_ids=list(range(len(cores))))
    return res.results


FUSED = True
AB_INPUTS = A_INPUTS + ("mla_w_uq", "mla_w_ukv", "mla_w_out")


def kernel_fused(inputs):
    sh, per_s = prep_shared(inputs)
    cores = list(range(8))
    nc = build_program("AB")
    in_maps = []
    for c in cores:
        m = prep_core(inputs, sh, per_s, c)
        in_maps.append({k: m[k] for k in AB_INPUTS})
    res = run_bass_kernel_spmd(nc, in_maps, core_ids=cores)
    out = np.empty((4, SEQ, D), np.float32)
    for c in cores:
        b, s = c // 2, c % 2
        out[b][_perm(s)[:NOWN]] = res.results[c]["out"]
    return out


def kernel(**inputs):
    inputs = {k: np.asarray(v) for k, v in inputs.items()}
    if FUSED:
        return kernel_fused(inputs)
    sh, per_s = prep_shared(inputs)
    cores = list(range(8))
    ra = run_stage_A(inputs, cores, sh, per_s)
    keep = ("modS", "x1q", "qlatT", "kvx_own", "kvx_ctx")
    resA = {c: {k: ra[c][k] for k in keep} for c in cores}
    del ra
    rb = run_stage_B(inputs, cores, resA, sh, per_s)
    out = np.empty((4, SEQ, D), np.float32)
    for c in cores:
        b, s = c // 2, c % 2
        out[b][_perm(s)[:NOWN]] = rb[c]["out"]
    return out
```

```python
import math
from contextlib import ExitStack

import numpy as np
import concourse.bass as bass
import concourse.mybir as mybir
from concourse.bass_utils import run_bass_kernel_spmd

F32 = mybir.dt.float32
BF16 = mybir.dt.bfloat16
AF = mybir.ActivationFunctionType
ALU = mybir.AluOpType
AX = mybir.AxisListType

D = 2048
KC = 16
SEQ = 4096
CTX = 256
NOWN = 2048
NHALO = 128
NQ = NOWN + NHALO
NQT = NQ // 128
NQC = NQ + CTX
NKV = SEQ + CTX
NKT = NKV // 128
DFF = 5504
FC = DFF // 128
EPS = 1e-6
NEG = -30000.0
NA_KT = 21
LAM_INIT0 = 0.8 - 0.6 * math.exp(-0.3 * 0)


class Buf:
    __slots__ = ("name", "writers", "readers")

    def __init__(self, name=""):
        self.name = name
        self.writers = []
        self.readers = []


class DSem:
    def __init__(self, h, shared=False, step=16):
        self.h = h
        self.count = 0
        self.step = step
        self.shared = shared
        self.cell = [0]
        self.closed = False


class Ins:
    __slots__ = ("eng", "meth", "args", "kw", "deps", "marked", "val", "dsem", "cell", "raw")


class Prog:
    CENG = ("pe", "act", "dve", "pool")

    def __init__(self, nc, stack):
        self.nc = nc
        self.stack = stack
        self.engs = {"pe": nc.tensor, "act": nc.scalar, "dve": nc.vector, "pool": nc.gpsimd, "sp": nc.sync}
        self.csem = {e: stack.enter_context(nc.semaphore("cs_" + e)) for e in self.CENG}
        self.ccount = {e: 0 for e in self.CENG}
        self.bar = stack.enter_context(nc.semaphore("bar"))
        self.barcount = 0
        self.lists = {e: [] for e in self.engs}
        self.known = {e: {} for e in self.engs}
        self.bufs = []
        self.dsems = []
        self.dnext = 0
        self.ninstr = 0

    def buf(self, name=""):
        b = Buf(name)
        self.bufs.append(b)
        return b

    def dsem(self, name, shared=False):
        if shared:
            return DSem(self.stack.enter_context(self.nc.semaphore(name)), True)
        if self.dnext == len(self.dsems):
            self.dsems.append(DSem(self.stack.enter_context(self.nc.semaphore(f"dp{self.dnext}")), False))
        d = self.dsems[self.dnext]
        self.dnext += 1
        return d

    def _rec(self, eng, meth, args, kw, reads, writes, accs, dsem):
        ins = Ins()
        ins.eng, ins.meth, ins.args, ins.kw = eng, meth, args, kw
        ins.marked = False
        ins.val = 0
        ins.dsem = dsem
        ins.cell = None
        ins.raw = []
        deps = []
        for b in reads:
            deps.extend(b.writers)
        for b in writes:
            deps.extend(b.writers)
            deps.extend(b.readers)
        for b in accs:
            deps.extend(b.readers)
        seen = set()
        ins.deps = []
        for p in deps:
            if id(p) in seen:
                continue
            seen.add(id(p))
            if p.eng == "pe" and eng == "pe" and p.dsem is None:
                continue
            p.marked = True
            ins.deps.append(p)
            if p.dsem is not None and p.dsem.shared:
                p.dsem.closed = True
        for b in reads:
            b.readers.append(ins)
        for b in writes:
            b.writers = [ins]
            b.readers = []
        for b in accs:
            b.writers.append(ins)
        if dsem is not None:
            if dsem.shared:
                assert eng == "sp"
                if dsem.closed:
                    ins.raw.append((dsem.h, dsem.count))
                    dsem.cell = [0]
                    dsem.closed = False
                ins.cell = dsem.cell
            dsem.count += dsem.step
            ins.val = dsem.count
            if dsem.shared:
                dsem.cell[0] = dsem.count
            ins.marked = True
        self.lists[eng].append(ins)
        self.ninstr += 1
        return ins

    def op(self, eng, meth, *args, reads=(), writes=(), accs=(), **kw):
        return self._rec(eng, meth, args, kw, reads, writes, accs, None)

    def dma(self, eng, dsem, out, in_, reads=(), writes=(), accs=(), **kw):
        return self._rec(eng, "dma_start", (), dict(out=out, in_=in_, **kw), reads, writes, accs, dsem)

    def _token(self, p):
        if p.dsem is not None:
            return p.dsem.h, (p.cell[0] if p.cell is not None else p.val)
        return self.csem[p.eng], p.val

    def flush(self):
        for e in self.CENG:
            for ins in reversed(self.lists[e]):
                if ins.dsem is None:
                    ins.marked = True
                    break
        for e in self.CENG:
            for ins in self.lists[e]:
                if ins.dsem is None and ins.marked:
                    self.ccount[e] += 1
                    ins.val = self.ccount[e]
        self.barcount += 1
        dma_final = {e: {} for e in self.engs}
        for e in self.engs:
            for ins in self.lists[e]:
                if ins.dsem is not None:
                    dma_final[e][id(ins.dsem)] = (ins.dsem.h, ins.val)
        ndma_eng = sum(1 for e in self.engs if dma_final[e])
        bar_target_add = ndma_eng
        self._bar_total = getattr(self, "_bar_total", 0) + bar_target_add
        bar_total = self._bar_total
        cfinal = dict(self.ccount)
        lists = self.lists

        def make_body(e):
            def body(eng):
                known = self.known[e]
                for ins in lists[e]:
                    waits = {}
                    for h, v in ins.raw:
                        if known.get(id(h), 0) < v:
                            waits[id(h)] = (h, v)
                    for p in ins.deps:
                        h, v = self._token(p)
                        if known.get(id(h), 0) >= v:
                            continue
                        if id(h) not in waits or waits[id(h)][1] < v:
                            waits[id(h)] = (h, v)
                    for h, v in waits.values():
                        eng.wait_ge(h, v)
                        known[id(h)] = v
                    bi = getattr(eng, ins.meth)(*ins.args, **ins.kw)
                    if ins.dsem is not None:
                        bi.then_inc(ins.dsem.h, ins.dsem.step)
                    elif ins.marked:
                        bi.then_inc(self.csem[e], 1)
                if dma_final[e]:
                    for h, v in dma_final[e].values():
                        if known.get(id(h), 0) < v:
                            eng.wait_ge(h, v)
                            known[id(h)] = v
                    eng.sem_inc(self.bar, 1)
                for ce in self.CENG:
                    h, v = self.csem[ce], cfinal[ce]
                    if v > 0 and known.get(id(h), 0) < v:
                        eng.wait_ge(h, v)
                        known[id(h)] = v
                if bar_total > 0 and known.get(id(self.bar), 0) < bar_total:
                    eng.wait_ge(self.bar, bar_total)
                    known[id(self.bar)] = bar_total
            return body

        with self.nc.Block() as block:
            block.tensor(make_body("pe"))
            block.scalar(make_body("act"))
            block.vector(make_body("dve"))
            block.gpsimd(make_body("pool"))
            block.sync(make_body("sp"))
        self.lists = {e: [] for e in self.engs}
        self.dnext = 0
        for b in self.bufs:
            b.writers = []
            b.readers = []


SM = {}
_off = 0
for _n, _sz in (("na_q", 512), ("na_k", 512), ("df_q", 512), ("df_k", 512), ("df_lam", 512), ("df_sub", 256),
                ("m_qa", 512), ("m_kva", 512), ("m_qn", 128), ("m_qr", 64), ("m_kn", 128), ("m_kr", 64),
                ("nmix0", 2048), ("nmix1", 2048), ("nffn0", 2048), ("nffn1", 2048)):
    SM[_n] = (_off, _sz)
    _off += _sz
NSM = _off


def build_program(stage, debug=False):
    nc = bass.Bass("TRN2", target_bir_lowering=False)
    doA = "A" in stage
    doB = "B" in stage
    fused = stage == "AB"

    def din(name, shape, dt=F32):
        return nc.dram_tensor(name, list(shape), dt, kind="ExternalInput").ap()

    def dout(name, shape, dt=F32):
        return nc.dram_tensor(name, list(shape), dt, kind="ExternalOutput").ap()

    def scr(name, shape, dt):
        return nc.dram_tensor(name, list(shape), dt, kind="ExternalOutput" if debug else "Internal").ap()

    def xfer(name, shape, dt):
        if fused:
            return scr(name, shape, dt)
        if stage == "A":
            return dout(name, shape, dt)
        return din(name, shape, dt)

    ident_in = din("ident", [128, 128])
    smalls = din("smalls", [1, NSM])
    NL = 2 if fused else 1
    ffn_w_in = din("ffn_w_in", [NL, D, 2 * DFF])
    ffn_w_out = din("ffn_w_out", [NL, DFF, D])
    cwt_in = din("cwt", [NL, 128, 86 * 3])
    modS = xfer("modS", [2, 2 * 6 * D], F32)
    x1q = xfer("x1q", [NQ, D], F32)
    qlatT = xfer("qlatT", [512, NQ], BF16)
    if doA:
        kvx_own = xfer("kvx_own", [576, NOWN], BF16)
        kvx_ctx = xfer("kvx_ctx", [576, CTX], BF16)
        xl = din("xl", [SEQ, D])
        cl = din("cl", [CTX, D])
        cc_in = din("cc", [128, 32])
        ada_w = din("ada_w", [2, D, 6 * D])
        ada_b = din("ada_b", [1, 2 * 6 * D])
        even_w_in = din("even_w_in", [D, 6144])
        even_w_out = din("even_w_out", [D, D])
        rope0 = din("rope0", [SEQ, 256])
        rope1 = din("rope1", [NQ, 128])
        biasT = din("biasT", [8, 128, 3 * 5 * 128])
        mla_w_down = din("mla_w_down", [D, 1088])
        QaT = scr("QaT", [8, 128, NQC], BF16)
        KaT = scr("KaT", [8, 128, NA_KT * 128], BF16)
        Va = scr("Va", [8, NA_KT * 128, 128], BF16)
        QbT = scr("QbT", [8, 128, NQC], BF16)
        KbT = scr("KbT", [8, 128, NKV], BF16)
        Vb = scr("Vb", [4, NKV, 256], BF16)
        xmid = scr("xmid", [NQ, D], F32)
        xcmid = scr("xcmid", [CTX, D], F32)
        xc1 = scr("xc1", [CTX, D], F32)
    if doB:
        mla_w_uq = din("mla_w_uq", [512, 3072])
        mla_w_ukv = din("mla_w_ukv", [512, 4096])
        mla_w_out = din("mla_w_out", [D, D])
        if not fused:
            kvx_all = din("kvx_all", [576, NKV], BF16)
            rope1 = din("rope1", [NQ, 128])
        knT = scr("knT", [16, 128, NKV], BF16)
        V1 = scr("V1", [16, NKV, 128], BF16)
        qnT = scr("qnT", [16, 128, NQ], BF16)
        qrT = scr("qrT", [16, 64, NQ], BF16)
        xmid1 = scr("xmid1", [NQ, D], F32)
        out = dout("out", [NOWN, D], F32)
    hTf_lat = scr("hTf_lat", [D, NQ + 2], BF16)
    hTf_ctx = scr("hTf_ctx", [D, CTX + 2], BF16)

    top = ExitStack()
    P = Prog(nc, top)
    psb = [top.enter_context(nc.psum_tensor(f"ps{i}", [128, 512], F32)) for i in range(8)]
    psB = [P.buf(f"ps{i}") for i in range(8)]

    _uid = [0]

    class T:
        def __init__(self, st, name, shape, dt):
            _uid[0] += 1
            self.t = st.enter_context(nc.sbuf_tensor(f"sb{_uid[0]}_{name}", list(shape), dt))
            self.b = P.buf(name)

        def __getitem__(self, k):
            return self.t[k]

    ident = T(top, "ident", [128, 128], BF16)
    identf = T(top, "identf", [128, 128], F32)
    ds_misc = P.dsem("ds_misc", shared=True)
    P.dma("sp", ds_misc, identf[:], ident_in[:, :], writes=[identf.b])
    P.op("dve", "tensor_copy", ident[:], identf[:], reads=[identf.b], writes=[ident.b])

    def bcast_load(dst, dsem, row_ap, eng="sp"):
        P.dma(eng, dsem, dst[:], row_ap.partition_broadcast(128), writes=[dst.b])

    def small_row(name):
        o, n = SM[name]
        return smalls[0:1, o:o + n]

    def mod_row(r, l, v):
        o = l * 6 * D + v * D
        return modS[r:r + 1, o:o + D]

    _rr = [0]

    def evac(out_ap, in_ap, reads, writes=(), accs=()):
        _rr[0] ^= 1
        if _rr[0]:
            P.op("act", "activation", out=out_ap, in_=in_ap, func=AF.Copy, reads=reads, writes=writes, accs=accs)
        else:
            P.op("dve", "tensor_copy", out_ap, in_ap, reads=reads, writes=writes, accs=accs)

    def rstd_from_ss(rs, ss, n, width=1):
        P.op("dve", "tensor_scalar", rs[:, 0:width], ss[:, 0:width], 1.0 / n, EPS, ALU.mult, ALU.add,
             reads=[ss.b], writes=[rs.b])
        P.op("act", "activation", out=rs[:, 0:width], in_=rs[:, 0:width], func=AF.Sqrt, reads=[rs.b], writes=[rs.b])
        P.op("dve", "reciprocal", rs[:, 0:width], rs[:, 0:width], reads=[rs.b], writes=[rs.b])

    def transposes(src, nblk, banks, width=128, kp=128):
        for i in range(nblk):
            bk = banks[i // 4]
            col = (i % 4) * 128
            P.op("pe", "matmul", psb[bk][0:width, col:col + 128], src[:, i * width:(i + 1) * width], ident[:],
                 start=True, stop=True, reads=[src.b, ident.b],
                 **({"writes": [psB[bk]]} if i % 4 == 0 else {"accs": [psB[bk]]}))

    if doA:
        with ExitStack() as ph:
            cc = T(ph, "cc", [128, 32], F32)
            sT = T(ph, "sT", [128, 16, 2], BF16)
            adab = T(ph, "adab", [2, 6 * D], F32)
            modsb = T(ph, "modsb", [2, 6 * D], F32)
            wa = [T(ph, f"wa{i}", [128, 16, 512], BF16) for i in range(3)]
            ds_wa = [P.dsem(f"ds_wa{i}") for i in range(3)]
            P.dma("sp", ds_misc, cc[:], cc_in[:, :], writes=[cc.b])
            P.op("act", "activation", out=sT[:].rearrange("p k m -> p (k m)"), in_=cc[:], func=AF.Silu,
                 reads=[cc.b], writes=[sT.b])
            n = 0
            for l in range(2):
                wsrc = ada_w[l].rearrange("(k p) n -> p k n", p=128)
                P.dma("sp", ds_misc, adab[:], ada_b[0:1, l * 6 * D:(l + 1) * 6 * D].partition_broadcast(2), writes=[adab.b])
                for g in range(24):
                    s = n % 3
                    P.dma("pool", ds_wa[s], wa[s][:], wsrc[:, :, g * 512:(g + 1) * 512], writes=[wa[s].b])
                    bk = n % 2
                    for k in range(16):
                        P.op("pe", "matmul", psb[bk][0:2, :], sT[:, k, :], wa[s][:, k, :], start=(k == 0), stop=(k == 15),
                             reads=[sT.b, wa[s].b], **({"writes": [psB[bk]]} if k == 0 else {"accs": [psB[bk]]}))
                    o = g * 512
                    P.op("dve", "tensor_tensor", modsb[0:2, o:o + 512], psb[bk][0:2, :], adab[0:2, o:o + 512], ALU.add,
                         reads=[psB[bk], adab.b], **({"writes": [modsb.b]} if g == 0 else {"accs": [modsb.b]}))
                    n += 1
                P.dma("sp", ds_misc, modS[:, l * 6 * D:(l + 1) * 6 * D], modsb[:], reads=[modsb.b])
            P.flush()

    def modulate_tile(xt, hb, ss, rs, A, Bv):
        P.op("act", "activation", out=hb[:], in_=xt[:], func=AF.Square, accum_out=ss[:, 0:1],
             reads=[xt.b], writes=[hb.b, ss.b])
        rstd_from_ss(rs, ss, D)
        P.op("dve", "scalar_tensor_tensor", out=xt[:], in0=xt[:], scalar=rs[:, 0:1], in1=A[:], op0=ALU.mult, op1=ALU.mult,
             reads=[xt.b, rs.b, A.b], writes=[xt.b])
        P.op("dve", "tensor_tensor", hb[:], xt[:], Bv[:], ALU.add, reads=[xt.b, Bv.b], writes=[hb.b])

    def load_AB(A, Bv, tmp, dsem, row, l, v_shift, v_scale, norm_name):
        bcast_load(A, dsem, mod_row(row, l, v_scale))
        bcast_load(tmp, dsem, small_row(norm_name))
        bcast_load(Bv, dsem, mod_row(row, l, v_shift))
        P.op("dve", "scalar_tensor_tensor", out=A[:], in0=A[:], scalar=1.0, in1=tmp[:], op0=ALU.add, op1=ALU.mult,
             reads=[A.b, tmp.b], writes=[A.b])

    if doA:
        with ExitStack() as ph:
            hT = T(ph, "hT", [128, 16, NKV], BF16)
            with ExitStack() as p1:
                A_l = T(p1, "A_l", [128, D], F32)
                B_l = T(p1, "B_l", [128, D], F32)
                A_c = T(p1, "A_c", [128, D], F32)
                B_c = T(p1, "B_c", [128, D], F32)
                xt = [T(p1, f"xt{i}", [128, D], F32) for i in range(2)]
                hb = [T(p1, f"hb{i}", [128, D], BF16) for i in range(2)]
                ss = [T(p1, f"ss{i}", [128, 1], F32) for i in range(2)]
                rs = [T(p1, f"rs{i}", [128, 1], F32) for i in range(2)]
                ds_x = [P.dsem(f"ds_x{i}") for i in range(2)]
                load_AB(A_l, B_l, xt[0], ds_misc, 0, 0, 0, 1, "nmix0")
                load_AB(A_c, B_c, xt[1], ds_misc, 1, 0, 0, 1, "nmix0")
                for t in range(NKT):
                    s = t % 2
                    src = xl[t * 128:(t + 1) * 128, :] if t < 32 else cl[(t - 32) * 128:(t - 31) * 128, :]
                    A, Bv = (A_l, B_l) if t < 32 else (A_c, B_c)
                    P.dma("sp", ds_x[s], xt[s][:], src, writes=[xt[s].b])
                    modulate_tile(xt[s], hb[s], ss[s], rs[s], A, Bv)
                    for q4 in range(4):
                        bk = 4 * s + q4
                        for i in range(4):
                            k = 4 * q4 + i
                            P.op("pe", "matmul", psb[bk][:, i * 128:(i + 1) * 128], hb[s][:, k * 128:(k + 1) * 128], ident[:],
                                 start=True, stop=True, reads=[hb[s].b, ident.b],
                                 **({"writes": [psB[bk]]} if i == 0 else {"accs": [psB[bk]]}))
                        evac(hT[:, 4 * q4:4 * q4 + 4, t * 128:(t + 1) * 128],
                             psb[bk][:].rearrange("p (a b) -> p a b", a=4), reads=[psB[bk]], accs=[hT.b])
                P.flush()
            with ExitStack() as p2:
                wp = [T(p2, f"wp{i}", [128, 16, 512], BF16) for i in range(2)]
                ds_wp = [P.dsem(f"ds_wp{i}") for i in range(2)]
                gains = {}
                gtmp = T(p2, "gtmp", [128, 128], F32)
                for nm, key, sc in (("qa", "na_q", 128 ** -0.5), ("ka", "na_k", 1.0), ("qb", "df_q", 128 ** -0.5), ("kb", "df_k", 1.0)):
                    g = T(p2, "g_" + nm, [128, 128], F32)
                    o, _ = SM[key]
                    P.dma("sp", ds_misc, g[:], smalls[0:1, o:o + 128].partition_broadcast(128), writes=[g.b])
                    if sc != 1.0:
                        P.op("dve", "tensor_scalar", g[:], g[:], sc, None, ALU.mult, reads=[g.b], writes=[g.b])
                    gains[nm] = g
                cs = [T(p2, f"cs{i}", [128, 256], F32) for i in range(3)]
                ds_cs = [P.dsem(f"ds_cs{i}") for i in range(3)]
                ss4 = [T(p2, f"ss4_{i}", [128, 4], F32) for i in range(3)]
                rs4 = [T(p2, f"rs4_{i}", [128, 4], F32) for i in range(3)]
                junk = T(p2, "junk", [128, 128], BF16)
                xg = [T(p2, f"xg{i}", [128, 512], F32) for i in range(3)]
                t1 = [T(p2, f"t1_{i}", [128, 512], F32) for i in range(3)]
                t2 = [T(p2, f"t2_{i}", [128, 512], F32) for i in range(3)]
                xb = [T(p2, f"xb{i}", [128, 512], BF16) for i in range(3)]
                stg = [T(p2, f"stg{i}", [128, 512], BF16) for i in range(3)]
                ds_stg = [P.dsem(f"ds_stg{i}") for i in range(3)]
                wsrc = even_w_in.rearrange("(k p) n -> p k n", p=128)
                q_tiles = list(range(NQT)) + [32, 33]
                na_tiles = list(range(19)) + [32, 33]
                all_tiles = list(range(NKT))
                u = 0
                pending = []
                for g in range(12):
                    kind = ("qa", "ka", "va", "qb", "kb", "vb")[g // 2]
                    half = g % 2
                    ws = g % 2
                    P.dma("pool", ds_wp[ws], wp[ws][:], wsrc[:, :, g * 512:(g + 1) * 512], writes=[wp[ws].b])
                    tiles = {"qa": q_tiles, "qb": q_tiles, "ka": na_tiles, "va": na_tiles, "kb": all_tiles, "vb": all_tiles}[kind]
                    for ti, t in enumerate(tiles):
                        s = u % 3
                        bk = u % 3
                        for k in range(16):
                            P.op("pe", "matmul", psb[bk][:, :], hT[:, k, t * 128:(t + 1) * 128], wp[ws][:, k, :],
                                 start=(k == 0), stop=(k == 15), reads=[hT.b, wp[ws].b],
                                 **({"writes": [psB[bk]]} if k == 0 else {"accs": [psB[bk]]}))
                        while pending:
                            pending.pop(0)()
                        if kind in ("va", "vb"):
                            evac(stg[s][:], psb[bk][:, :], reads=[psB[bk]], writes=[stg[s].b])
                            if kind == "va":
                                dst = Va[4 * half:4 * half + 4, ti * 128:(ti + 1) * 128, :].rearrange("h t d -> t h d")
                                P.dma("sp", ds_stg[s], dst, stg[s][:].rearrange("t (h d) -> t h d", h=4), reads=[stg[s].b])
                            else:
                                dst = Vb[2 * half:2 * half + 2, t * 128:(t + 1) * 128, :].rearrange("h t d -> t h d")
                                P.dma("sp", ds_stg[s], dst, stg[s][:].rearrange("t (h d) -> t h d", h=2), reads=[stg[s].b])
                            u += 1
                            continue
                        for hh in range(4):
                            P.op("act", "activation", out=junk[:], in_=psb[bk][:, hh * 128:(hh + 1) * 128], func=AF.Square,
                                 accum_out=ss4[s][:, hh:hh + 1], reads=[psB[bk]],
                                 **({"writes": [junk.b, ss4[s].b]} if hh == 0 else {"accs": [junk.b, ss4[s].b]}))
                        rstd_from_ss(rs4[s], ss4[s], 128, 4)
                        rope = kind in ("qb", "kb") and t < 32
                        dstT = xg[s] if rope else xb[s]
                        for hh in range(4):
                            P.op("dve", "scalar_tensor_tensor", out=dstT[:, hh * 128:(hh + 1) * 128],
                                 in0=psb[bk][:, hh * 128:(hh + 1) * 128], scalar=rs4[s][:, hh:hh + 1], in1=gains[kind][:],
                                 op0=ALU.mult, op1=ALU.mult, reads=[psB[bk], rs4[s].b, gains[kind].b],
                                 **({"writes": [dstT.b]} if hh == 0 else {"accs": [dstT.b]}))
                        if rope:
                            P.dma("sp", ds_cs[s], cs[s][:], rope0[t * 128:(t + 1) * 128, :], writes=[cs[s].b])
                            x3 = xg[s][:].rearrange("p (h d) -> p h d", h=4)
                            P.op("dve", "tensor_tensor", t1[s][:].rearrange("p (h d) -> p h d", h=4), x3,
                                 cs[s][:, None, 0:128].broadcast_to([128, 4, 128]), ALU.mult,
                                 reads=[xg[s].b, cs[s].b], writes=[t1[s].b])
                            x4 = xg[s][:].rearrange("p (h b e d) -> p h b e d", h=4, b=2, e=2)
                            sn = cs[s][:, 128:256].rearrange("p (b e d) -> p b e d", b=2, e=2)
                            for e in range(2):
                                P.op("dve", "tensor_tensor",
                                     t2[s][:].rearrange("p (h b e d) -> p h b e d", h=4, b=2, e=2)[:, :, :, e, :],
                                     x4[:, :, :, 1 - e, :],
                                     sn[:, None, :, e, :].broadcast_to([128, 4, 2, 32]), ALU.mult,
                                     reads=[xg[s].b, cs[s].b], **({"writes": [t2[s].b]} if e == 0 else {"accs": [t2[s].b]}))
                            P.op("dve", "tensor_tensor", xb[s][:], t1[s][:], t2[s][:], ALU.add,
                                 reads=[t1[s].b, t2[s].b], writes=[xb[s].b])
                        def tail(s=s, u=u, kind=kind, half=half, t=t, ti=ti):
                            tb = 4 + (u % 3)
                            transposes(xb[s], 4, [tb])
                            evac(stg[s][:], psb[tb][:, :], reads=[psB[tb]], writes=[stg[s].b])
                            sv = stg[s][:].rearrange("d (h t) -> d h t", h=4)
                            if kind == "qa":
                                col = t * 128 if t < 32 else NQ + (t - 32) * 128
                                dst = QaT[4 * half:4 * half + 4, :, col:col + 128]
                            elif kind == "qb":
                                col = t * 128 if t < 32 else NQ + (t - 32) * 128
                                dst = QbT[4 * half:4 * half + 4, :, col:col + 128]
                            elif kind == "ka":
                                dst = KaT[4 * half:4 * half + 4, :, ti * 128:(ti + 1) * 128]
                            else:
                                dst = KbT[4 * half:4 * half + 4, :, t * 128:(t + 1) * 128]
                            P.dma("sp", ds_stg[s], dst.rearrange("h d t -> d h t"), sv, reads=[stg[s].b])
                        pending.append(tail)
                        u += 1
                while pending:
                    pending.pop(0)()
                P.flush()
        if _UPTO[0] == 2:
            top.close()
            return nc

    def attn_dense(parts, V, kchunks, qcol0, nq, dv, S_banks, pTs, O_banks, fin):
        n = len(kchunks)
        NS = len(S_banks)
        nsub = nq // 128

        def emitS(i):
            bk = S_banks[i % NS]
            kc = kchunks[i]
            for pi, (kT, Kp, qT) in enumerate(parts):
                P.op("pe", "matmul", psb[bk][:, 0:nq], kT[0:Kp, kc * 128:(kc + 1) * 128], qT[0:Kp, qcol0:qcol0 + nq],
                     start=(pi == 0), stop=(pi == len(parts) - 1), reads=[kT.b, qT.b],
                     **({"writes": [psB[bk]]} if pi == 0 else {"accs": [psB[bk]]}))

        def emitE(i):
            bk = S_banks[i % NS]
            pT = pTs[i % NS]
            P.op("act", "activation", out=pT[:, 0:nq], in_=psb[bk][:, 0:nq], func=AF.Exp, reads=[psB[bk]], writes=[pT.b])

        def emitPV(i):
            pT = pTs[i % NS]
            kc = kchunks[i]
            for sub in range(nsub):
                ob = O_banks[sub]
                P.op("pe", "matmul", psb[ob][:, 0:dv + 1], pT[:, sub * 128:(sub + 1) * 128], V[:, kc, 0:dv + 1],
                     start=(i == 0), stop=(i == n - 1), reads=[pT.b, V.b],
                     **({"writes": [psB[ob]]} if i == 0 else {"accs": [psB[ob]]}))

        for i in range(min(NS - 1, n)):
            emitS(i)
        for i in range(n):
            if i + NS - 1 < n:
                emitS(i + NS - 1)
            emitE(i)
            emitPV(i)
        for sub in range(nsub):
            fin(sub, O_banks[sub])


    def post_attn(ph, O_src, ntiles, nlat, w_out_ap, l, x_src_fn, xmid_dst_fn, hT_dst_fn, nmix_unused=None):
        wo = T(ph, "wo", [128, 16, D], BF16)
        G = T(ph, "G", [128, D], F32)
        Af = T(ph, "Af", [128, D], F32)
        Bf = T(ph, "Bf", [128, D], F32)
        xt = [T(ph, f"pxt{i}", [128, D], F32) for i in range(2)]
        oTs = [T(ph, f"oT{i}", [128, 16, 128], BF16) for i in range(2)]
        h2b = T(ph, "h2b", [128, D], BF16)
        h2T = T(ph, "h2T", [128, 16, 128], BF16)
        tq = [T(ph, f"tq{i}", [128, 512], F32) for i in range(2)]
        ss = [T(ph, f"pss{i}", [128, 1], F32) for i in range(2)]
        rs = [T(ph, f"prs{i}", [128, 1], F32) for i in range(2)]
        ds_w = P.dsem("x")
        ds_x = [P.dsem("x") for _ in range(2)]
        ds_xo = [P.dsem("x") for _ in range(2)]
        ds_h = P.dsem("x")
        wsrc = w_out_ap.rearrange("(k p) n -> p k n", p=128)
        for q in range(4):
            P.dma("pool", ds_w, wo[:, 4 * q:4 * q + 4, :], wsrc[:, 4 * q:4 * q + 4, :],
                  **({"writes": [wo.b]} if q == 0 else {"accs": [wo.b]}))
        nffn = "nffn%d" % l

        def o_transposes(j):
            oT = oTs[j % 2]
            for q4 in range(4):
                for i in range(4):
                    k = 4 * q4 + i
                    P.op("pe", "matmul", psb[q4][:, i * 128:(i + 1) * 128], O_src[:, j, k * 128:(k + 1) * 128], ident[:],
                         start=True, stop=True, reads=[O_src.b, ident.b],
                         **({"writes": [psB[q4]]} if i == 0 else {"accs": [psB[q4]]}))
                P.op("act", "activation", out=oT[:, 4 * q4:4 * q4 + 4, :], in_=psb[q4][:].rearrange("p (a b) -> p a b", a=4),
                     func=AF.Copy, reads=[psB[q4]], **({"writes": [oT.b]} if q4 == 0 else {"accs": [oT.b]}))

        pending = []
        o_transposes(0)
        for j in range(ntiles):
            s = j % 2
            oT = oTs[j % 2]
            row = 0 if j < nlat else 1
            if j == 0 or j == nlat:
                bcast_load(G, ds_misc, mod_row(row, l, 2))
                load_AB(Af, Bf, xt[1 - s], ds_misc, row, l, 3, 4, nffn)
            P.dma("sp", ds_x[s], xt[s][:], x_src_fn(j), writes=[xt[s].b])
            for dg in range(4):
                bk = 4 + dg
                for k in range(16):
                    P.op("pe", "matmul", psb[bk][:, :], oT[:, k, :], wo[:, k, dg * 512:(dg + 1) * 512], start=(k == 0), stop=(k == 15),
                         reads=[oT.b, wo.b], **({"writes": [psB[bk]]} if k == 0 else {"accs": [psB[bk]]}))
                z = dg % 2
                P.op("dve", "tensor_tensor", tq[z][:], psb[bk][:, :], G[:, dg * 512:(dg + 1) * 512], ALU.mult,
                     reads=[psB[bk], G.b], writes=[tq[z].b])
                P.op("dve", "tensor_tensor", xt[s][:, dg * 512:(dg + 1) * 512], tq[z][:], xt[s][:, dg * 512:(dg + 1) * 512], ALU.add,
                     reads=[tq[z].b, xt[s].b], writes=[xt[s].b])
            if j + 1 < ntiles:
                o_transposes(j + 1)
            while pending:
                pending.pop(0)()
            P.dma("sp", ds_xo[s], xmid_dst_fn(j), xt[s][:], reads=[xt[s].b])
            modulate_tile(xt[s], h2b, ss[s], rs[s], Af, Bf)

            def tail(j=j):
                for q4 in range(4):
                    for i in range(4):
                        k = 4 * q4 + i
                        P.op("pe", "matmul", psb[q4][:, i * 128:(i + 1) * 128], h2b[:, k * 128:(k + 1) * 128], ident[:],
                             start=True, stop=True, reads=[h2b.b, ident.b],
                             **({"writes": [psB[q4]]} if i == 0 else {"accs": [psB[q4]]}))
                    P.op("act", "activation", out=h2T[:, 4 * q4:4 * q4 + 4, :], in_=psb[q4][:].rearrange("p (a b) -> p a b", a=4),
                         func=AF.Copy, reads=[psB[q4]], **({"writes": [h2T.b]} if q4 == 0 else {"accs": [h2T.b]}))
                P.dma("sp", ds_h, hT_dst_fn(j), h2T[:], reads=[h2T.b])
            pending.append(tail)
        while pending:
            pending.pop(0)()

    def ffn(ph, l, supers):
        gT = T(ph, "gT", [128, FC, 512], BF16)
        hTs = [T(ph, f"hTs{i}", [128, 16, 2, 258], BF16) for i in range(2)]
        wi = [T(ph, f"wi{i}", [128, 16, 2, 128], BF16) for i in range(3)]
        wo = [T(ph, f"fwo{i}", [128, FC, 256], BF16) for i in range(2)]
        Gs = [T(ph, f"Gf{i}", [128, D], F32) for i in range(2)]
        cw = T(ph, "cw", [128, 86 * 3], F32)
        ta = [T(ph, f"ta{i}", [128, 256], F32) for i in range(2)]
        tb = [T(ph, f"tb{i}", [128, 256], F32) for i in range(2)]
        sa = [T(ph, f"sa{i}", [128, 256], F32) for i in range(2)]
        xr = [T(ph, f"xr{i}", [128, 256], F32) for i in range(2)]
        ot = [T(ph, f"ot{i}", [128, 256], F32) for i in range(2)]
        ds_h = [[P.dsem("x") for _ in range(2)] for _ in range(2)]
        ds_wi = [P.dsem("x") for _ in range(3)]
        ds_wo = [P.dsem("x") for _ in range(2)]
        ds_xr = [P.dsem("x") for _ in range(2)]
        ds_ot = [P.dsem("x") for _ in range(2)]
        li = l if fused else 0
        P.dma("sp", ds_misc, cw[:], cwt_in[li], writes=[cw.b])
        rows = sorted({u["row"] for sp_ in supers for u in sp_})
        for r in rows:
            bcast_load(Gs[r], ds_misc, mod_row(r, l, 5))
        w_in_v = ffn_w_in[li].rearrange("(k p) n -> p k n", p=128)
        w_out_v = ffn_w_out[li].rearrange("(c p) n -> p c n", p=128)
        nwi = 0
        nwo = 0
        nz = 0
        nd = 0
        for si, units in enumerate(supers):
            sl = si % 2
            for ui, u in enumerate(units):
                n = u["n"]
                P.dma("sp", ds_h[sl][ui], hTs[sl][:, :, ui, 0:n + 2],
                      u["src"].rearrange("(k p) t -> p k t", p=128)[:, :, u["c0"]:u["c0"] + n + 2],
                      **({"writes": [hTs[sl].b]} if ui == 0 else {"accs": [hTs[sl].b]}))
            for c in range(FC):
                ws = nwi % 3
                nwi += 1
                for half in range(2):
                    P.dma("pool", ds_wi[ws], wi[ws][:, :, half, :], w_in_v[:, :, half * DFF + c * 128:half * DFF + (c + 1) * 128],
                          **({"writes": [wi[ws].b]} if half == 0 else {"accs": [wi[ws].b]}))
                for ui, u in enumerate(units):
                    n = u["n"]
                    z = nz % 2
                    nz += 1
                    ba, bb = 2 * z, 2 * z + 1
                    for half, bk in ((0, ba), (1, bb)):
                        for k in range(16):
                            P.op("pe", "matmul", psb[bk][:, 0:n + 2], wi[ws][:, k, half, :], hTs[sl][:, k, ui, 0:n + 2],
                                 start=(k == 0), stop=(k == 15), reads=[wi[ws].b, hTs[sl].b],
                                 **({"writes": [psB[bk]]} if k == 0 else {"accs": [psB[bk]]}))
                    for (tt, bk, ci) in ((ta[z], ba, c), (tb[z], bb, FC + c)):
                        P.op("act", "activation", out=tt[:, 0:n], in_=psb[bk][:, 1:n + 1], func=AF.Copy,
                             scale=cw[:, 3 * ci + 1:3 * ci + 2], reads=[psB[bk], cw.b], writes=[tt.b])
                        P.op("dve", "scalar_tensor_tensor", out=tt[:, 0:n], in0=psb[bk][:, 0:n], scalar=cw[:, 3 * ci:3 * ci + 1],
                             in1=tt[:, 0:n], op0=ALU.mult, op1=ALU.add, reads=[psB[bk], cw.b, tt.b], writes=[tt.b])
                        P.op("dve", "scalar_tensor_tensor", out=tt[:, 0:n], in0=psb[bk][:, 2:n + 2], scalar=cw[:, 3 * ci + 2:3 * ci + 3],
                             in1=tt[:, 0:n], op0=ALU.mult, op1=ALU.add, reads=[psB[bk], cw.b, tt.b], writes=[tt.b])
                    P.op("act", "activation", out=sa[z][:, 0:n], in_=ta[z][:, 0:n], func=AF.Silu, reads=[ta[z].b], writes=[sa[z].b])
                    P.op("dve", "tensor_tensor", gT[:, c, ui * 256:ui * 256 + n], sa[z][:, 0:n], tb[z][:, 0:n], ALU.mult,
                         reads=[sa[z].b, tb[z].b], **({"writes": [gT.b]} if (c == 0 and ui == 0) else {"accs": [gT.b]}))
            for dg in range(8):
                ws = nwo % 2
                nwo += 1
                P.dma("pool", ds_wo[ws], wo[ws][:], w_out_v[:, :, dg * 256:(dg + 1) * 256], writes=[wo[ws].b])
                for ui, u in enumerate(units):
                    for sub in range(u["n"] // 128):
                        z = nd % 2
                        nd += 1
                        bk = 4 + z
                        tc0 = ui * 256 + sub * 128
                        for c in range(FC):
                            P.op("pe", "matmul", psb[bk][:, 0:256], gT[:, c, tc0:tc0 + 128], wo[ws][:, c, :],
                                 start=(c == 0), stop=(c == FC - 1), reads=[gT.b, wo[ws].b],
                                 **({"writes": [psB[bk]]} if c == 0 else {"accs": [psB[bk]]}))
                        r0 = sub * 128
                        P.dma("sp", ds_xr[z], xr[z][:], u["res"][r0:r0 + 128, dg * 256:(dg + 1) * 256], writes=[xr[z].b])
                        P.op("dve", "tensor_tensor", ot[z][:], psb[bk][:, 0:256], Gs[u["row"]][:, dg * 256:(dg + 1) * 256], ALU.mult,
                             reads=[psB[bk], Gs[u["row"]].b], writes=[ot[z].b])
                        P.op("dve", "tensor_tensor", ot[z][:], ot[z][:], xr[z][:], ALU.add,
                             reads=[ot[z].b, xr[z].b], writes=[ot[z].b])
                        P.dma("sp", ds_ot[z], u["dst"][r0:r0 + 128, dg * 256:(dg + 1) * 256], ot[z][:], reads=[ot[z].b])


    if doA:
        with ExitStack() as ph:
            O_all = T(ph, "O_all", [128, 19, D], BF16)
            with ExitStack() as p3:
                kT = [T(p3, f"kT{i}", [128, NA_KT * 128], BF16) for i in range(2)]
                vA = [T(p3, f"vA{i}", [128, NA_KT, 129], BF16) for i in range(2)]
                qT = [T(p3, f"qT{i}", [128, NQC], BF16) for i in range(2)]
                bT = [T(p3, f"bT{i}", [128, 3, 5, 128], F32) for i in range(2)]
                dsl = [[P.dsem("x") for _ in range(4)] for _ in range(2)]
                sb = [T(p3, f"sb{i}", [128, 640], F32) for i in range(2)]
                pT = [T(p3, f"pT{i}", [128, 896], BF16) for i in range(2)]
                rinv = [T(p3, f"rinv{i}", [128, 1], F32) for i in range(2)]
                for i in range(2):
                    P.op("dve", "memset", vA[i][:, :, 128:129], 1.0, writes=[vA[i].b])
                n = 0
                for h in range(8):
                    s = h % 2
                    P.dma("sp", dsl[s][0], kT[s][:], KaT[h], writes=[kT[s].b])
                    P.dma("sp", dsl[s][1], vA[s][:, :, 0:128], Va[h].rearrange("(kt p) d -> p kt d", p=128), writes=[vA[s].b])
                    P.dma("sp", dsl[s][2], qT[s][:], QaT[h], writes=[qT[s].b])
                    P.dma("sp", dsl[s][3], bT[s][:].rearrange("p a b c -> p (a b c)"), biasT[h], writes=[bT[s].b])
                    for j in range(19):
                        lat = j < NQT
                        if lat:
                            st = min(max(2 * j - 4, 0), 54)
                            kts = [st // 2 + i for i in range(5)] + [19, 20]
                            var = min(j, 2)
                            qc = j * 128
                        else:
                            kts = [19, 20]
                            qc = NQ + (j - NQT) * 128
                        z = n % 2
                        bA, bB, bO = 2 * z, 2 * z + 1, 4 + z
                        nk = len(kts)
                        for i, kt in enumerate(kts):
                            bk, col = (bA, i * 128) if i < 4 else (bB, (i - 4) * 128)
                            P.op("pe", "matmul", psb[bk][:, col:col + 128], kT[s][:, kt * 128:(kt + 1) * 128], qT[s][:, qc:qc + 128],
                                 start=True, stop=True, reads=[kT[s].b, qT[s].b],
                                 **({"writes": [psB[bk]]} if i in (0, 4) else {"accs": [psB[bk]]}))
                        if lat:
                            P.op("dve", "tensor_tensor", sb[z][:, 0:512], psb[bA][:, :],
                                 bT[s][:, var, 0:4, :].rearrange("p a b -> p (a b)"), ALU.add,
                                 reads=[psB[bA], bT[s].b], writes=[sb[z].b])
                            P.op("dve", "tensor_tensor", sb[z][:, 512:640], psb[bB][:, 0:128], bT[s][:, var, 4, :], ALU.add,
                                 reads=[psB[bB], bT[s].b], accs=[sb[z].b])
                            P.op("act", "activation", out=pT[z][:, 0:640], in_=sb[z][:, 0:640], func=AF.Exp,
                                 reads=[sb[z].b], writes=[pT[z].b])
                            P.op("act", "activation", out=pT[z][:, 640:896], in_=psb[bB][:, 128:384], func=AF.Exp,
                                 reads=[psB[bB]], accs=[pT[z].b])
                        else:
                            P.op("act", "activation", out=pT[z][:, 0:256], in_=psb[bA][:, 0:256], func=AF.Exp,
                                 reads=[psB[bA]], writes=[pT[z].b])
                        for i, kt in enumerate(kts):
                            P.op("pe", "matmul", psb[bO][:, 0:129], pT[z][:, i * 128:(i + 1) * 128], vA[s][:, kt, :],
                                 start=(i == 0), stop=(i == nk - 1), reads=[pT[z].b, vA[s].b],
                                 **({"writes": [psB[bO]]} if i == 0 else {"accs": [psB[bO]]}))
                        P.op("dve", "reciprocal", rinv[z][:], psb[bO][:, 128:129], reads=[psB[bO]], writes=[rinv[z].b])
                        P.op("dve", "tensor_scalar", O_all[:, j, h * 128:(h + 1) * 128], psb[bO][:, 0:128], rinv[z][:, 0:1], None, ALU.mult,
                             reads=[psB[bO], rinv[z].b], accs=[O_all.b])
                        n += 1
                P.flush()
            with ExitStack() as p3:
                kT = [T(p3, f"dkT{i}", [128, 2, NKV], BF16) for i in range(2)]
                vB = [T(p3, f"vB{i}", [128, NKT, 257], BF16) for i in range(2)]
                qT = [T(p3, f"dqT{i}", [128, 2, NQC], BF16) for i in range(2)]
                dsl = [[P.dsem("x") for _ in range(3)] for _ in range(2)]
                pTs = [T(p3, f"dpT{i}", [128, 512], BF16) for i in range(3)]
                o1 = [T(p3, f"o1_{i}", [128, 256], F32) for i in range(4)]
                od = [T(p3, f"od_{i}", [128, 256], F32) for i in range(2)]
                rinv = [T(p3, f"drinv{i}", [128, 1], F32) for i in range(2)]
                ssd = [T(p3, f"ssd{i}", [128, 1], F32) for i in range(2)]
                rsd = [T(p3, f"rsd{i}", [128, 1], F32) for i in range(2)]
                junkd = T(p3, "junkd", [128, 256], BF16)
                lamt = T(p3, "lamt", [128, 512], F32)
                ltmp = T(p3, "ltmp", [128, 128], F32)
                e12 = T(p3, "e12", [128, 2], F32)
                nlam = T(p3, "nlam", [128, 1], F32)
                subg = T(p3, "subg", [128, 256], F32)
                for i in range(2):
                    P.op("dve", "memset", vB[i][:, :, 256:257], 1.0, writes=[vB[i].b])
                bcast_load(lamt, ds_misc, small_row("df_lam"))
                bcast_load(subg, ds_misc, small_row("df_sub"))
                for i in range(2):
                    P.op("dve", "tensor_tensor", ltmp[:], lamt[:, 256 * i:256 * i + 128], lamt[:, 256 * i + 128:256 * i + 256], ALU.mult,
                         reads=[lamt.b], writes=[ltmp.b])
                    P.op("dve", "reduce_sum", e12[:, i:i + 1], ltmp[:], AX.X, reads=[ltmp.b],
                         **({"writes": [e12.b]} if i == 0 else {"accs": [e12.b]}))
                P.op("act", "activation", out=e12[:], in_=e12[:], func=AF.Exp, reads=[e12.b], writes=[e12.b])
                P.op("dve", "tensor_tensor", nlam[:], e12[:, 1:2], e12[:, 0:1], ALU.subtract, reads=[e12.b], writes=[nlam.b])
                P.op("dve", "tensor_scalar", nlam[:], nlam[:], -LAM_INIT0, None, ALU.add, reads=[nlam.b], writes=[nlam.b])
                P.op("dve", "tensor_scalar", subg[:], subg[:], 1.0 - LAM_INIT0, None, ALU.mult, reads=[subg.b], writes=[subg.b])
                fcount = [0]
                for h in range(4):
                    s = h % 2
                    P.dma("sp", dsl[s][0], kT[s][:], KbT[2 * h:2 * h + 2].rearrange("a d t -> d a t"), writes=[kT[s].b])
                    P.dma("sp", dsl[s][1], vB[s][:, :, 0:256], Vb[h].rearrange("(kt p) d -> p kt d", p=128), writes=[vB[s].b])
                    P.dma("sp", dsl[s][2], qT[s][:], QbT[2 * h:2 * h + 2].rearrange("a d t -> d a t"), writes=[qT[s].b])
                    blocks = [(0, 512, "lat"), (512, 512, "lat"), (1024, 512, "lat"), (1536, 512, "lat"), (2048, 128, "lat"),
                              (NQ, 256, "ctx")]
                    for (q0, nq, kind) in blocks:
                        kch = list(range(NKT)) if kind == "lat" else [32, 33]
                        for sidx in range(2):
                            kTs = T.__new__(T)
                            kTs.t = kT[s].t[:, sidx, :]
                            kTs.b = kT[s].b
                            qTs = T.__new__(T)
                            qTs.t = qT[s].t[:, sidx, :]
                            qTs.b = qT[s].b

                            def fin(sub, ob, sidx=sidx, q0=q0, h=h, kind=kind):
                                z = fcount[0] % 2
                                fcount[0] += 1
                                tile = (q0 // 128 + sub) if kind == "lat" else (NQT + sub)
                                P.op("dve", "reciprocal", rinv[z][:], psb[ob][:, 256:257], reads=[psB[ob]], writes=[rinv[z].b])
                                if sidx == 0:
                                    P.op("dve", "tensor_scalar", o1[sub][:], psb[ob][:, 0:256], rinv[z][:, 0:1], None, ALU.mult,
                                         reads=[psB[ob], rinv[z].b], writes=[o1[sub].b])
                                    return
                                P.op("dve", "tensor_tensor", rinv[z][:], rinv[z][:], nlam[:], ALU.mult,
                                     reads=[rinv[z].b, nlam.b], writes=[rinv[z].b])
                                P.op("dve", "scalar_tensor_tensor", out=od[z][:], in0=psb[ob][:, 0:256], scalar=rinv[z][:, 0:1],
                                     in1=o1[sub][:], op0=ALU.mult, op1=ALU.add,
                                     reads=[psB[ob], rinv[z].b, o1[sub].b], writes=[od[z].b])
                                P.op("act", "activation", out=junkd[:], in_=od[z][:], func=AF.Square, accum_out=ssd[z][:, 0:1],
                                     reads=[od[z].b], writes=[junkd.b, ssd[z].b])
                                rstd_from_ss(rsd[z], ssd[z], 256)
                                P.op("dve", "scalar_tensor_tensor", out=O_all[:, tile, 1024 + 256 * h:1024 + 256 * (h + 1)],
                                     in0=od[z][:], scalar=rsd[z][:, 0:1], in1=subg[:], op0=ALU.mult, op1=ALU.mult,
                                     reads=[od[z].b, rsd[z].b, subg.b], accs=[O_all.b])

                            attn_dense([(kTs, 128, qTs)], vB[s], kch, q0, nq, 256, [0, 1, 2], pTs, [3, 4, 5, 6], fin)
                P.flush()
            if _UPTO[0] == 3:
                top.close()
                return nc
            with ExitStack() as p4:
                zt = T(p4, "zt", [128, 16, 1], BF16)
                P.op("dve", "memset", zt[:], 0.0, writes=[zt.b])
                hl = hTf_lat.rearrange("(k p) t -> p k t", p=128)
                hc = hTf_ctx.rearrange("(k p) t -> p k t", p=128)
                for dstv, col in ((hl, 0), (hl, NQ + 1), (hc, 0), (hc, CTX + 1)):
                    P.dma("sp", ds_misc, dstv[:, :, col:col + 1], zt[:], reads=[zt.b], allow_slow_non_contiguous=True)
                post_attn(p4, O_all, 19, NQT, even_w_out, 0,
                          lambda j: xl[j * 128:(j + 1) * 128, :] if j < NQT else cl[(j - NQT) * 128:(j - NQT + 1) * 128, :],
                          lambda j: xmid[j * 128:(j + 1) * 128, :] if j < NQT else xcmid[(j - NQT) * 128:(j - NQT + 1) * 128, :],
                          lambda j: (hl[:, :, 1 + j * 128:1 + (j + 1) * 128] if j < NQT
                                     else hc[:, :, 1 + (j - NQT) * 128:1 + (j - NQT + 1) * 128]))
                P.flush()
        if _UPTO[0] == 4:
            top.close()
            return nc
        with ExitStack() as p5:
            def unit(src, c0, n, res, dst, row):
                return {"src": src, "c0": c0, "n": n, "res": res, "dst": dst, "row": row}
            lat_units = [unit(hTf_lat, 256 * u, 256, xmid[256 * u:256 * (u + 1), :], x1q[256 * u:256 * (u + 1), :], 0) for u in range(8)]
            halo_unit = unit(hTf_lat, NOWN, 128, xmid[NOWN:NQ, :], x1q[NOWN:NQ, :], 0)
            ctx_unit = unit(hTf_ctx, 0, 256, xcmid[:, :], xc1[:, :], 1)
            supers = [lat_units[2 * i:2 * i + 2] for i in range(4)] + [[halo_unit, ctx_unit]]
            ffn(p5, 0, supers)
            P.flush()
        if _UPTO[0] == 5:
            top.close()
            return nc
        with ExitStack() as p6:
            wd = T(p6, "wd", [128, 16, 1088], BF16)
            A1 = T(p6, "A1", [128, D], F32)
            B1 = T(p6, "B1", [128, D], F32)
            xt = [T(p6, f"dxt{i}", [128, D], F32) for i in range(2)]
            hbs = [T(p6, f"dhb{i}", [128, D], BF16) for i in range(2)]
            hT1s = [T(p6, f"hT1_{i}", [128, 16, 128], BF16) for i in range(2)]
            gq = T(p6, "gq", [128, 512], F32)
            gkv = T(p6, "gkv", [128, 512], F32)
            gkr = T(p6, "gkr", [128, 64], F32)
            cs1 = [T(p6, f"cs1_{i}", [128, 128], F32) for i in range(2)]
            ss = [T(p6, f"dss{i}", [128, 1], F32) for i in range(2)]
            rs = [T(p6, f"drs{i}", [128, 1], F32) for i in range(2)]
            ss3 = [T(p6, f"dss3{i}", [128, 3], F32) for i in range(2)]
            rs3 = [T(p6, f"drs3{i}", [128, 3], F32) for i in range(2)]
            junk6 = T(p6, "junk6", [128, 512], BF16)
            nb = [T(p6, f"nb{i}", [128, 512], BF16) for i in range(2)]
            krfs = [T(p6, f"krf{i}", [128, 64], F32) for i in range(2)]
            kt1s = [T(p6, f"kt1_{i}", [128, 64], F32) for i in range(2)]
            kt2s = [T(p6, f"kt2_{i}", [128, 64], F32) for i in range(2)]
            krbs = [T(p6, f"krb{i}", [128, 64], BF16) for i in range(2)]
            stg6 = [T(p6, f"stg6_{i}", [128, 4, 128], BF16) for i in range(2)]
            stgrs = [T(p6, f"stgr{i}", [64, 128], BF16) for i in range(2)]
            ds_w = P.dsem("x")
            ds_x = [P.dsem("x") for _ in range(2)]
            ds_c = [P.dsem("x") for _ in range(2)]
            ds_s = [P.dsem("x") for _ in range(2)]
            ds_rs = [P.dsem("x") for _ in range(2)]
            P.dma("pool", ds_w, wd[:], mla_w_down.rearrange("(k p) n -> p k n", p=128), writes=[wd.b])
            bcast_load(gq, ds_misc, small_row("m_qa"))
            bcast_load(gkv, ds_misc, small_row("m_kva"))
            bcast_load(gkr, ds_misc, small_row("m_kr"))
            nst = 0
            for j in range(19):
                s = j % 2
                lat = j < NQT
                hb, hT1, krf, kt1, kt2, krb, stgr, ds_r = hbs[s], hT1s[s], krfs[s], kt1s[s], kt2s[s], krbs[s], stgrs[s], ds_rs[s]
                if j == 0 or j == NQT:
                    load_AB(A1, B1, xt[1 - s], ds_misc, 0 if lat else 1, 1, 0, 1, "nmix1")
                src = x1q[j * 128:(j + 1) * 128, :] if lat else xc1[(j - NQT) * 128:(j - NQT + 1) * 128, :]
                P.dma("sp", ds_x[s], xt[s][:], src, writes=[xt[s].b])
                modulate_tile(xt[s], hb, ss[s], rs[s], A1, B1)
                for q4 in range(4):
                    for i in range(4):
                        k = 4 * q4 + i
                        P.op("pe", "matmul", psb[q4][:, i * 128:(i + 1) * 128], hb[:, k * 128:(k + 1) * 128], ident[:],
                             start=True, stop=True, reads=[hb.b, ident.b],
                             **({"writes": [psB[q4]]} if i == 0 else {"accs": [psB[q4]]}))
                    evac(hT1[:, 4 * q4:4 * q4 + 4, :], psb[q4][:].rearrange("p (a b) -> p a b", a=4), reads=[psB[q4]],
                         **({"writes": [hT1.b]} if q4 == 0 else {"accs": [hT1.b]}))
                for gi, (c0, w) in enumerate(((0, 512), (512, 512), (1024, 64))):
                    bk = 4 + gi
                    for k in range(16):
                        P.op("pe", "matmul", psb[bk][:, 0:w], hT1[:, k, :], wd[:, k, c0:c0 + w], start=(k == 0), stop=(k == 15),
                             reads=[hT1.b, wd.b], **({"writes": [psB[bk]]} if k == 0 else {"accs": [psB[bk]]}))
                for gi, w in ((0, 512), (1, 512), (2, 64)):
                    P.op("act", "activation", out=junk6[:, 0:w], in_=psb[4 + gi][:, 0:w], func=AF.Square,
                         accum_out=ss3[s][:, gi:gi + 1], reads=[psB[4 + gi]],
                         **({"writes": [junk6.b, ss3[s].b]} if gi == 0 else {"accs": [junk6.b, ss3[s].b]}))
                P.op("dve", "tensor_scalar", rs3[s][:, 0:2], ss3[s][:, 0:2], 1.0 / 512, EPS, ALU.mult, ALU.add,
                     reads=[ss3[s].b], writes=[rs3[s].b])
                P.op("dve", "tensor_scalar", rs3[s][:, 2:3], ss3[s][:, 2:3], 1.0 / 64, EPS, ALU.mult, ALU.add,
                     reads=[ss3[s].b], accs=[rs3[s].b])
                P.op("act", "activation", out=rs3[s][:], in_=rs3[s][:], func=AF.Sqrt, reads=[rs3[s].b], writes=[rs3[s].b])
                P.op("dve", "reciprocal", rs3[s][:], rs3[s][:], reads=[rs3[s].b], writes=[rs3[s].b])
                jobs = []
                if lat:
                    jobs.append((0, gq, qlatT.rearrange("(c p) t -> p c t", p=128)[:, :, j * 128:(j + 1) * 128]))
                if j < 16:
                    jobs.append((1, gkv, kvx_own[0:512, :].rearrange("(c p) t -> p c t", p=128)[:, :, j * 128:(j + 1) * 128]))
                if not lat:
                    jj = j - NQT
                    jobs.append((1, gkv, kvx_ctx[0:512, :].rearrange("(c p) t -> p c t", p=128)[:, :, jj * 128:(jj + 1) * 128]))
                for (gi, gain, dst) in jobs:
                    z = nst % 2
                    nst += 1
                    P.op("dve", "scalar_tensor_tensor", out=nb[z][:], in0=psb[4 + gi][:, :], scalar=rs3[s][:, gi:gi + 1], in1=gain[:],
                         op0=ALU.mult, op1=ALU.mult, reads=[psB[4 + gi], rs3[s].b, gain.b], writes=[nb[z].b])
                    tbk = 2 + z
                    transposes(nb[z], 4, [tbk])
                    evac(stg6[z][:].rearrange("p a b -> p (a b)"), psb[tbk][:, :], reads=[psB[tbk]], writes=[stg6[z].b])
                    P.dma("sp", ds_s[z], dst, stg6[z][:], reads=[stg6[z].b])
                if j < 16 or not lat:
                    P.op("dve", "scalar_tensor_tensor", out=krf[:], in0=psb[6][:, 0:64], scalar=rs3[s][:, 2:3], in1=gkr[:],
                         op0=ALU.mult, op1=ALU.mult, reads=[psB[6], rs3[s].b, gkr.b], writes=[krf.b])
                    if lat:
                        P.dma("sp", ds_c[s], cs1[s][:], rope1[j * 128:(j + 1) * 128, :], writes=[cs1[s].b])
                        P.op("dve", "tensor_tensor", kt1[:], krf[:], cs1[s][:, 0:64], ALU.mult, reads=[krf.b, cs1[s].b], writes=[kt1.b])
                        x4 = krf[:].rearrange("p (b e d) -> p b e d", b=2, e=2)
                        sn = cs1[s][:, 64:128].rearrange("p (b e d) -> p b e d", b=2, e=2)
                        o4 = kt2[:].rearrange("p (b e d) -> p b e d", b=2, e=2)
                        for e in range(2):
                            P.op("dve", "tensor_tensor", o4[:, :, e, :], x4[:, :, 1 - e, :], sn[:, :, e, :], ALU.mult,
                                 reads=[krf.b, cs1[s].b], **({"writes": [kt2.b]} if e == 0 else {"accs": [kt2.b]}))
                        P.op("dve", "tensor_tensor", krb[:], kt1[:], kt2[:], ALU.add, reads=[kt1.b, kt2.b], writes=[krb.b])
                    else:
                        P.op("dve", "tensor_copy", krb[:], krf[:], reads=[krf.b], writes=[krb.b])
                    P.op("pe", "matmul", psb[7][0:64, 0:128], krb[:, 0:64], ident[:], start=True, stop=True,
                         reads=[krb.b, ident.b], writes=[psB[7]])
                    evac(stgr[:], psb[7][0:64, 0:128], reads=[psB[7]], writes=[stgr.b])
                    if lat:
                        dstr = kvx_own[512:576, j * 128:(j + 1) * 128]
                    else:
                        dstr = kvx_ctx[512:576, (j - NQT) * 128:(j - NQT + 1) * 128]
                    P.dma("sp", ds_r, dstr, stgr[:], reads=[stgr.b])
            P.flush()
    if doB:
        if fused:
            kvx_pair = scr("kvx_pair", [5, 256, NOWN], BF16)
            cc = DSem(top.enter_context(nc.semaphore("cc_sem")), False, 1)
            for i in range(5):
                rows = 128 if i < 4 else 64
                P._rec("pool", "collective_compute", ("AllGather", ALU.bypass),
                       dict(replica_groups=[[0, 1], [2, 3], [4, 5], [6, 7]], ins=[kvx_own[128 * i:128 * i + rows, :]],
                            outs=[kvx_pair[i, 0:2 * rows, :]]), (), (), (), cc)
            P.flush()

            def kv_lat_src(kt):
                if kt < 32:
                    r, c = kt // 16, kt % 16
                    return kvx_pair[0:4, r * 128:(r + 1) * 128, c * 128:(c + 1) * 128].rearrange("c p t -> p c t")
                return kvx_ctx[0:512, :].rearrange("(c p) t -> p c t", p=128)[:, :, (kt - 32) * 128:(kt - 31) * 128]
            kr_srcs = [(0, NOWN, kvx_pair[4, 0:64, :]), (NOWN, 2 * NOWN, kvx_pair[4, 64:128, :]),
                       (2 * NOWN, NKV, kvx_ctx[512:576, :])]
        else:
            kv_lat_src = lambda kt: kvx_all[0:512, :].rearrange("(c p) t -> p c t", p=128)[:, :, kt * 128:(kt + 1) * 128]
            kr_srcs = [(0, NKV, kvx_all[512:576, :])]
        SCL = (128 + 64) ** -0.5
        with ExitStack() as p7:
            wkv = T(p7, "wkv", [128, 4, 4096], BF16)
            wq = T(p7, "wq", [128, 4, 3072], BF16)
            gkn = T(p7, "gkn", [128, 128], F32)
            gqn = T(p7, "gqn", [128, 128], F32)
            gqr = T(p7, "gqr", [128, 64], F32)
            lt = [T(p7, f"lt{i}", [128, 4, 128], BF16) for i in range(2)]
            ss2 = [T(p7, f"ss2_{i}", [128, 4], F32) for i in range(4)]
            rs2 = [T(p7, f"rs2_{i}", [128, 4], F32) for i in range(4)]
            junk7 = T(p7, "junk7", [128, 128], BF16)
            xb7 = [T(p7, f"xb7_{i}", [128, 256], BF16) for i in range(4)]
            st7 = [T(p7, f"st7_{i}", [128, 2, 128], BF16) for i in range(4)]
            vs7 = [T(p7, f"vs7_{i}", [128, 2, 128], BF16) for i in range(4)]
            qrf = T(p7, "qrf", [128, 128], F32)
            qt1 = T(p7, "qt1", [128, 128], F32)
            qt2 = T(p7, "qt2", [128, 128], F32)
            qrb = [T(p7, f"qrb{i}", [128, 128], BF16) for i in range(4)]
            sr7 = [T(p7, f"sr7_{i}", [128, 128], BF16) for i in range(4)]
            cs1 = [T(p7, f"cs7_{i}", [128, 128], F32) for i in range(2)]
            ds_w = [P.dsem("x") for _ in range(2)]
            ds_l = [P.dsem("x") for _ in range(2)]
            ds_k = [P.dsem("x") for _ in range(4)]
            ds_v = [P.dsem("x") for _ in range(4)]
            ds_q = [P.dsem("x") for _ in range(4)]
            ds_c = [P.dsem("x") for _ in range(2)]
            P.dma("pool", ds_w[0], wkv[:], mla_w_ukv.rearrange("(k p) n -> p k n", p=128), writes=[wkv.b])
            P.dma("pool", ds_w[1], wq[:], mla_w_uq.rearrange("(k p) n -> p k n", p=128), writes=[wq.b])
            bcast_load(gkn, ds_misc, small_row("m_kn"))
            bcast_load(gqn, ds_misc, small_row("m_qn"))
            bcast_load(gqr, ds_misc, small_row("m_qr"))
            P.op("dve", "tensor_scalar", gqn[:], gqn[:], SCL, None, ALU.mult, reads=[gqn.b], writes=[gqn.b])
            P.op("dve", "tensor_scalar", gqr[:], gqr[:], SCL, None, ALU.mult, reads=[gqr.b], writes=[gqr.b])
            u = 0
            pending = []
            for kt in range(NKT):
                ls = kt % 2
                P.dma("sp", ds_l[ls], lt[ls][:], kv_lat_src(kt), writes=[lt[ls].b])
                for g in range(8):
                    z = u % 4
                    bk = z
                    for k in range(4):
                        P.op("pe", "matmul", psb[bk][:, :], lt[ls][:, k, :], wkv[:, k, g * 512:(g + 1) * 512], start=(k == 0), stop=(k == 3),
                             reads=[lt[ls].b, wkv.b], **({"writes": [psB[bk]]} if k == 0 else {"accs": [psB[bk]]}))
                    while len(pending) > 1:
                        pending.pop(0)()
                    for hh in range(2):
                        P.op("act", "activation", out=junk7[:], in_=psb[bk][:, hh * 256:hh * 256 + 128], func=AF.Square,
                             accum_out=ss2[z][:, hh:hh + 1], reads=[psB[bk]],
                             **({"writes": [junk7.b, ss2[z].b]} if hh == 0 else {"accs": [junk7.b, ss2[z].b]}))
                    rstd_from_ss(rs2[z], ss2[z], 128, 2)
                    for hh in range(2):
                        P.op("dve", "scalar_tensor_tensor", out=xb7[z][:, hh * 128:(hh + 1) * 128],
                             in0=psb[bk][:, hh * 256:hh * 256 + 128], scalar=rs2[z][:, hh:hh + 1], in1=gkn[:],
                             op0=ALU.mult, op1=ALU.mult, reads=[psB[bk], rs2[z].b, gkn.b],
                             **({"writes": [xb7[z].b]} if hh == 0 else {"accs": [xb7[z].b]}))
                    evac(vs7[z][:], psb[bk][:].rearrange("p (h a d) -> p h a d", h=2, a=2)[:, :, 1, :], reads=[psB[bk]], writes=[vs7[z].b])
                    P.dma("sp", ds_v[z], V1[2 * g:2 * g + 2, kt * 128:(kt + 1) * 128, :].rearrange("h t d -> t h d"), vs7[z][:],
                          reads=[vs7[z].b])
                    def tail(z=z, g=g, kt=kt):
                        tbk = 4 + z
                        transposes(xb7[z], 2, [tbk])
                        evac(st7[z][:].rearrange("p a b -> p (a b)"), psb[tbk][:, 0:256], reads=[psB[tbk]], writes=[st7[z].b])
                        P.dma("sp", ds_k[z], knT[2 * g:2 * g + 2, :, kt * 128:(kt + 1) * 128].rearrange("h d t -> d h t"), st7[z][:],
                              reads=[st7[z].b])
                    pending.append(tail)
                    u += 1
            for j in range(NQT):
                ls = j % 2
                P.dma("sp", ds_l[ls], lt[ls][:], qlatT.rearrange("(c p) t -> p c t", p=128)[:, :, j * 128:(j + 1) * 128],
                      writes=[lt[ls].b])
                P.dma("sp", ds_c[ls], cs1[ls][:], rope1[j * 128:(j + 1) * 128, :], writes=[cs1[ls].b])
                for g in range(8):
                    z = u % 4
                    bk = z
                    for k in range(4):
                        P.op("pe", "matmul", psb[bk][:, 0:384], lt[ls][:, k, :], wq[:, k, g * 384:(g + 1) * 384], start=(k == 0), stop=(k == 3),
                             reads=[lt[ls].b, wq.b], **({"writes": [psB[bk]]} if k == 0 else {"accs": [psB[bk]]}))
                    while len(pending) > 1:
                        pending.pop(0)()
                    for hh in range(2):
                        P.op("act", "activation", out=junk7[:], in_=psb[bk][:, hh * 192:hh * 192 + 128], func=AF.Square,
                             accum_out=ss2[z][:, hh:hh + 1], reads=[psB[bk]],
                             **({"writes": [junk7.b, ss2[z].b]} if hh == 0 else {"accs": [junk7.b, ss2[z].b]}))
                        P.op("act", "activation", out=junk7[:, 0:64], in_=psb[bk][:, hh * 192 + 128:hh * 192 + 192], func=AF.Square,
                             accum_out=ss2[z][:, 2 + hh:3 + hh], reads=[psB[bk]], accs=[junk7.b, ss2[z].b])
                    P.op("dve", "tensor_scalar", rs2[z][:, 0:2], ss2[z][:, 0:2], 1.0 / 128, EPS, ALU.mult, ALU.add,
                         reads=[ss2[z].b], writes=[rs2[z].b])
                    P.op("dve", "tensor_scalar", rs2[z][:, 2:4], ss2[z][:, 2:4], 1.0 / 64, EPS, ALU.mult, ALU.add,
                         reads=[ss2[z].b], accs=[rs2[z].b])
                    P.op("act", "activation", out=rs2[z][:], in_=rs2[z][:], func=AF.Sqrt, reads=[rs2[z].b], writes=[rs2[z].b])
                    P.op("dve", "reciprocal", rs2[z][:], rs2[z][:], reads=[rs2[z].b], writes=[rs2[z].b])
                    for hh in range(2):
                        P.op("dve", "scalar_tensor_tensor", out=xb7[z][:, hh * 128:(hh + 1) * 128],
                             in0=psb[bk][:, hh * 192:hh * 192 + 128], scalar=rs2[z][:, hh:hh + 1], in1=gqn[:],
                             op0=ALU.mult, op1=ALU.mult, reads=[psB[bk], rs2[z].b, gqn.b],
                             **({"writes": [xb7[z].b]} if hh == 0 else {"accs": [xb7[z].b]}))
                        P.op("dve", "scalar_tensor_tensor", out=qrf[:, hh * 64:(hh + 1) * 64],
                             in0=psb[bk][:, hh * 192 + 128:hh * 192 + 192], scalar=rs2[z][:, 2 + hh:3 + hh], in1=gqr[:],
                             op0=ALU.mult, op1=ALU.mult, reads=[psB[bk], rs2[z].b, gqr.b],
                             **({"writes": [qrf.b]} if hh == 0 else {"accs": [qrf.b]}))
                    P.op("dve", "tensor_tensor", qt1[:].rearrange("p (h d) -> p h d", h=2), qrf[:].rearrange("p (h d) -> p h d", h=2),
                         cs1[ls][:, None, 0:64].broadcast_to([128, 2, 64]), ALU.mult, reads=[qrf.b, cs1[ls].b], writes=[qt1.b])
                    x5 = qrf[:].rearrange("p (h b e d) -> p h b e d", h=2, b=2, e=2)
                    o5 = qt2[:].rearrange("p (h b e d) -> p h b e d", h=2, b=2, e=2)
                    sn = cs1[ls][:, 64:128].rearrange("p (b e d) -> p b e d", b=2, e=2)
                    for e in range(2):
                        P.op("dve", "tensor_tensor", o5[:, :, :, e, :], x5[:, :, :, 1 - e, :],
                             sn[:, None, :, e, :].broadcast_to([128, 2, 2, 16]), ALU.mult,
                             reads=[qrf.b, cs1[ls].b], **({"writes": [qt2.b]} if e == 0 else {"accs": [qt2.b]}))
                    P.op("dve", "tensor_tensor", qrb[z][:], qt1[:], qt2[:], ALU.add, reads=[qt1.b, qt2.b], writes=[qrb[z].b])
                    def tail(z=z, g=g, j=j):
                        tbk = 4 + z
                        transposes(xb7[z], 2, [tbk])
                        evac(st7[z][:].rearrange("p a b -> p (a b)"), psb[tbk][:, 0:256], reads=[psB[tbk]], writes=[st7[z].b])
                        P.dma("sp", ds_k[z], qnT[2 * g:2 * g + 2, :, j * 128:(j + 1) * 128].rearrange("h d t -> d h t"), st7[z][:],
                              reads=[st7[z].b])
                        P.op("pe", "matmul", psb[tbk][:, 256:384], qrb[z][:], ident[:], start=True, stop=True,
                             reads=[qrb[z].b, ident.b], accs=[psB[tbk]])
                        evac(sr7[z][:], psb[tbk][:, 256:384], reads=[psB[tbk]], writes=[sr7[z].b])
                        P.dma("sp", ds_q[z], qrT[2 * g:2 * g + 2].rearrange("h d t -> (h d) t")[:, j * 128:(j + 1) * 128], sr7[z][:],
                              reads=[sr7[z].b])
                    pending.append(tail)
                    u += 1
            while pending:
                pending.pop(0)()
            P.flush()
        with ExitStack() as ph:
            O1 = T(ph, "O1", [128, NQT, D], BF16)
            with ExitStack() as p8:
                kn = [T(p8, f"kn{i}", [128, NKV], BF16) for i in range(2)]
                v1 = [T(p8, f"v1_{i}", [128, NKT, 129], BF16) for i in range(2)]
                qn = [T(p8, f"qn{i}", [128, NQ], BF16) for i in range(2)]
                qr = [T(p8, f"qr{i}", [64, NQ], BF16) for i in range(2)]
                kr = T(p8, "kr", [64, NKV], BF16)
                pTs = [T(p8, f"mpT{i}", [128, 512], BF16) for i in range(3)]
                rinv = [T(p8, f"mrinv{i}", [128, 1], F32) for i in range(2)]
                dsl = [[P.dsem("x") for _ in range(4)] for _ in range(2)]
                for i, (c0, c1, src) in enumerate(kr_srcs):
                    P.dma("sp", ds_misc, kr[:, c0:c1], src, **({"writes": [kr.b]} if i == 0 else {"accs": [kr.b]}))
                for i in range(2):
                    P.op("dve", "memset", v1[i][:, :, 128:129], 1.0, writes=[v1[i].b])
                fcount = [0]
                for h in range(16):
                    s = h % 2
                    P.dma("sp", dsl[s][0], kn[s][:], knT[h], writes=[kn[s].b])
                    P.dma("sp", dsl[s][1], v1[s][:, :, 0:128], V1[h].rearrange("(kt p) d -> p kt d", p=128), writes=[v1[s].b])
                    P.dma("sp", dsl[s][2], qn[s][:], qnT[h], writes=[qn[s].b])
                    P.dma("sp", dsl[s][3], qr[s][:], qrT[h], writes=[qr[s].b])
                    for (q0, nq) in ((0, 512), (512, 512), (1024, 512), (1536, 512), (2048, 128)):
                        def fin(sub, ob, q0=q0, h=h):
                            z = fcount[0] % 2
                            fcount[0] += 1
                            tile = q0 // 128 + sub
                            P.op("dve", "reciprocal", rinv[z][:], psb[ob][:, 128:129], reads=[psB[ob]], writes=[rinv[z].b])
                            P.op("dve", "tensor_scalar", O1[:, tile, h * 128:(h + 1) * 128], psb[ob][:, 0:128], rinv[z][:, 0:1], None,
                                 ALU.mult, reads=[psB[ob], rinv[z].b], accs=[O1.b])
                        attn_dense([(kn[s], 128, qn[s]), (kr, 64, qr[s])], v1[s], list(range(NKT)), q0, nq, 128,
                                   [0, 1, 2], pTs, [3, 4, 5, 6], fin)
                P.flush()
            with ExitStack() as p9:
                zt = T(p9, "zt9", [128, 16, 1], BF16)
                P.op("dve", "memset", zt[:], 0.0, writes=[zt.b])
                hl = hTf_lat.rearrange("(k p) t -> p k t", p=128)
                P.dma("sp", ds_misc, hl[:, :, 0:1], zt[:], reads=[zt.b], allow_slow_non_contiguous=True)
                post_attn(p9, O1, NQT, NQT, mla_w_out, 1,
                          lambda j: x1q[j * 128:(j + 1) * 128, :],
                          lambda j: xmid1[j * 128:(j + 1) * 128, :],
                          lambda j: hl[:, :, 1 + j * 128:1 + (j + 1) * 128])
                P.flush()
        with ExitStack() as p10:
            units = [{"src": hTf_lat, "c0": 256 * u, "n": 256, "res": xmid1[256 * u:256 * (u + 1), :],
                      "dst": out[256 * u:256 * (u + 1), :], "row": 0} for u in range(8)]
            ffn(p10, 1, [units[2 * i:2 * i + 2] for i in range(4)])
            P.flush()
    top.close()
    return nc


_UPTO = [99]


def _perm(s):
    L = np.arange(SEQ)
    return L if s == 0 else (SEQ - 1 - L)


def _rope_tables(tok, half):
    freqs = (np.float32(10000.0) ** (-np.arange(half, dtype=np.float32) / np.float32(half))).astype(np.float32)
    r = (tok // 64).astype(np.float32)[:, None] * freqs[None, :]
    c = (tok % 64).astype(np.float32)[:, None] * freqs[None, :]
    cr, sr, ccs, sc = np.cos(r), np.sin(r), np.cos(c), np.sin(c)
    cos = np.concatenate([cr, cr, ccs, ccs], 1)
    sin = np.concatenate([-sr, sr, -sc, sc], 1)
    return np.concatenate([cos, sin], 1).astype(np.float32)


def _bias_tables(rpb, s):
    perm = _perm(s)
    out = np.empty((8, 128, 3, 5, 128), np.float32)
    p = np.arange(128)
    for v in range(3):
        tq = perm[128 * v + p]
        r, c = tq // 64, tq % 64
        rs0 = np.clip(r - 4, 0, 56)
        cs0 = np.clip(c - 8, 0, 48)
        for i in range(5):
            tk = perm[128 * i + p]
            kr, kc = tk // 64, tk % 64
            okr = (kr[:, None] >= rs0[None, :]) & (kr[:, None] < rs0[None, :] + 8)
            okc = (kc[:, None] >= cs0[None, :]) & (kc[:, None] < cs0[None, :] + 16)
            dr = np.clip(kr[:, None] - r[None, :] + 7, 0, 14)
            dc = np.clip(kc[:, None] - c[None, :] + 15, 0, 30)
            ok = okr & okc
            for h in range(8):
                out[h, :, v, i, :] = np.where(ok, rpb[h][dr, dc], np.float32(NEG))
    return out.reshape(8, 128, 3 * 5 * 128)


def _smalls(inp):
    v = np.zeros((1, NSM), np.float32)

    def put(name, arr):
        o, n = SM[name]
        a = np.asarray(arr, np.float32).reshape(-1)
        v[0, o:o + n] = np.tile(a, n // a.size)
    put("na_q", inp["na_q_norm"][0]); put("na_k", inp["na_k_norm"][0])
    put("df_q", inp["diff_q_norm"][0]); put("df_k", inp["diff_k_norm"][0])
    put("df_lam", inp["diff_lambda"][0]); put("df_sub", inp["diff_subln"][0])
    put("m_qa", inp["mla_q_a_norm"][0]); put("m_kva", inp["mla_kv_a_norm"][0])
    put("m_qn", inp["mla_q_nope_norm"][0]); put("m_qr", inp["mla_q_rope_norm"][0])
    put("m_kn", inp["mla_k_nope_norm"][0]); put("m_kr", inp["mla_k_rope_norm"][0])
    put("nmix0", inp["norm_mix"][0]); put("nmix1", inp["norm_mix"][1])
    put("nffn0", inp["norm_ffn"][0]); put("nffn1", inp["norm_ffn"][1])
    return v


def prep_shared(inp):
    sh = {
        "ident": np.eye(128, dtype=np.float32),
        "smalls": _smalls(inp),
        "ffn_w_in": np.ascontiguousarray(inp["ffn_w_in"], np.float32),
        "ffn_w_out": np.ascontiguousarray(inp["ffn_w_out"], np.float32),
        "ada_w": np.ascontiguousarray(inp["ada_w"], np.float32),
        "ada_b": np.ascontiguousarray(inp["ada_b"], np.float32).reshape(1, -1),
        "even_w_in": np.ascontiguousarray(inp["even_w_in"][0], np.float32),
        "even_w_out": np.ascontiguousarray(inp["even_w_out"][0], np.float32),
        "mla_w_down": np.ascontiguousarray(inp["mla_w_down"][0], np.float32),
        "mla_w_uq": np.ascontiguousarray(inp["mla_w_uq"][0], np.float32),
        "mla_w_ukv": np.ascontiguousarray(inp["mla_w_ukv"][0], np.float32),
        "mla_w_out": np.ascontiguousarray(inp["mla_w_out"][0], np.float32),
    }
    per_s = []
    for s in range(2):
        perm = _perm(s)
        conv = np.asarray(inp["ffn_conv"], np.float32)
        if s == 1:
            conv = conv[:, ::-1, :]
        cwt = conv.reshape(2, 3, 86, 128).transpose(0, 3, 2, 1).reshape(2, 128, 86 * 3)
        per_s.append({
            "cwt": np.ascontiguousarray(cwt),
            "rope0": _rope_tables(perm, 32),
            "rope1": _rope_tables(perm[:NQ], 16),
            "biasT": _bias_tables(np.asarray(inp["na_rpb"][0], np.float32), s),
        })
    return sh, per_s


def prep_core(inp, sh, per_s, core):
    b, s = core // 2, core % 2
    perm = _perm(s)
    x = np.asarray(inp["x"], np.float32)
    ctx = np.asarray(inp["ctx"], np.float32)
    cc = np.empty((128, 16, 2), np.float32)
    cc[:, :, 0] = np.asarray(inp["c"], np.float32)[b].reshape(16, 128).T
    cc[:, :, 1] = np.asarray(inp["c_ctx"], np.float32).reshape(16, 128).T
    m = dict(sh)
    m.update(per_s[s])
    m["xl"] = np.ascontiguousarray(x[b][perm])
    m["cl"] = np.ascontiguousarray(ctx[b] if s == 0 else ctx[b][::-1])
    m["cc"] = cc.reshape(128, 32)
    return m


A_INPUTS = ("ident", "smalls", "ffn_w_in", "ffn_w_out", "cwt", "xl", "cl", "cc", "ada_w", "ada_b", "even_w_in", "even_w_out",
            "rope0", "rope1", "biasT", "mla_w_down")
B_INPUTS = ("ident", "smalls", "ffn_w_in", "ffn_w_out", "cwt", "mla_w_uq", "mla_w_ukv", "mla_w_out", "rope1")


def _stage_inputs(m, names, layer):
    d = {}
    for k in names:
        v = m[k]
        if k in ("ffn_w_in", "ffn_w_out", "cwt"):
            v = v[layer:layer + 1]
        d[k] = v
    return d


def run_stage_A(inputs, cores, sh=None, per_s=None):
    if sh is None:
        sh, per_s = prep_shared(inputs)
    nc = build_program("A")
    in_maps = [_stage_inputs(prep_core(inputs, sh, per_s, c), A_INPUTS, 0) for c in cores]
    res = run_bass_kernel_spmd(nc, in_maps, core_ids=list(range(len(cores))))
    return res.results


def run_stage_B(inputs, cores, resA, sh=None, per_s=None):
    if sh is None:
        sh, per_s = prep_shared(inputs)
    nc = build_program("B")
    in_maps = []
    for c in cores:
        m = dict(sh)
        m.update(per_s[c % 2])
        d = _stage_inputs(m, B_INPUTS, 1)
        ra, rp = resA[c], resA[c ^ 1]
        d["modS"] = ra["modS"]
        d["x1q"] = ra["x1q"]
        d["qlatT"] = ra["qlatT"]
        d["kvx_all"] = np.ascontiguousarray(np.concatenate([ra["kvx_own"], rp["kvx_own"], ra["kvx_ctx"]], axis=1))
        in_maps.append(d)
    res = run_bass_kernel_spmd(nc, in_maps, core_ids=list(range(len(cores))))
    return res.results


FUSED = True
AB_INPUTS = A_INPUTS + ("mla_w_uq", "mla_w_ukv", "mla_w_out")


def kernel_fused(inputs):
    sh, per_s = prep_shared(inputs)
    cores = list(range(8))
    nc = build_program("AB")
    in_maps = []
    for c in cores:
        m = prep_core(inputs, sh, per_s, c)
        in_maps.append({k: m[k] for k in AB_INPUTS})
    res = run_bass_kernel_spmd(nc, in_maps, core_ids=cores)
    out = np.empty((4, SEQ, D), np.float32)
    for c in cores:
        b, s = c // 2, c % 2
        out[b][_perm(s)[:NOWN]] = res.results[c]["out"]
    return out


def kernel(**inputs):
    inputs = {k: np.asarray(v) for k, v in inputs.items()}
    if FUSED:
        return kernel_fused(inputs)
    sh, per_s = prep_shared(inputs)
    cores = list(range(8))
    ra = run_stage_A(inputs, cores, sh, per_s)
    keep = ("modS", "x1q", "qlatT", "kvx_own", "kvx_ctx")
    resA = {c: {k: ra[c][k] for k in keep} for c in cores}
    del ra
    rb = run_stage_B(inputs, cores, resA, sh, per_s)
    out = np.empty((4, SEQ, D), np.float32)
    for c in cores:
        b, s = c // 2, c % 2
        out[b][_perm(s)[:NOWN]] = rb[c]["out"]
    return out
```

```python
import math
from contextlib import ExitStack

import numpy as np
import concourse.bass as bass
import concourse.mybir as mybir
from concourse.bass_utils import run_bass_kernel_spmd

F32 = mybir.dt.float32
BF16 = mybir.dt.bfloat16
AF = mybir.ActivationFunctionType
ALU = mybir.AluOpType
AX = mybir.AxisListType

D = 2048
KC = 16
SEQ = 4096
CTX = 256
NOWN = 2048
NHALO = 128
NQ = NOWN + NHALO
NQT = NQ // 128
NQC = NQ + CTX
NKV = SEQ + CTX
NKT = NKV // 128
DFF = 5504
FC = DFF // 128
EPS = 1e-6
NEG = -30000.0
NA_KT = 21
LAM_INIT0 = 0.8 - 0.6 * math.exp(-0.3 * 0)


class Buf:
    __slots__ = ("name", "writers", "readers")

    def __init__(self, name=""):
        self.name = name
        self.writers = []
        self.readers = []


class DSem:
    def __init__(self, h, shared=False, step=16):
        self.h = h
        self.count = 0
        self.step = step
        self.shared = shared
        self.cell = [0]
        self.closed = False


class Ins:
    __slots__ = ("eng", "meth", "args", "kw", "deps", "marked", "val", "dsem", "cell", "raw")


class Prog:
    CENG = ("pe", "act", "dve", "pool")

    def __init__(self, nc, stack):
        self.nc = nc
        self.stack = stack
        self.engs = {"pe": nc.tensor, "act": nc.scalar, "dve": nc.vector, "pool": nc.gpsimd, "sp": nc.sync}
        self.csem = {e: stack.enter_context(nc.semaphore("cs_" + e)) for e in self.CENG}
        self.ccount = {e: 0 for e in self.CENG}
        self.bar = stack.enter_context(nc.semaphore("bar"))
        self.barcount = 0
        self.lists = {e: [] for e in self.engs}
        self.known = {e: {} for e in self.engs}
        self.bufs = []
        self.dsems = []
        self.dnext = 0
        self.ninstr = 0

    def buf(self, name=""):
        b = Buf(name)
        self.bufs.append(b)
        return b

    def dsem(self, name, shared=False):
        if shared:
            return DSem(self.stack.enter_context(self.nc.semaphore(name)), True)
        if self.dnext == len(self.dsems):
            self.dsems.append(DSem(self.stack.enter_context(self.nc.semaphore(f"dp{self.dnext}")), False))
        d = self.dsems[self.dnext]
        self.dnext += 1
        return d

    def _rec(self, eng, meth, args, kw, reads, writes, accs, dsem):
        ins = Ins()
        ins.eng, ins.meth, ins.args, ins.kw = eng, meth, args, kw
        ins.marked = False
        ins.val = 0
        ins.dsem = dsem
        ins.cell = None
        ins.raw = []
        deps = []
        for b in reads:
            deps.extend(b.writers)
        for b in writes:
            deps.extend(b.writers)
            deps.extend(b.readers)
        for b in accs:
            deps.extend(b.readers)
        seen = set()
        ins.deps = []
        for p in deps:
            if id(p) in seen:
                continue
            seen.add(id(p))
            if p.eng == "pe" and eng == "pe" and p.dsem is None:
                continue
            p.marked = True
            ins.deps.append(p)
            if p.dsem is not None and p.dsem.shared:
                p.dsem.closed = True
        for b in reads:
            b.readers.append(ins)
        for b in writes:
            b.writers = [ins]
            b.readers = []
        for b in accs:
            b.writers.append(ins)
        if dsem is not None:
            if dsem.shared:
                assert eng == "sp"
                if dsem.closed:
                    ins.raw.append((dsem.h, dsem.count))
                    dsem.cell = [0]
                    dsem.closed = False
                ins.cell = dsem.cell
            dsem.count += dsem.step
            ins.val = dsem.count
            if dsem.shared:
                dsem.cell[0] = dsem.count
            ins.marked = True
        self.lists[eng].append(ins)
        self.ninstr += 1
        return ins

    def op(self, eng, meth, *args, reads=(), writes=(), accs=(), **kw):
        return self._rec(eng, meth, args, kw, reads, writes, accs, None)

    def dma(self, eng, dsem, out, in_, reads=(), writes=(), accs=(), **kw):
        return self._rec(eng, "dma_start", (), dict(out=out, in_=in_, **kw), reads, writes, accs, dsem)

    def _token(self, p):
        if p.dsem is not None:
            return p.dsem.h, (p.cell[0] if p.cell is not None else p.val)
        return self.csem[p.eng], p.val

    def flush(self):
        for e in self.CENG:
            for ins in reversed(self.lists[e]):
                if ins.dsem is None:
                    ins.marked = True
                    break
        for e in self.CENG:
            for ins in self.lists[e]:
                if ins.dsem is None and ins.marked:
                    self.ccount[e] += 1
                    ins.val = self.ccount[e]
        self.barcount += 1
        dma_final = {e: {} for e in self.engs}
        for e in self.engs:
            for ins in self.lists[e]:
                if ins.dsem is not None:
                    dma_final[e][id(ins.dsem)] = (ins.dsem.h, ins.val)
        ndma_eng = sum(1 for e in self.engs if dma_final[e])
        bar_target_add = ndma_eng
        self._bar_total = getattr(self, "_bar_total", 0) + bar_target_add
        bar_total = self._bar_total
        cfinal = dict(self.ccount)
        lists = self.lists

        def make_body(e):
            def body(eng):
                known = self.known[e]
                for ins in lists[e]:
                    waits = {}
                    for h, v in ins.raw:
                        if known.get(id(h), 0) < v:
                            waits[id(h)] = (h, v)
                    for p in ins.deps:
                        h, v = self._token(p)
                        if known.get(id(h), 0) >= v:
                            continue
                        if id(h) not in waits or waits[id(h)][1] < v:
                            waits[id(h)] = (h, v)
                    for h, v in waits.values():
                        eng.wait_ge(h, v)
                        known[id(h)] = v
                    bi = getattr(eng, ins.meth)(*ins.args, **ins.kw)
                    if ins.dsem is not None:
                        bi.then_inc(ins.dsem.h, ins.dsem.step)
                    elif ins.marked:
                        bi.then_inc(self.csem[e], 1)
                if dma_final[e]:
                    for h, v in dma_final[e].values():
                        if known.get(id(h), 0) < v:
                            eng.wait_ge(h, v)
                            known[id(h)] = v
                    eng.sem_inc(self.bar, 1)
                for ce in self.CENG:
                    h, v = self.csem[ce], cfinal[ce]
                    if v > 0 and known.get(id(h), 0) < v:
                        eng.wait_ge(h, v)
                        known[id(h)] = v
                if bar_total > 0 and known.get(id(self.bar), 0) < bar_total:
                    eng.wait_ge(self.bar, bar_total)
                    known[id(self.bar)] = bar_total
            return body

        with self.nc.Block() as block:
            block.tensor(make_body("pe"))
            block.scalar(make_body("act"))
            block.vector(make_body("dve"))
            block.gpsimd(make_body("pool"))
            block.sync(make_body("sp"))
        self.lists = {e: [] for e in self.engs}
        self.dnext = 0
        for b in self.bufs:
            b.writers = []
            b.readers = []


SM = {}
_off = 0
for _n, _sz in (("na_q", 512), ("na_k", 512), ("df_q", 512), ("df_k", 512), ("df_lam", 512), ("df_sub", 256),
                ("m_qa", 512), ("m_kva", 512), ("m_qn", 128), ("m_qr", 64), ("m_kn", 128), ("m_kr", 64),
                ("nmix0", 2048), ("nmix1", 2048), ("nffn0", 2048), ("nffn1", 2048)):
    SM[_n] = (_off, _sz)
    _off += _sz
NSM = _off


def build_program(stage, debug=False):
    nc = bass.Bass("TRN2", target_bir_lowering=False)
    doA = "A" in stage
    doB = "B" in stage
    fused = stage == "AB"

    def din(name, shape, dt=F32):
        return nc.dram_tensor(name, list(shape), dt, kind="ExternalInput").ap()

    def dout(name, shape, dt=F32):
        return nc.dram_tensor(name, list(shape), dt, kind="ExternalOutput").ap()

    def scr(name, shape, dt):
        return nc.dram_tensor(name, list(shape), dt, kind="ExternalOutput" if debug else "Internal").ap()

    def xfer(name, shape, dt):
        if fused:
            return scr(name, shape, dt)
        if stage == "A":
            return dout(name, shape, dt)
        return din(name, shape, dt)

    ident_in = din("ident", [128, 128])
    smalls = din("smalls", [1, NSM])
    NL = 2 if fused else 1
    ffn_w_in = din("ffn_w_in", [NL, D, 2 * DFF])
    ffn_w_out = din("ffn_w_out", [NL, DFF, D])
    cwt_in = din("cwt", [NL, 128, 86 * 3])
    modS = xfer("modS", [2, 2 * 6 * D], F32)
    x1q = xfer("x1q", [NQ, D], F32)
    qlatT = xfer("qlatT", [512, NQ], BF16)
    if doA:
        kvx_own = xfer("kvx_own", [576, NOWN], BF16)
        kvx_ctx = xfer("kvx_ctx", [576, CTX], BF16)
        xl = din("xl", [SEQ, D])
        cl = din("cl", [CTX, D])
        cc_in = din("cc", [128, 32])
        ada_w = din("ada_w", [2, D, 6 * D])
        ada_b = din("ada_b", [1, 2 * 6 * D])
        even_w_in = din("even_w_in", [D, 6144])
        even_w_out = din("even_w_out", [D, D])
        rope0 = din("rope0", [SEQ, 256])
        rope1 = din("rope1", [NQ, 128])
        biasT = din("biasT", [8, 128, 3 * 5 * 128])
        mla_w_down = din("mla_w_down", [D, 1088])
        QaT = scr("QaT", [8, 128, NQC], BF16)
        KaT = scr("KaT", [8, 128, NA_KT * 128], BF16)
        Va = scr("Va", [8, NA_KT * 128, 128], BF16)
        QbT = scr("QbT", [8, 128, NQC], BF16)
        KbT = scr("KbT", [8, 128, NKV], BF16)
        Vb = scr("Vb", [4, NKV, 256], BF16)
        xmid = scr("xmid", [NQ, D], F32)
        xcmid = scr("xcmid", [CTX, D], F32)
        xc1 = scr("xc1", [CTX, D], F32)
    if doB:
        mla_w_uq = din("mla_w_uq", [512, 3072])
        mla_w_ukv = din("mla_w_ukv", [512, 4096])
        mla_w_out = din("mla_w_out", [D, D])
        if not fused:
            kvx_all = din("kvx_all", [576, NKV], BF16)
            rope1 = din("rope1", [NQ, 128])
        knT = scr("knT", [16, 128, NKV], BF16)
        V1 = scr("V1", [16, NKV, 128], BF16)
        qnT = scr("qnT", [16, 128, NQ], BF16)
        qrT = scr("qrT", [16, 64, NQ], BF16)
        xmid1 = scr("xmid1", [NQ, D], F32)
        out = dout("out", [NOWN, D], F32)
    hTf_lat = scr("hTf_lat", [D, NQ + 2], BF16)
    hTf_ctx = scr("hTf_ctx", [D, CTX + 2], BF16)

    top = ExitStack()
    P = Prog(nc, top)
    psb = [top.enter_context(nc.psum_tensor(f"ps{i}", [128, 512], F32)) for i in range(8)]
    psB = [P.buf(f"ps{i}") for i in range(8)]

    _uid = [0]

    class T:
        def __init__(self, st, name, shape, dt):
            _uid[0] += 1
            self.t = st.enter_context(nc.sbuf_tensor(f"sb{_uid[0]}_{name}", list(shape), dt))
            self.b = P.buf(name)

        def __getitem__(self, k):
            return self.t[k]

    ident = T(top, "ident", [128, 128], BF16)
    identf = T(top, "identf", [128, 128], F32)
    ds_misc = P.dsem("ds_misc", shared=True)
    P.dma("sp", ds_misc, identf[:], ident_in[:, :], writes=[identf.b])
    P.op("dve", "tensor_copy", ident[:], identf[:], reads=[identf.b], writes=[ident.b])

    def bcast_load(dst, dsem, row_ap, eng="sp"):
        P.dma(eng, dsem, dst[:], row_ap.partition_broadcast(128), writes=[dst.b])

    def small_row(name):
        o, n = SM[name]
        return smalls[0:1, o:o + n]

    def mod_row(r, l, v):
        o = l * 6 * D + v * D
        return modS[r:r + 1, o:o + D]

    _rr = [0]

    def evac(out_ap, in_ap, reads, writes=(), accs=()):
        _rr[0] ^= 1
        if _rr[0]:
            P.op("act", "activation", out=out_ap, in_=in_ap, func=AF.Copy, reads=reads, writes=writes, accs=accs)
        else:
            P.op("dve", "tensor_copy", out_ap, in_ap, reads=reads, writes=writes, accs=accs)

    def rstd_from_ss(rs, ss, n, width=1):
        P.op("dve", "tensor_scalar", rs[:, 0:width], ss[:, 0:width], 1.0 / n, EPS, ALU.mult, ALU.add,
             reads=[ss.b], writes=[rs.b])
        P.op("act", "activation", out=rs[:, 0:width], in_=rs[:, 0:width], func=AF.Sqrt, reads=[rs.b], writes=[rs.b])
        P.op("dve", "reciprocal", rs[:, 0:width], rs[:, 0:width], reads=[rs.b], writes=[rs.b])

    def transposes(src, nblk, banks, width=128, kp=128):
        for i in range(nblk):
            bk = banks[i // 4]
            col = (i % 4) * 128
            P.op("pe", "matmul", psb[bk][0:width, col:col + 128], src[:, i * width:(i + 1) * width], ident[:],
                 start=True, stop=True, reads=[src.b, ident.b],
                 **({"writes": [psB[bk]]} if i % 4 == 0 else {"accs": [psB[bk]]}))

    if doA:
        with ExitStack() as ph:
            cc = T(ph, "cc", [128, 32], F32)
            sT = T(ph, "sT", [128, 16, 2], BF16)
            adab = T(ph, "adab", [2, 6 * D], F32)
            modsb = T(ph, "modsb", [2, 6 * D], F32)
            wa = [T(ph, f"wa{i}", [128, 16, 512], BF16) for i in range(3)]
            ds_wa = [P.dsem(f"ds_wa{i}") for i in range(3)]
            P.dma("sp", ds_misc, cc[:], cc_in[:, :], writes=[cc.b])
            P.op("act", "activation", out=sT[:].rearrange("p k m -> p (k m)"), in_=cc[:], func=AF.Silu,
                 reads=[cc.b], writes=[sT.b])
            n = 0
            for l in range(2):
                wsrc = ada_w[l].rearrange("(k p) n -> p k n", p=128)
                P.dma("sp", ds_misc, adab[:], ada_b[0:1, l * 6 * D:(l + 1) * 6 * D].partition_broadcast(2), writes=[adab.b])
                for g in range(24):
                    s = n % 3
                    P.dma("pool", ds_wa[s], wa[s][:], wsrc[:, :, g * 512:(g + 1) * 512], writes=[wa[s].b])
                    bk = n % 2
                    for k in range(16):
                        P.op("pe", "matmul", psb[bk][0:2, :], sT[:, k, :], wa[s][:, k, :], start=(k == 0), stop=(k == 15),
                             reads=[sT.b, wa[s].b], **({"writes": [psB[bk]]} if k == 0 else {"accs": [psB[bk]]}))
                    o = g * 512
                    P.op("dve", "tensor_tensor", modsb[0:2, o:o + 512], psb[bk][0:2, :], adab[0:2, o:o + 512], ALU.add,
                         reads=[psB[bk], adab.b], **({"writes": [modsb.b]} if g == 0 else {"accs": [modsb.b]}))
                    n += 1
                P.dma("sp", ds_misc, modS[:, l * 6 * D:(l + 1) * 6 * D], modsb[:], reads=[modsb.b])
            P.flush()

    def modulate_tile(xt, hb, ss, rs, A, Bv):
        P.op("act", "activation", out=hb[:], in_=xt[:], func=AF.Square, accum_out=ss[:, 0:1],
             reads=[xt.b], writes=[hb.b, ss.b])
        rstd_from_ss(rs, ss, D)
        P.op("dve", "scalar_tensor_tensor", out=xt[:], in0=xt[:], scalar=rs[:, 0:1], in1=A[:], op0=ALU.mult, op1=ALU.mult,
             reads=[xt.b, rs.b, A.b], writes=[xt.b])
        P.op("dve", "tensor_tensor", hb[:], xt[:], Bv[:], ALU.add, reads=[xt.b, Bv.b], writes=[hb.b])

    def load_AB(A, Bv, tmp, dsem, row, l, v_shift, v_scale, norm_name):
        bcast_load(A, dsem, mod_row(row, l, v_scale))
        bcast_load(tmp, dsem, small_row(norm_name))
        bcast_load(Bv, dsem, mod_row(row, l, v_shift))
        P.op("dve", "scalar_tensor_tensor", out=A[:], in0=A[:], scalar=1.0, in1=tmp[:], op0=ALU.add, op1=ALU.mult,
             reads=[A.b, tmp.b], writes=[A.b])

    if doA:
        with ExitStack() as ph:
            hT = T(ph, "hT", [128, 16, NKV], BF16)
            with ExitStack() as p1:
                A_l = T(p1, "A_l", [128, D], F32)
                B_l = T(p1, "B_l", [128, D], F32)
                A_c = T(p1, "A_c", [128, D], F32)
                B_c = T(p1, "B_c", [128, D], F32)
                xt = [T(p1, f"xt{i}", [128, D], F32) for i in range(2)]
                hb = [T(p1, f"hb{i}", [128, D], BF16) for i in range(2)]
                ss = [T(p1, f"ss{i}", [128, 1], F32) for i in range(2)]
                rs = [T(p1, f"rs{i}", [128, 1], F32) for i in range(2)]
                ds_x = [P.dsem(f"ds_x{i}") for i in range(2)]
                load_AB(A_l, B_l, xt[0], ds_misc, 0, 0, 0, 1, "nmix0")
                load_AB(A_c, B_c, xt[1], ds_misc, 1, 0, 0, 1, "nmix0")
                for t in range(NKT):
                    s = t % 2
                    src = xl[t * 128:(t + 1) * 128, :] if t < 32 else cl[(t - 32) * 128:(t - 31) * 128, :]
                    A, Bv = (A_l, B_l) if t < 32 else (A_c, B_c)
                    P.dma("sp", ds_x[s], xt[s][:], src, writes=[xt[s].b])
                    modulate_tile(xt[s], hb[s], ss[s], rs[s], A, Bv)
                    for q4 in range(4):
                        bk = 4 * s + q4
                        for i in range(4):
                            k = 4 * q4 + i
                            P.op("pe", "matmul", psb[bk][:, i * 128:(i + 1) * 128], hb[s][:, k * 128:(k + 1) * 128], ident[:],
                                 start=True, stop=True, reads=[hb[s].b, ident.b],
                                 **({"writes": [psB[bk]]} if i == 0 else {"accs": [psB[bk]]}))
                        evac(hT[:, 4 * q4:4 * q4 + 4, t * 128:(t + 1) * 128],
                             psb[bk][:].rearrange("p (a b) -> p a b", a=4), reads=[psB[bk]], accs=[hT.b])
                P.flush()
            with ExitStack() as p2:
                wp = [T(p2, f"wp{i}", [128, 16, 512], BF16) for i in range(2)]
                ds_wp = [P.dsem(f"ds_wp{i}") for i in range(2)]
                gains = {}
                gtmp = T(p2, "gtmp", [128, 128], F32)
                for nm, key, sc in (("qa", "na_q", 128 ** -0.5), ("ka", "na_k", 1.0), ("qb", "df_q", 128 ** -0.5), ("kb", "df_k", 1.0)):
                    g = T(p2, "g_" + nm, [128, 128], F32)
                    o, _ = SM[key]
                    P.dma("sp", ds_misc, g[:], smalls[0:1, o:o + 128].partition_broadcast(128), writes=[g.b])
                    if sc != 1.0:
                        P.op("dve", "tensor_scalar", g[:], g[:], sc, None, ALU.mult, reads=[g.b], writes=[g.b])
                    gains[nm] = g
                cs = [T(p2, f"cs{i}", [128, 256], F32) for i in range(3)]
                ds_cs = [P.dsem(f"ds_cs{i}") for i in range(3)]
                ss4 = [T(p2, f"ss4_{i}", [128, 4], F32) for i in range(3)]
                rs4 = [T(p2, f"rs4_{i}", [128, 4], F32) for i in range(3)]
                junk = T(p2, "junk", [128, 128], BF16)
                xg = [T(p2, f"xg{i}", [128, 512], F32) for i in range(3)]
                t1 = [T(p2, f"t1_{i}", [128, 512], F32) for i in range(3)]
                t2 = [T(p2, f"t2_{i}", [128, 512], F32) for i in range(3)]
                xb = [T(p2, f"xb{i}", [128, 512], BF16) for i in range(3)]
                stg = [T(p2, f"stg{i}", [128, 512], BF16) for i in range(6)]
                ds_stg = [P.dsem(f"ds_stg{i}") for i in range(6)]
                wsrc = even_w_in.rearrange("(k p) n -> p k n", p=128)
                q_tiles = list(range(NQT)) + [32, 33]
                na_tiles = list(range(19)) + [32, 33]
                all_tiles = list(range(NKT))
                u = 0
                pending = []
                for g in range(12):
                    kind = ("qa", "ka", "va", "qb", "kb", "vb")[g // 2]
                    half = g % 2
                    ws = g % 2
                    P.dma("pool", ds_wp[ws], wp[ws][:], wsrc[:, :, g * 512:(g + 1) * 512], writes=[wp[ws].b])
                    tiles = {"qa": q_tiles, "qb": q_tiles, "ka": na_tiles, "va": na_tiles, "kb": all_tiles, "vb": all_tiles}[kind]
                    for ti, t in enumerate(tiles):
                        s = u % 3
                        bk = u % 3
                        for k in range(16):
                            P.op("pe", "matmul", psb[bk][:, :], hT[:, k, t * 128:(t + 1) * 128], wp[ws][:, k, :],
                                 start=(k == 0), stop=(k == 15), reads=[hT.b, wp[ws].b],
                                 **({"writes": [psB[bk]]} if k == 0 else {"accs": [psB[bk]]}))
                        while pending:
                            pending.pop(0)()
                        if kind in ("va", "vb"):
                            evac(stg[u % 6][:], psb[bk][:, :], reads=[psB[bk]], writes=[stg[u % 6].b])
                            if kind == "va":
                                dst = Va[4 * half:4 * half + 4, ti * 128:(ti + 1) * 128, :].rearrange("h t d -> t h d")
                                P.dma("sp", ds_stg[u % 6], dst, stg[u % 6][:].rearrange("t (h d) -> t h d", h=4), reads=[stg[u % 6].b])
                            else:
                                dst = Vb[2 * half:2 * half + 2, t * 128:(t + 1) * 128, :].rearrange("h t d -> t h d")
                                P.dma("sp", ds_stg[u % 6], dst, stg[u % 6][:].rearrange("t (h d) -> t h d", h=2), reads=[stg[u % 6].b])
                            u += 1
                            continue
                        for hh in range(4):
                            P.op("act", "activation", out=junk[:], in_=psb[bk][:, hh * 128:(hh + 1) * 128], func=AF.Square,
                                 accum_out=ss4[s][:, hh:hh + 1], reads=[psB[bk]],
                                 **({"writes": [junk.b, ss4[s].b]} if hh == 0 else {"accs": [junk.b, ss4[s].b]}))
                        rstd_from_ss(rs4[s], ss4[s], 128, 4)
                        rope = kind in ("qb", "kb") and t < 32
                        dstT = xg[s] if rope else xb[s]
                        for hh in range(4):
                            P.op("dve", "scalar_tensor_tensor", out=dstT[:, hh * 128:(hh + 1) * 128],
                                 in0=psb[bk][:, hh * 128:(hh + 1) * 128], scalar=rs4[s][:, hh:hh + 1], in1=gains[kind][:],
                                 op0=ALU.mult, op1=ALU.mult, reads=[psB[bk], rs4[s].b, gains[kind].b],
                                 **({"writes": [dstT.b]} if hh == 0 else {"accs": [dstT.b]}))
                        if rope:
                            P.dma("sp", ds_cs[s], cs[s][:], rope0[t * 128:(t + 1) * 128, :], writes=[cs[s].b])
                            x3 = xg[s][:].rearrange("p (h d) -> p h d", h=4)
                            P.op("dve", "tensor_tensor", t1[s][:].rearrange("p (h d) -> p h d", h=4), x3,
                                 cs[s][:, None, 0:128].broadcast_to([128, 4, 128]), ALU.mult,
                                 reads=[xg[s].b, cs[s].b], writes=[t1[s].b])
                            x4 = xg[s][:].rearrange("p (h b e d) -> p h b e d", h=4, b=2, e=2)
                            sn = cs[s][:, 128:256].rearrange("p (b e d) -> p b e d", b=2, e=2)
                            for e in range(2):
                                P.op("dve", "tensor_tensor",
                                     t2[s][:].rearrange("p (h b e d) -> p h b e d", h=4, b=2, e=2)[:, :, :, e, :],
                                     x4[:, :, :, 1 - e, :],
                                     sn[:, None, :, e, :].broadcast_to([128, 4, 2, 32]), ALU.mult,
                                     reads=[xg[s].b, cs[s].b], **({"writes": [t2[s].b]} if e == 0 else {"accs": [t2[s].b]}))
                            P.op("dve", "tensor_tensor", xb[s][:], t1[s][:], t2[s][:], ALU.add,
                                 reads=[t1[s].b, t2[s].b], writes=[xb[s].b])
                        def tail(s=s, u=u, kind=kind, half=half, t=t, ti=ti):
                            tb = 4 + (u % 3)
                            transposes(xb[s], 4, [tb])
                            evac(stg[u % 6][:], psb[tb][:, :], reads=[psB[tb]], writes=[stg[u % 6].b])
                            sv = stg[u % 6][:].rearrange("d (h t) -> d h t", h=4)
                            if kind == "qa":
                                col = t * 128 if t < 32 else NQ + (t - 32) * 128
                                dst = QaT[4 * half:4 * half + 4, :, col:col + 128]
                            elif kind == "qb":
                                col = t * 128 if t < 32 else NQ + (t - 32) * 128
                                dst = QbT[4 * half:4 * half + 4, :, col:col + 128]
                            elif kind == "ka":
                                dst = KaT[4 * half:4 * half + 4, :, ti * 128:(ti + 1) * 128]
                            else:
                                dst = KbT[4 * half:4 * half + 4, :, t * 128:(t + 1) * 128]
                            P.dma("sp", ds_stg[u % 6], dst.rearrange("h d t -> d h t"), sv, reads=[stg[u % 6].b])
                        pending.append(tail)
                        u += 1
                while pending:
                    pending.pop(0)()
                P.flush()
        if _UPTO[0] == 2:
            top.close()
            return nc

    def attn_dense(parts, V, kchunks, qcol0, nq, dv, S_banks, pTs, O_banks, fin):
        n = len(kchunks)
        NS = len(S_banks)
        nsub = nq // 128

        def emitS(i):
            bk = S_banks[i % NS]
            kc = kchunks[i]
            for pi, (kT, Kp, qT) in enumerate(parts):
                P.op("pe", "matmul", psb[bk][:, 0:nq], kT[0:Kp, kc * 128:(kc + 1) * 128], qT[0:Kp, qcol0:qcol0 + nq],
                     start=(pi == 0), stop=(pi == len(parts) - 1), reads=[kT.b, qT.b],
                     **({"writes": [psB[bk]]} if pi == 0 else {"accs": [psB[bk]]}))

        def emitE(i):
            bk = S_banks[i % NS]
            pT = pTs[i % NS]
            P.op("act", "activation", out=pT[:, 0:nq], in_=psb[bk][:, 0:nq], func=AF.Exp, reads=[psB[bk]], writes=[pT.b])

        def emitPV(i):
            pT = pTs[i % NS]
            kc = kchunks[i]
            for sub in range(nsub):
                ob = O_banks[sub]
                P.op("pe", "matmul", psb[ob][:, 0:dv + 1], pT[:, sub * 128:(sub + 1) * 128], V[:, kc, 0:dv + 1],
                     start=(i == 0), stop=(i == n - 1), reads=[pT.b, V.b],
                     **({"writes": [psB[ob]]} if i == 0 else {"accs": [psB[ob]]}))

        for i in range(min(NS - 1, n)):
            emitS(i)
        for i in range(n):
            if i + NS - 1 < n:
                emitS(i + NS - 1)
            emitE(i)
            emitPV(i)
        for sub in range(nsub):
            fin(sub, O_banks[sub])


    def post_attn(ph, O_src, ntiles, nlat, w_out_ap, l, x_src_fn, xmid_dst_fn, hT_dst_fn, nmix_unused=None):
        wo = T(ph, "wo", [128, 16, D], BF16)
        G = T(ph, "G", [128, D], F32)
        Af = T(ph, "Af", [128, D], F32)
        Bf = T(ph, "Bf", [128, D], F32)
        xt = [T(ph, f"pxt{i}", [128, D], F32) for i in range(2)]
        oTs = [T(ph, f"oT{i}", [128, 16, 128], BF16) for i in range(2)]
        h2b = T(ph, "h2b", [128, D], BF16)
        h2T = T(ph, "h2T", [128, 16, 128], BF16)
        tq = [T(ph, f"tq{i}", [128, 512], F32) for i in range(2)]
        ss = [T(ph, f"pss{i}", [128, 1], F32) for i in range(2)]
        rs = [T(ph, f"prs{i}", [128, 1], F32) for i in range(2)]
        ds_w = P.dsem("x")
        ds_x = [P.dsem("x") for _ in range(2)]
        ds_xo = [P.dsem("x") for _ in range(2)]
        ds_h = P.dsem("x")
        wsrc = w_out_ap.rearrange("(k p) n -> p k n", p=128)
        for q in range(4):
            P.dma("pool", ds_w, wo[:, 4 * q:4 * q + 4, :], wsrc[:, 4 * q:4 * q + 4, :],
                  **({"writes": [wo.b]} if q == 0 else {"accs": [wo.b]}))
        nffn = "nffn%d" % l

        def o_transposes(j):
            oT = oTs[j % 2]
            for q4 in range(4):
                for i in range(4):
                    k = 4 * q4 + i
                    P.op("pe", "matmul", psb[q4][:, i * 128:(i + 1) * 128], O_src[:, j, k * 128:(k + 1) * 128], ident[:],
                         start=True, stop=True, reads=[O_src.b, ident.b],
                         **({"writes": [psB[q4]]} if i == 0 else {"accs": [psB[q4]]}))
                P.op("act", "activation", out=oT[:, 4 * q4:4 * q4 + 4, :], in_=psb[q4][:].rearrange("p (a b) -> p a b", a=4),
                     func=AF.Copy, reads=[psB[q4]], **({"writes": [oT.b]} if q4 == 0 else {"accs": [oT.b]}))

        pending = []
        o_transposes(0)
        for j in range(ntiles):
            s = j % 2
            oT = oTs[j % 2]
            row = 0 if j < nlat else 1
            if j == 0 or j == nlat:
                bcast_load(G, ds_misc, mod_row(row, l, 2))
                load_AB(Af, Bf, xt[1 - s], ds_misc, row, l, 3, 4, nffn)
            P.dma("sp", ds_x[s], xt[s][:], x_src_fn(j), writes=[xt[s].b])
            for dg in range(4):
                bk = 4 + dg
                for k in range(16):
                    P.op("pe", "matmul", psb[bk][:, :], oT[:, k, :], wo[:, k, dg * 512:(dg + 1) * 512], start=(k == 0), stop=(k == 15),
                         reads=[oT.b, wo.b], **({"writes": [psB[bk]]} if k == 0 else {"accs": [psB[bk]]}))
                z = dg % 2
                P.op("dve", "tensor_tensor", tq[z][:], psb[bk][:, :], G[:, dg * 512:(dg + 1) * 512], ALU.mult,
                     reads=[psB[bk], G.b], writes=[tq[z].b])
                P.op("dve", "tensor_tensor", xt[s][:, dg * 512:(dg + 1) * 512], tq[z][:], xt[s][:, dg * 512:(dg + 1) * 512], ALU.add,
                     reads=[tq[z].b, xt[s].b], writes=[xt[s].b])
            if j + 1 < ntiles:
                o_transposes(j + 1)
            while pending:
                pending.pop(0)()
            P.dma("sp", ds_xo[s], xmid_dst_fn(j), xt[s][:], reads=[xt[s].b])
            modulate_tile(xt[s], h2b, ss[s], rs[s], Af, Bf)

            def tail(j=j):
                for q4 in range(4):
                    for i in range(4):
                        k = 4 * q4 + i
                        P.op("pe", "matmul", psb[q4][:, i * 128:(i + 1) * 128], h2b[:, k * 128:(k + 1) * 128], ident[:],
                             start=True, stop=True, reads=[h2b.b, ident.b],
                             **({"writes": [psB[q4]]} if i == 0 else {"accs": [psB[q4]]}))
                    P.op("act", "activation", out=h2T[:, 4 * q4:4 * q4 + 4, :], in_=psb[q4][:].rearrange("p (a b) -> p a b", a=4),
                         func=AF.Copy, reads=[psB[q4]], **({"writes": [h2T.b]} if q4 == 0 else {"accs": [h2T.b]}))
                P.dma("sp", ds_h, hT_dst_fn(j), h2T[:], reads=[h2T.b])
            pending.append(tail)
        while pending:
            pending.pop(0)()

    def ffn(ph, l, supers):
        gT = T(ph, "gT", [128, FC, 512], BF16)
        hTs = [T(ph, f"hTs{i}", [128, 16, 2, 258], BF16) for i in range(2)]
        wi = [T(ph, f"wi{i}", [128, 16, 2, 128], BF16) for i in range(3)]
        wo = [T(ph, f"fwo{i}", [128, FC, 256], BF16) for i in range(2)]
        Gs = [T(ph, f"Gf{i}", [128, D], F32) for i in range(2)]
        cw = T(ph, "cw", [128, 86 * 3], F32)
        ta = [T(ph, f"ta{i}", [128, 256], F32) for i in range(2)]
        tb = [T(ph, f"tb{i}", [128, 256], F32) for i in range(2)]
        sa = [T(ph, f"sa{i}", [128, 256], F32) for i in range(2)]
        xr = [T(ph, f"xr{i}", [128, 256], F32) for i in range(2)]
        ot = [T(ph, f"ot{i}", [128, 256], F32) for i in range(2)]
        ds_h = [[P.dsem("x") for _ in range(2)] for _ in range(2)]
        ds_wi = [P.dsem("x") for _ in range(3)]
        ds_wo = [P.dsem("x") for _ in range(2)]
        ds_xr = [P.dsem("x") for _ in range(2)]
        ds_ot = [P.dsem("x") for _ in range(2)]
        li = l if fused else 0
        P.dma("sp", ds_misc, cw[:], cwt_in[li], writes=[cw.b])
        rows = sorted({u["row"] for sp_ in supers for u in sp_})
        for r in rows:
            bcast_load(Gs[r], ds_misc, mod_row(r, l, 5))
        w_in_v = ffn_w_in[li].rearrange("(k p) n -> p k n", p=128)
        w_out_v = ffn_w_out[li].rearrange("(c p) n -> p c n", p=128)
        nwi = 0
        nwo = 0
        nz = 0
        nd = 0
        for si, units in enumerate(supers):
            sl = si % 2
            for ui, u in enumerate(units):
                n = u["n"]
                P.dma("sp", ds_h[sl][ui], hTs[sl][:, :, ui, 0:n + 2],
                      u["src"].rearrange("(k p) t -> p k t", p=128)[:, :, u["c0"]:u["c0"] + n + 2],
                      **({"writes": [hTs[sl].b]} if ui == 0 else {"accs": [hTs[sl].b]}))
            for c in range(FC):
                ws = nwi % 3
                nwi += 1
                for half in range(2):
                    P.dma("pool", ds_wi[ws], wi[ws][:, :, half, :], w_in_v[:, :, half * DFF + c * 128:half * DFF + (c + 1) * 128],
                          **({"writes": [wi[ws].b]} if half == 0 else {"accs": [wi[ws].b]}))
                for ui, u in enumerate(units):
                    n = u["n"]
                    z = nz % 2
                    nz += 1
                    ba, bb = 2 * z, 2 * z + 1
                    for half, bk in ((0, ba), (1, bb)):
                        for k in range(16):
                            P.op("pe", "matmul", psb[bk][:, 0:n + 2], wi[ws][:, k, half, :], hTs[sl][:, k, ui, 0:n + 2],
                                 start=(k == 0), stop=(k == 15), reads=[wi[ws].b, hTs[sl].b],
                                 **({"writes": [psB[bk]]} if k == 0 else {"accs": [psB[bk]]}))
                    for (tt, bk, ci) in ((ta[z], ba, c), (tb[z], bb, FC + c)):
                        P.op("act", "activation", out=tt[:, 0:n], in_=psb[bk][:, 1:n + 1], func=AF.Copy,
                             scale=cw[:, 3 * ci + 1:3 * ci + 2], reads=[psB[bk], cw.b], writes=[tt.b])
                        P.op("dve", "scalar_tensor_tensor", out=tt[:, 0:n], in0=psb[bk][:, 0:n], scalar=cw[:, 3 * ci:3 * ci + 1],
                             in1=tt[:, 0:n], op0=ALU.mult, op1=ALU.add, reads=[psB[bk], cw.b, tt.b], writes=[tt.b])
                        P.op("dve", "scalar_tensor_tensor", out=tt[:, 0:n], in0=psb[bk][:, 2:n + 2], scalar=cw[:, 3 * ci + 2:3 * ci + 3],
                             in1=tt[:, 0:n], op0=ALU.mult, op1=ALU.add, reads=[psB[bk], cw.b, tt.b], writes=[tt.b])
                    P.op("act", "activation", out=sa[z][:, 0:n], in_=ta[z][:, 0:n], func=AF.Silu, reads=[ta[z].b], writes=[sa[z].b])
                    P.op("dve", "tensor_tensor", gT[:, c, ui * 256:ui * 256 + n], sa[z][:, 0:n], tb[z][:, 0:n], ALU.mult,
                         reads=[sa[z].b, tb[z].b], **({"writes": [gT.b]} if (c == 0 and ui == 0) else {"accs": [gT.b]}))
            for dg in range(8):
                ws = nwo % 2
                nwo += 1
                P.dma("pool", ds_wo[ws], wo[ws][:], w_out_v[:, :, dg * 256:(dg + 1) * 256], writes=[wo[ws].b])
                for ui, u in enumerate(units):
                    for sub in range(u["n"] // 128):
                        z = nd % 2
                        nd += 1
                        bk = 4 + z
                        tc0 = ui * 256 + sub * 128
                        for c in range(FC):
                            P.op("pe", "matmul", psb[bk][:, 0:256], gT[:, c, tc0:tc0 + 128], wo[ws][:, c, :],
                                 start=(c == 0), stop=(c == FC - 1), reads=[gT.b, wo[ws].b],
                                 **({"writes": [psB[bk]]} if c == 0 else {"accs": [psB[bk]]}))
                        r0 = sub * 128
                        P.dma("sp", ds_xr[z], xr[z][:], u["res"][r0:r0 + 128, dg * 256:(dg + 1) * 256], writes=[xr[z].b])
                        P.op("dve", "tensor_tensor", ot[z][:], psb[bk][:, 0:256], Gs[u["row"]][:, dg * 256:(dg + 1) * 256], ALU.mult,
                             reads=[psB[bk], Gs[u["row"]].b], writes=[ot[z].b])
                        P.op("dve", "tensor_tensor", ot[z][:], ot[z][:], xr[z][:], ALU.add,
                             reads=[ot[z].b, xr[z].b], writes=[ot[z].b])
                        P.dma("sp", ds_ot[z], u["dst"][r0:r0 + 128, dg * 256:(dg + 1) * 256], ot[z][:], reads=[ot[z].b])


    if doA:
        with ExitStack() as ph:
            O_all = T(ph, "O_all", [128, 19, D], BF16)
            with ExitStack() as p3:
                kT = [T(p3, f"kT{i}", [128, NA_KT * 128], BF16) for i in range(2)]
                vA = [T(p3, f"vA{i}", [128, NA_KT, 129], BF16) for i in range(2)]
                qT = [T(p3, f"qT{i}", [128, NQC], BF16) for i in range(2)]
                bT = [T(p3, f"bT{i}", [128, 3, 5, 128], F32) for i in range(2)]
                dsl = [[P.dsem("x") for _ in range(4)] for _ in range(2)]
                sb = [T(p3, f"sb{i}", [128, 640], F32) for i in range(2)]
                pT = [T(p3, f"pT{i}", [128, 896], BF16) for i in range(2)]
                rinv = [T(p3, f"rinv{i}", [128, 1], F32) for i in range(2)]
                for i in range(2):
                    P.op("dve", "memset", vA[i][:, :, 128:129], 1.0, writes=[vA[i].b])
                n = 0
                for h in range(8):
                    s = h % 2
                    P.dma("sp", dsl[s][0], kT[s][:], KaT[h], writes=[kT[s].b])
                    P.dma("sp", dsl[s][1], vA[s][:, :, 0:128], Va[h].rearrange("(kt p) d -> p kt d", p=128), writes=[vA[s].b])
                    P.dma("sp", dsl[s][2], qT[s][:], QaT[h], writes=[qT[s].b])
                    P.dma("sp", dsl[s][3], bT[s][:].rearrange("p a b c -> p (a b c)"), biasT[h], writes=[bT[s].b])
                    for j in range(19):
                        lat = j < NQT
                        if lat:
                            st = min(max(2 * j - 4, 0), 54)
                            kts = [st // 2 + i for i in range(5)] + [19, 20]
                            var = min(j, 2)
                            qc = j * 128
                        else:
                            kts = [19, 20]
                            qc = NQ + (j - NQT) * 128
                        z = n % 2
                        bA, bB, bO = 2 * z, 2 * z + 1, 4 + z
                        nk = len(kts)
                        for i, kt in enumerate(kts):
                            bk, col = (bA, i * 128) if i < 4 else (bB, (i - 4) * 128)
                            P.op("pe", "matmul", psb[bk][:, col:col + 128], kT[s][:, kt * 128:(kt + 1) * 128], qT[s][:, qc:qc + 128],
                                 start=True, stop=True, reads=[kT[s].b, qT[s].b],
                                 **({"writes": [psB[bk]]} if i in (0, 4) else {"accs": [psB[bk]]}))
                        if lat:
                            P.op("dve", "tensor_tensor", sb[z][:, 0:512], psb[bA][:, :],
                                 bT[s][:, var, 0:4, :].rearrange("p a b -> p (a b)"), ALU.add,
                                 reads=[psB[bA], bT[s].b], writes=[sb[z].b])
                            P.op("dve", "tensor_tensor", sb[z][:, 512:640], psb[bB][:, 0:128], bT[s][:, var, 4, :], ALU.add,
                                 reads=[psB[bB], bT[s].b], accs=[sb[z].b])
                            P.op("act", "activation", out=pT[z][:, 0:640], in_=sb[z][:, 0:640], func=AF.Exp,
                                 reads=[sb[z].b], writes=[pT[z].b])
                            P.op("act", "activation", out=pT[z][:, 640:896], in_=psb[bB][:, 128:384], func=AF.Exp,
                                 reads=[psB[bB]], accs=[pT[z].b])
                        else:
                            P.op("act", "activation", out=pT[z][:, 0:256], in_=psb[bA][:, 0:256], func=AF.Exp,
                                 reads=[psB[bA]], writes=[pT[z].b])
                        for i, kt in enumerate(kts):
                            P.op("pe", "matmul", psb[bO][:, 0:129], pT[z][:, i * 128:(i + 1) * 128], vA[s][:, kt, :],
                                 start=(i == 0), stop=(i == nk - 1), reads=[pT[z].b, vA[s].b],
                                 **({"writes": [psB[bO]]} if i == 0 else {"accs": [psB[bO]]}))
                        P.op("dve", "reciprocal", rinv[z][:], psb[bO][:, 128:129], reads=[psB[bO]], writes=[rinv[z].b])
                        P.op("dve", "tensor_scalar", O_all[:, j, h * 128:(h + 1) * 128], psb[bO][:, 0:128], rinv[z][:, 0:1], None, ALU.mult,
                             reads=[psB[bO], rinv[z].b], accs=[O_all.b])
                        n += 1
                P.flush()
            with ExitStack() as p3:
                kT = [T(p3, f"dkT{i}", [128, 2, NKV], BF16) for i in range(2)]
                vB = [T(p3, f"vB{i}", [128, NKT, 257], BF16) for i in range(2)]
                qT = [T(p3, f"dqT{i}", [128, 2, NQC], BF16) for i in range(2)]
                dsl = [[P.dsem("x") for _ in range(3)] for _ in range(2)]
                pTs = [T(p3, f"dpT{i}", [128, 512], BF16) for i in range(3)]
                o1 = [T(p3, f"o1_{i}", [128, 256], F32) for i in range(4)]
                od = [T(p3, f"od_{i}", [128, 256], F32) for i in range(2)]
                rinv = [T(p3, f"drinv{i}", [128, 1], F32) for i in range(2)]
                ssd = [T(p3, f"ssd{i}", [128, 1], F32) for i in range(2)]
                rsd = [T(p3, f"rsd{i}", [128, 1], F32) for i in range(2)]
                junkd = T(p3, "junkd", [128, 256], BF16)
                lamt = T(p3, "lamt", [128, 512], F32)
                ltmp = T(p3, "ltmp", [128, 128], F32)
                e12 = T(p3, "e12", [128, 2], F32)
                nlam = T(p3, "nlam", [128, 1], F32)
                subg = T(p3, "subg", [128, 256], F32)
                for i in range(2):
                    P.op("dve", "memset", vB[i][:, :, 256:257], 1.0, writes=[vB[i].b])
                bcast_load(lamt, ds_misc, small_row("df_lam"))
                bcast_load(subg, ds_misc, small_row("df_sub"))
                for i in range(2):
                    P.op("dve", "tensor_tensor", ltmp[:], lamt[:, 256 * i:256 * i + 128], lamt[:, 256 * i + 128:256 * i + 256], ALU.mult,
                         reads=[lamt.b], writes=[ltmp.b])
                    P.op("dve", "reduce_sum", e12[:, i:i + 1], ltmp[:], AX.X, reads=[ltmp.b],
                         **({"writes": [e12.b]} if i == 0 else {"accs": [e12.b]}))
                P.op("act", "activation", out=e12[:], in_=e12[:], func=AF.Exp, reads=[e12.b], writes=[e12.b])
                P.op("dve", "tensor_tensor", nlam[:], e12[:, 1:2], e12[:, 0:1], ALU.subtract, reads=[e12.b], writes=[nlam.b])
                P.op("dve", "tensor_scalar", nlam[:], nlam[:], -LAM_INIT0, None, ALU.add, reads=[nlam.b], writes=[nlam.b])
                P.op("dve", "tensor_scalar", subg[:], subg[:], 1.0 - LAM_INIT0, None, ALU.mult, reads=[subg.b], writes=[subg.b])
                fcount = [0]
                for h in range(4):
                    s = h % 2
                    P.dma("sp", dsl[s][0], kT[s][:], KbT[2 * h:2 * h + 2].rearrange("a d t -> d a t"), writes=[kT[s].b])
                    P.dma("sp", dsl[s][1], vB[s][:, :, 0:256], Vb[h].rearrange("(kt p) d -> p kt d", p=128), writes=[vB[s].b])
                    P.dma("sp", dsl[s][2], qT[s][:], QbT[2 * h:2 * h + 2].rearrange("a d t -> d a t"), writes=[qT[s].b])
                    blocks = [(0, 512, "lat"), (512, 512, "lat"), (1024, 512, "lat"), (1536, 512, "lat"), (2048, 128, "lat"),
                              (NQ, 256, "ctx")]
                    for (q0, nq, kind) in blocks:
                        kch = list(range(NKT)) if kind == "lat" else [32, 33]
                        for sidx in range(2):
                            kTs = T.__new__(T)
                            kTs.t = kT[s].t[:, sidx, :]
                            kTs.b = kT[s].b
                            qTs = T.__new__(T)
                            qTs.t = qT[s].t[:, sidx, :]
                            qTs.b = qT[s].b

                            def fin(sub, ob, sidx=sidx, q0=q0, h=h, kind=kind):
                                z = fcount[0] % 2
                                fcount[0] += 1
                                tile = (q0 // 128 + sub) if kind == "lat" else (NQT + sub)
                                P.op("dve", "reciprocal", rinv[z][:], psb[ob][:, 256:257], reads=[psB[ob]], writes=[rinv[z].b])
                                if sidx == 0:
                                    P.op("dve", "tensor_scalar", o1[sub][:], psb[ob][:, 0:256], rinv[z][:, 0:1], None, ALU.mult,
                                         reads=[psB[ob], rinv[z].b], writes=[o1[sub].b])
                                    return
                                P.op("dve", "tensor_tensor", rinv[z][:], rinv[z][:], nlam[:], ALU.mult,
                                     reads=[rinv[z].b, nlam.b], writes=[rinv[z].b])
                                P.op("dve", "scalar_tensor_tensor", out=od[z][:], in0=psb[ob][:, 0:256], scalar=rinv[z][:, 0:1],
                                     in1=o1[sub][:], op0=ALU.mult, op1=ALU.add,
                                     reads=[psB[ob], rinv[z].b, o1[sub].b], writes=[od[z].b])
                                P.op("act", "activation", out=junkd[:], in_=od[z][:], func=AF.Square, accum_out=ssd[z][:, 0:1],
                                     reads=[od[z].b], writes=[junkd.b, ssd[z].b])
                                rstd_from_ss(rsd[z], ssd[z], 256)
                                P.op("dve", "scalar_tensor_tensor", out=O_all[:, tile, 1024 + 256 * h:1024 + 256 * (h + 1)],
                                     in0=od[z][:], scalar=rsd[z][:, 0:1], in1=subg[:], op0=ALU.mult, op1=ALU.mult,
                                     reads=[od[z].b, rsd[z].b, subg.b], accs=[O_all.b])

                            attn_dense([(kTs, 128, qTs)], vB[s], kch, q0, nq, 256, [0, 1, 2], pTs, [3, 4, 5, 6], fin)
                P.flush()
            if _UPTO[0] == 3:
                top.close()
                return nc
            with ExitStack() as p4:
                zt = T(p4, "zt", [128, 16, 1], BF16)
                P.op("dve", "memset", zt[:], 0.0, writes=[zt.b])
                hl = hTf_lat.rearrange("(k p) t -> p k t", p=128)
                hc = hTf_ctx.rearrange("(k p) t -> p k t", p=128)
                for dstv, col in ((hl, 0), (hl, NQ + 1), (hc, 0), (hc, CTX + 1)):
                    P.dma("sp", ds_misc, dstv[:, :, col:col + 1], zt[:], reads=[zt.b], allow_slow_non_contiguous=True)
                post_attn(p4, O_all, 19, NQT, even_w_out, 0,
                          lambda j: xl[j * 128:(j + 1) * 128, :] if j < NQT else cl[(j - NQT) * 128:(j - NQT + 1) * 128, :],
                          lambda j: xmid[j * 128:(j + 1) * 128, :] if j < NQT else xcmid[(j - NQT) * 128:(j - NQT + 1) * 128, :],
                          lambda j: (hl[:, :, 1 + j * 128:1 + (j + 1) * 128] if j < NQT
                                     else hc[:, :, 1 + (j - NQT) * 128:1 + (j - NQT + 1) * 128]))
                P.flush()
        if _UPTO[0] == 4:
            top.close()
            return nc
        with ExitStack() as p5:
            def unit(src, c0, n, res, dst, row):
                return {"src": src, "c0": c0, "n": n, "res": res, "dst": dst, "row": row}
            lat_units = [unit(hTf_lat, 256 * u, 256, xmid[256 * u:256 * (u + 1), :], x1q[256 * u:256 * (u + 1), :], 0) for u in range(8)]
            halo_unit = unit(hTf_lat, NOWN, 128, xmid[NOWN:NQ, :], x1q[NOWN:NQ, :], 0)
            ctx_unit = unit(hTf_ctx, 0, 256, xcmid[:, :], xc1[:, :], 1)
            supers = [lat_units[2 * i:2 * i + 2] for i in range(4)] + [[halo_unit, ctx_unit]]
            ffn(p5, 0, supers)
            P.flush()
        if _UPTO[0] == 5:
            top.close()
            return nc
        with ExitStack() as p6:
            wd = T(p6, "wd", [128, 16, 1088], BF16)
            A1 = T(p6, "A1", [128, D], F32)
            B1 = T(p6, "B1", [128, D], F32)
            xt = [T(p6, f"dxt{i}", [128, D], F32) for i in range(2)]
            hbs = [T(p6, f"dhb{i}", [128, D], BF16) for i in range(2)]
            hT1s = [T(p6, f"hT1_{i}", [128, 16, 128], BF16) for i in range(2)]
            gq = T(p6, "gq", [128, 512], F32)
            gkv = T(p6, "gkv", [128, 512], F32)
            gkr = T(p6, "gkr", [128, 64], F32)
            cs1 = [T(p6, f"cs1_{i}", [128, 128], F32) for i in range(2)]
            ss = [T(p6, f"dss{i}", [128, 1], F32) for i in range(2)]
            rs = [T(p6, f"drs{i}", [128, 1], F32) for i in range(2)]
            ss3 = [T(p6, f"dss3{i}", [128, 3], F32) for i in range(2)]
            rs3 = [T(p6, f"drs3{i}", [128, 3], F32) for i in range(2)]
            junk6 = T(p6, "junk6", [128, 512], BF16)
            nb = [T(p6, f"nb{i}", [128, 512], BF16) for i in range(2)]
            krfs = [T(p6, f"krf{i}", [128, 64], F32) for i in range(2)]
            kt1s = [T(p6, f"kt1_{i}", [128, 64], F32) for i in range(2)]
            kt2s = [T(p6, f"kt2_{i}", [128, 64], F32) for i in range(2)]
            krbs = [T(p6, f"krb{i}", [128, 64], BF16) for i in range(2)]
            stg6 = [T(p6, f"stg6_{i}", [128, 4, 128], BF16) for i in range(2)]
            stgrs = [T(p6, f"stgr{i}", [64, 128], BF16) for i in range(2)]
            ds_w = P.dsem("x")
            ds_x = [P.dsem("x") for _ in range(2)]
            ds_c = [P.dsem("x") for _ in range(2)]
            ds_s = [P.dsem("x") for _ in range(2)]
            ds_rs = [P.dsem("x") for _ in range(2)]
            P.dma("pool", ds_w, wd[:], mla_w_down.rearrange("(k p) n -> p k n", p=128), writes=[wd.b])
            bcast_load(gq, ds_misc, small_row("m_qa"))
            bcast_load(gkv, ds_misc, small_row("m_kva"))
            bcast_load(gkr, ds_misc, small_row("m_kr"))
            nst = 0
            for j in range(19):
                s = j % 2
                lat = j < NQT
                hb, hT1, krf, kt1, kt2, krb, stgr, ds_r = hbs[s], hT1s[s], krfs[s], kt1s[s], kt2s[s], krbs[s], stgrs[s], ds_rs[s]
                if j == 0 or j == NQT:
                    load_AB(A1, B1, xt[1 - s], ds_misc, 0 if lat else 1, 1, 0, 1, "nmix1")
                src = x1q[j * 128:(j + 1) * 128, :] if lat else xc1[(j - NQT) * 128:(j - NQT + 1) * 128, :]
                P.dma("sp", ds_x[s], xt[s][:], src, writes=[xt[s].b])
                modulate_tile(xt[s], hb, ss[s], rs[s], A1, B1)
                for q4 in range(4):
                    for i in range(4):
                        k = 4 * q4 + i
                        P.op("pe", "matmul", psb[q4][:, i * 128:(i + 1) * 128], hb[:, k * 128:(k + 1) * 128], ident[:],
                             start=True, stop=True, reads=[hb.b, ident.b],
                             **({"writes": [psB[q4]]} if i == 0 else {"accs": [psB[q4]]}))
                    evac(hT1[:, 4 * q4:4 * q4 + 4, :], psb[q4][:].rearrange("p (a b) -> p a b", a=4), reads=[psB[q4]],
                         **({"writes": [hT1.b]} if q4 == 0 else {"accs": [hT1.b]}))
                for gi, (c0, w) in enumerate(((0, 512), (512, 512), (1024, 64))):
                    bk = 4 + gi
                    for k in range(16):
                        P.op("pe", "matmul", psb[bk][:, 0:w], hT1[:, k, :], wd[:, k, c0:c0 + w], start=(k == 0), stop=(k == 15),
                             reads=[hT1.b, wd.b], **({"writes": [psB[bk]]} if k == 0 else {"accs": [psB[bk]]}))
                for gi, w in ((0, 512), (1, 512), (2, 64)):
                    P.op("act", "activation", out=junk6[:, 0:w], in_=psb[4 + gi][:, 0:w], func=AF.Square,
                         accum_out=ss3[s][:, gi:gi + 1], reads=[psB[4 + gi]],
                         **({"writes": [junk6.b, ss3[s].b]} if gi == 0 else {"accs": [junk6.b, ss3[s].b]}))
                P.op("dve", "tensor_scalar", rs3[s][:, 0:2], ss3[s][:, 0:2], 1.0 / 512, EPS, ALU.mult, ALU.add,
                     reads=[ss3[s].b], writes=[rs3[s].b])
                P.op("dve", "tensor_scalar", rs3[s][:, 2:3], ss3[s][:, 2:3], 1.0 / 64, EPS, ALU.mult, ALU.add,
                     reads=[ss3[s].b], accs=[rs3[s].b])
                P.op("act", "activation", out=rs3[s][:], in_=rs3[s][:], func=AF.Sqrt, reads=[rs3[s].b], writes=[rs3[s].b])
                P.op("dve", "reciprocal", rs3[s][:], rs3[s][:], reads=[rs3[s].b], writes=[rs3[s].b])
                jobs = []
                if lat:
                    jobs.append((0, gq, qlatT.rearrange("(c p) t -> p c t", p=128)[:, :, j * 128:(j + 1) * 128]))
                if j < 16:
                    jobs.append((1, gkv, kvx_own[0:512, :].rearrange("(c p) t -> p c t", p=128)[:, :, j * 128:(j + 1) * 128]))
                if not lat:
                    jj = j - NQT
                    jobs.append((1, gkv, kvx_ctx[0:512, :].rearrange("(c p) t -> p c t", p=128)[:, :, jj * 128:(jj + 1) * 128]))
                for (gi, gain, dst) in jobs:
                    z = nst % 2
                    nst += 1
                    P.op("dve", "scalar_tensor_tensor", out=nb[z][:], in0=psb[4 + gi][:, :], scalar=rs3[s][:, gi:gi + 1], in1=gain[:],
                         op0=ALU.mult, op1=ALU.mult, reads=[psB[4 + gi], rs3[s].b, gain.b], writes=[nb[z].b])
                    tbk = 2 + z
                    transposes(nb[z], 4, [tbk])
                    evac(stg6[z][:].rearrange("p a b -> p (a b)"), psb[tbk][:, :], reads=[psB[tbk]], writes=[stg6[z].b])
                    P.dma("sp", ds_s[z], dst, stg6[z][:], reads=[stg6[z].b])
                if j < 16 or not lat:
                    P.op("dve", "scalar_tensor_tensor", out=krf[:], in0=psb[6][:, 0:64], scalar=rs3[s][:, 2:3], in1=gkr[:],
                         op0=ALU.mult, op1=ALU.mult, reads=[psB[6], rs3[s].b, gkr.b], writes=[krf.b])
                    if lat:
                        P.dma("sp", ds_c[s], cs1[s][:], rope1[j * 128:(j + 1) * 128, :], writes=[cs1[s].b])
                        P.op("dve", "tensor_tensor", kt1[:], krf[:], cs1[s][:, 0:64], ALU.mult, reads=[krf.b, cs1[s].b], writes=[kt1.b])
                        x4 = krf[:].rearrange("p (b e d) -> p b e d", b=2, e=2)
                        sn = cs1[s][:, 64:128].rearrange("p (b e d) -> p b e d", b=2, e=2)
                        o4 = kt2[:].rearrange("p (b e d) -> p b e d", b=2, e=2)
                        for e in range(2):
                            P.op("dve", "tensor_tensor", o4[:, :, e, :], x4[:, :, 1 - e, :], sn[:, :, e, :], ALU.mult,
                                 reads=[krf.b, cs1[s].b], **({"writes": [kt2.b]} if e == 0 else {"accs": [kt2.b]}))
                        P.op("dve", "tensor_tensor", krb[:], kt1[:], kt2[:], ALU.add, reads=[kt1.b, kt2.b], writes=[krb.b])
                    else:
                        P.op("dve", "tensor_copy", krb[:], krf[:], reads=[krf.b], writes=[krb.b])
                    P.op("pe", "matmul", psb[7][0:64, 0:128], krb[:, 0:64], ident[:], start=True, stop=True,
                         reads=[krb.b, ident.b], writes=[psB[7]])
                    evac(stgr[:], psb[7][0:64, 0:128], reads=[psB[7]], writes=[stgr.b])
                    if lat:
                        dstr = kvx_own[512:576, j * 128:(j + 1) * 128]
                    else:
                        dstr = kvx_ctx[512:576, (j - NQT) * 128:(j - NQT + 1) * 128]
                    P.dma("sp", ds_r, dstr, stgr[:], reads=[stgr.b])
            P.flush()
    if doB:
        if fused:
            kvx_pair = scr("kvx_pair", [5, 256, NOWN], BF16)
            cc = DSem(top.enter_context(nc.semaphore("cc_sem")), False, 1)
            for i in range(5):
                rows = 128 if i < 4 else 64
                P._rec("pool", "collective_compute", ("AllGather", ALU.bypass),
                       dict(replica_groups=[[0, 1], [2, 3], [4, 5], [6, 7]], ins=[kvx_own[128 * i:128 * i + rows, :]],
                            outs=[kvx_pair[i, 0:2 * rows, :]]), (), (), (), cc)
            P.flush()

            def kv_lat_src(kt):
                if kt < 32:
                    r, c = kt // 16, kt % 16
                    return kvx_pair[0:4, r * 128:(r + 1) * 128, c * 128:(c + 1) * 128].rearrange("c p t -> p c t")
                return kvx_ctx[0:512, :].rearrange("(c p) t -> p c t", p=128)[:, :, (kt - 32) * 128:(kt - 31) * 128]
            kr_srcs = [(0, NOWN, kvx_pair[4, 0:64, :]), (NOWN, 2 * NOWN, kvx_pair[4, 64:128, :]),
                       (2 * NOWN, NKV, kvx_ctx[512:576, :])]
        else:
            kv_lat_src = lambda kt: kvx_all[0:512, :].rearrange("(c p) t -> p c t", p=128)[:, :, kt * 128:(kt + 1) * 128]
            kr_srcs = [(0, NKV, kvx_all[512:576, :])]
        SCL = (128 + 64) ** -0.5
        with ExitStack() as p7:
            wkv = T(p7, "wkv", [128, 4, 4096], BF16)
            wq = T(p7, "wq", [128, 4, 3072], BF16)
            gkn = T(p7, "gkn", [128, 128], F32)
            gqn = T(p7, "gqn", [128, 128], F32)
            gqr = T(p7, "gqr", [128, 64], F32)
            lt = [T(p7, f"lt{i}", [128, 4, 128], BF16) for i in range(2)]
            ss2 = [T(p7, f"ss2_{i}", [128, 4], F32) for i in range(4)]
            rs2 = [T(p7, f"rs2_{i}", [128, 4], F32) for i in range(4)]
            junk7 = T(p7, "junk7", [128, 128], BF16)
            xb7 = [T(p7, f"xb7_{i}", [128, 256], BF16) for i in range(4)]
            st7 = [T(p7, f"st7_{i}", [128, 2, 128], BF16) for i in range(4)]
            vs7 = [T(p7, f"vs7_{i}", [128, 2, 128], BF16) for i in range(4)]
            qrf = T(p7, "qrf", [128, 128], F32)
            qt1 = T(p7, "qt1", [128, 128], F32)
            qt2 = T(p7, "qt2", [128, 128], F32)
            qrb = [T(p7, f"qrb{i}", [128, 128], BF16) for i in range(4)]
            sr7 = [T(p7, f"sr7_{i}", [128, 128], BF16) for i in range(4)]
            cs1 = [T(p7, f"cs7_{i}", [128, 128], F32) for i in range(2)]
            ds_w = [P.dsem("x") for _ in range(2)]
            ds_l = [P.dsem("x") for _ in range(2)]
            ds_k = [P.dsem("x") for _ in range(4)]
            ds_v = [P.dsem("x") for _ in range(4)]
            ds_q = [P.dsem("x") for _ in range(4)]
            ds_c = [P.dsem("x") for _ in range(2)]
            P.dma("pool", ds_w[0], wkv[:], mla_w_ukv.rearrange("(k p) n -> p k n", p=128), writes=[wkv.b])
            P.dma("pool", ds_w[1], wq[:], mla_w_uq.rearrange("(k p) n -> p k n", p=128), writes=[wq.b])
            bcast_load(gkn, ds_misc, small_row("m_kn"))
            bcast_load(gqn, ds_misc, small_row("m_qn"))
            bcast_load(gqr, ds_misc, small_row("m_qr"))
            P.op("dve", "tensor_scalar", gqn[:], gqn[:], SCL, None, ALU.mult, reads=[gqn.b], writes=[gqn.b])
            P.op("dve", "tensor_scalar", gqr[:], gqr[:], SCL, None, ALU.mult, reads=[gqr.b], writes=[gqr.b])
            u = 0
            pending = []
            for kt in range(NKT):
                ls = kt % 2
                P.dma("sp", ds_l[ls], lt[ls][:], kv_lat_src(kt), writes=[lt[ls].b])
                for g in range(8):
                    z = u % 4
                    bk = z
                    for k in range(4):
                        P.op("pe", "matmul", psb[bk][:, :], lt[ls][:, k, :], wkv[:, k, g * 512:(g + 1) * 512], start=(k == 0), stop=(k == 3),
                             reads=[lt[ls].b, wkv.b], **({"writes": [psB[bk]]} if k == 0 else {"accs": [psB[bk]]}))
                    while len(pending) > 1:
                        pending.pop(0)()
                    for hh in range(2):
                        P.op("act", "activation", out=junk7[:], in_=psb[bk][:, hh * 256:hh * 256 + 128], func=AF.Square,
                             accum_out=ss2[z][:, hh:hh + 1], reads=[psB[bk]],
                             **({"writes": [junk7.b, ss2[z].b]} if hh == 0 else {"accs": [junk7.b, ss2[z].b]}))
                    rstd_from_ss(rs2[z], ss2[z], 128, 2)
                    for hh in range(2):
                        P.op("dve", "scalar_tensor_tensor", out=xb7[z][:, hh * 128:(hh + 1) * 128],
                             in0=psb[bk][:, hh * 256:hh * 256 + 128], scalar=rs2[z][:, hh:hh + 1], in1=gkn[:],
                             op0=ALU.mult, op1=ALU.mult, reads=[psB[bk], rs2[z].b, gkn.b],
                             **({"writes": [xb7[z].b]} if hh == 0 else {"accs": [xb7[z].b]}))
                    evac(vs7[z][:], psb[bk][:].rearrange("p (h a d) -> p h a d", h=2, a=2)[:, :, 1, :], reads=[psB[bk]], writes=[vs7[z].b])
                    P.dma("sp", ds_v[z], V1[2 * g:2 * g + 2, kt * 128:(kt + 1) * 128, :].rearrange("h t d -> t h d"), vs7[z][:],
                          reads=[vs7[z].b])
                    def tail(z=z, g=g, kt=kt):
                        tbk = 4 + z
                        transposes(xb7[z], 2, [tbk])
                        evac(st7[z][:].rearrange("p a b -> p (a b)"), psb[tbk][:, 0:256], reads=[psB[tbk]], writes=[st7[z].b])
                        P.dma("sp", ds_k[z], knT[2 * g:2 * g + 2, :, kt * 128:(kt + 1) * 128].rearrange("h d t -> d h t"), st7[z][:],
                              reads=[st7[z].b])
                    pending.append(tail)
                    u += 1
            for j in range(NQT):
                ls = j % 2
                P.dma("sp", ds_l[ls], lt[ls][:], qlatT.rearrange("(c p) t -> p c t", p=128)[:, :, j * 128:(j + 1) * 128],
                      writes=[lt[ls].b])
                P.dma("sp", ds_c[ls], cs1[ls][:], rope1[j * 128:(j + 1) * 128, :], writes=[cs1[ls].b])
                for g in range(8):
                    z = u % 4
                    bk = z
                    for k in range(4):
                        P.op("pe", "matmul", psb[bk][:, 0:384], lt[ls][:, k, :], wq[:, k, g * 384:(g + 1) * 384], start=(k == 0), stop=(k == 3),
                             reads=[lt[ls].b, wq.b], **({"writes": [psB[bk]]} if k == 0 else {"accs": [psB[bk]]}))
                    while len(pending) > 1:
                        pending.pop(0)()
                    for hh in range(2):
                        P.op("act", "activation", out=junk7[:], in_=psb[bk][:, hh * 192:hh * 192 + 128], func=AF.Square,
                             accum_out=ss2[z][:, hh:hh + 1], reads=[psB[bk]],
                             **({"writes": [junk7.b, ss2[z].b]} if hh == 0 else {"accs": [junk7.b, ss2[z].b]}))
                        P.op("act", "activation", out=junk7[:, 0:64], in_=psb[bk][:, hh * 192 + 128:hh * 192 + 192], func=AF.Square,
                             accum_out=ss2[z][:, 2 + hh:3 + hh], reads=[psB[bk]], accs=[junk7.b, ss2[z].b])
                    P.op("dve", "tensor_scalar", rs2[z][:, 0:2], ss2[z][:, 0:2], 1.0 / 128, EPS, ALU.mult, ALU.add,
                         reads=[ss2[z].b], writes=[rs2[z].b])
                    P.op("dve", "tensor_scalar", rs2[z][:, 2:4], ss2[z][:, 2:4], 1.0 / 64, EPS, ALU.mult, ALU.add,
                         reads=[ss2[z].b], accs=[rs2[z].b])
                    P.op("act", "activation", out=rs2[z][:], in_=rs2[z][:], func=AF.Sqrt, reads=[rs2[z].b], writes=[rs2[z].b])
                    P.op("dve", "reciprocal", rs2[z][:], rs2[z][:], reads=[rs2[z].b], writes=[rs2[z].b])
                    for hh in range(2):
                        P.op("dve", "scalar_tensor_tensor", out=xb7[z][:, hh * 128:(hh + 1) * 128],
                             in0=psb[bk][:, hh * 192:hh * 192 + 128], scalar=rs2[z][:, hh:hh + 1], in1=gqn[:],
                             op0=ALU.mult, op1=ALU.mult, reads=[psB[bk], rs2[z].b, gqn.b],
                             **({"writes": [xb7[z].b]} if hh == 0 else {"accs": [xb7[z].b]}))
                        P.op("dve", "scalar_tensor_tensor", out=qrf[:, hh * 64:(hh + 1) * 64],
                             in0=psb[bk][:, hh * 192 + 128:hh * 192 + 192], scalar=rs2[z][:, 2 + hh:3 + hh], in1=gqr[:],
                             op0=ALU.mult, op1=ALU.mult, reads=[psB[bk], rs2[z].b, gqr.b],
                             **({"writes": [qrf.b]} if hh == 0 else {"accs": [qrf.b]}))
                    P.op("dve", "tensor_tensor", qt1[:].rearrange("p (h d) -> p h d", h=2), qrf[:].rearrange("p (h d) -> p h d", h=2),
                         cs1[ls][:, None, 0:64].broadcast_to([128, 2, 64]), ALU.mult, reads=[qrf.b, cs1[ls].b], writes=[qt1.b])
                    x5 = qrf[:].rearrange("p (h b e d) -> p h b e d", h=2, b=2, e=2)
                    o5 = qt2[:].rearrange("p (h b e d) -> p h b e d", h=2, b=2, e=2)
                    sn = cs1[ls][:, 64:128].rearrange("p (b e d) -> p b e d", b=2, e=2)
                    for e in range(2):
                        P.op("dve", "tensor_tensor", o5[:, :, :, e, :], x5[:, :, :, 1 - e, :],
                             sn[:, None, :, e, :].broadcast_to([128, 2, 2, 16]), ALU.mult,
                             reads=[qrf.b, cs1[ls].b], **({"writes": [qt2.b]} if e == 0 else {"accs": [qt2.b]}))
                    P.op("dve", "tensor_tensor", qrb[z][:], qt1[:], qt2[:], ALU.add, reads=[qt1.b, qt2.b], writes=[qrb[z].b])
                    def tail(z=z, g=g, j=j):
                        tbk = 4 + z
                        transposes(xb7[z], 2, [tbk])
                        evac(st7[z][:].rearrange("p a b -> p (a b)"), psb[tbk][:, 0:256], reads=[psB[tbk]], writes=[st7[z].b])
                        P.dma("sp", ds_k[z], qnT[2 * g:2 * g + 2, :, j * 128:(j + 1) * 128].rearrange("h d t -> d h t"), st7[z][:],
                              reads=[st7[z].b])
                        P.op("pe", "matmul", psb[tbk][:, 256:384], qrb[z][:], ident[:], start=True, stop=True,
                             reads=[qrb[z].b, ident.b], accs=[psB[tbk]])
                        evac(sr7[z][:], psb[tbk][:, 256:384], reads=[psB[tbk]], writes=[sr7[z].b])
                        P.dma("sp", ds_q[z], qrT[2 * g:2 * g + 2].rearrange("h d t -> (h d) t")[:, j * 128:(j + 1) * 128], sr7[z][:],
                              reads=[sr7[z].b])
                    pending.append(tail)
                    u += 1
            while pending:
                pending.pop(0)()
            P.flush()
        with ExitStack() as ph:
            O1 = T(ph, "O1", [128, NQT, D], BF16)
            with ExitStack() as p8:
                kn = [T(p8, f"kn{i}", [128, NKV], BF16) for i in range(2)]
                v1 = [T(p8, f"v1_{i}", [128, NKT, 129], BF16) for i in range(2)]
                qn = [T(p8, f"qn{i}", [128, NQ], BF16) for i in range(2)]
                qr = [T(p8, f"qr{i}", [64, NQ], BF16) for i in range(2)]
                kr = T(p8, "kr", [64, NKV], BF16)
                pTs = [T(p8, f"mpT{i}", [128, 512], BF16) for i in range(3)]
                rinv = [T(p8, f"mrinv{i}", [128, 1], F32) for i in range(2)]
                dsl = [[P.dsem("x") for _ in range(4)] for _ in range(2)]
                for i, (c0, c1, src) in enumerate(kr_srcs):
                    P.dma("sp", ds_misc, kr[:, c0:c1], src, **({"writes": [kr.b]} if i == 0 else {"accs": [kr.b]}))
                for i in range(2):
                    P.op("dve", "memset", v1[i][:, :, 128:129], 1.0, writes=[v1[i].b])
                fcount = [0]
                for h in range(16):
                    s = h % 2
                    P.dma("sp", dsl[s][0], kn[s][:], knT[h], writes=[kn[s].b])
                    P.dma("sp", dsl[s][1], v1[s][:, :, 0:128], V1[h].rearrange("(kt p) d -> p kt d", p=128), writes=[v1[s].b])
                    P.dma("sp", dsl[s][2], qn[s][:], qnT[h], writes=[qn[s].b])
                    P.dma("sp", dsl[s][3], qr[s][:], qrT[h], writes=[qr[s].b])
                    for (q0, nq) in ((0, 512), (512, 512), (1024, 512), (1536, 512), (2048, 128)):
                        def fin(sub, ob, q0=q0, h=h):
                            z = fcount[0] % 2
                            fcount[0] += 1
                            tile = q0 // 128 + sub
                            P.op("dve", "reciprocal", rinv[z][:], psb[ob][:, 128:129], reads=[psB[ob]], writes=[rinv[z].b])
                            P.op("dve", "tensor_scalar", O1[:, tile, h * 128:(h + 1) * 128], psb[ob][:, 0:128], rinv[z][:, 0:1], None,
                                 ALU.mult, reads=[psB[ob], rinv[z].b], accs=[O1.b])
                        attn_dense([(kn[s], 128, qn[s]), (kr, 64, qr[s])], v1[s], list(range(NKT)), q0, nq, 128,
                                   [0, 1, 2], pTs, [3, 4, 5, 6], fin)
                P.flush()
            with ExitStack() as p9:
                zt = T(p9, "zt9", [128, 16, 1], BF16)
                P.op("dve", "memset", zt[:], 0.0, writes=[zt.b])
                hl = hTf_lat.rearrange("(k p) t -> p k t", p=128)
                P.dma("sp", ds_misc, hl[:, :, 0:1], zt[:], reads=[zt.b], allow_slow_non_contiguous=True)
                post_attn(p9, O1, NQT, NQT, mla_w_out, 1,
                          lambda j: x1q[j * 128:(j + 1) * 128, :],
                          lambda j: xmid1[j * 128:(j + 1) * 128, :],
                          lambda j: hl[:, :, 1 + j * 128:1 + (j + 1) * 128])
                P.flush()
        with ExitStack() as p10:
            units = [{"src": hTf_lat, "c0": 256 * u, "n": 256, "res": xmid1[256 * u:256 * (u + 1), :],
                      "dst": out[256 * u:256 * (u + 1), :], "row": 0} for u in range(8)]
            ffn(p10, 1, [units[2 * i:2 * i + 2] for i in range(4)])
            P.flush()
    top.close()
    return nc


_UPTO = [99]


def _perm(s):
    L = np.arange(SEQ)
    return L if s == 0 else (SEQ - 1 - L)


def _rope_tables(tok, half):
    freqs = (np.float32(10000.0) ** (-np.arange(half, dtype=np.float32) / np.float32(half))).astype(np.float32)
    r = (tok // 64).astype(np.float32)[:, None] * freqs[None, :]
    c = (tok % 64).astype(np.float32)[:, None] * freqs[None, :]
    cr, sr, ccs, sc = np.cos(r), np.sin(r), np.cos(c), np.sin(c)
    cos = np.concatenate([cr, cr, ccs, ccs], 1)
    sin = np.concatenate([-sr, sr, -sc, sc], 1)
    return np.concatenate([cos, sin], 1).astype(np.float32)


def _bias_tables(rpb, s):
    perm = _perm(s)
    out = np.empty((8, 128, 3, 5, 128), np.float32)
    p = np.arange(128)
    for v in range(3):
        tq = perm[128 * v + p]
        r, c = tq // 64, tq % 64
        rs0 = np.clip(r - 4, 0, 56)
        cs0 = np.clip(c - 8, 0, 48)
        for i in range(5):
            tk = perm[128 * i + p]
            kr, kc = tk // 64, tk % 64
            okr = (kr[:, None] >= rs0[None, :]) & (kr[:, None] < rs0[None, :] + 8)
            okc = (kc[:, None] >= cs0[None, :]) & (kc[:, None] < cs0[None, :] + 16)
            dr = np.clip(kr[:, None] - r[None, :] + 7, 0, 14)
            dc = np.clip(kc[:, None] - c[None, :] + 15, 0, 30)
            ok = okr & okc
            for h in range(8):
                out[h, :, v, i, :] = np.where(ok, rpb[h][dr, dc], np.float32(NEG))
    return out.reshape(8, 128, 3 * 5 * 128)


def _smalls(inp):
    v = np.zeros((1, NSM), np.float32)

    def put(name, arr):
        o, n = SM[name]
        a = np.asarray(arr, np.float32).reshape(-1)
        v[0, o:o + n] = np.tile(a, n // a.size)
    put("na_q", inp["na_q_norm"][0]); put("na_k", inp["na_k_norm"][0])
    put("df_q", inp["diff_q_norm"][0]); put("df_k", inp["diff_k_norm"][0])
    put("df_lam", inp["diff_lambda"][0]); put("df_sub", inp["diff_subln"][0])
    put("m_qa", inp["mla_q_a_norm"][0]); put("m_kva", inp["mla_kv_a_norm"][0])
    put("m_qn", inp["mla_q_nope_norm"][0]); put("m_qr", inp["mla_q_rope_norm"][0])
    put("m_kn", inp["mla_k_nope_norm"][0]); put("m_kr", inp["mla_k_rope_norm"][0])
    put("nmix0", inp["norm_mix"][0]); put("nmix1", inp["norm_mix"][1])
    put("nffn0", inp["norm_ffn"][0]); put("nffn1", inp["norm_ffn"][1])
    return v


def prep_shared(inp):
    sh = {
        "ident": np.eye(128, dtype=np.float32),
        "smalls": _smalls(inp),
        "ffn_w_in": np.ascontiguousarray(inp["ffn_w_in"], np.float32),
        "ffn_w_out": np.ascontiguousarray(inp["ffn_w_out"], np.float32),
        "ada_w": np.ascontiguousarray(inp["ada_w"], np.float32),
        "ada_b": np.ascontiguousarray(inp["ada_b"], np.float32).reshape(1, -1),
        "even_w_in": np.ascontiguousarray(inp["even_w_in"][0], np.float32),
        "even_w_out": np.ascontiguousarray(inp["even_w_out"][0], np.float32),
        "mla_w_down": np.ascontiguousarray(inp["mla_w_down"][0], np.float32),
        "mla_w_uq": np.ascontiguousarray(inp["mla_w_uq"][0], np.float32),
        "mla_w_ukv": np.ascontiguousarray(inp["mla_w_ukv"][0], np.float32),
        "mla_w_out": np.ascontiguousarray(inp["mla_w_out"][0], np.float32),
    }
    per_s = []
    for s in range(2):
        perm = _perm(s)
        conv = np.asarray(inp["ffn_conv"], np.float32)
        if s == 1:
            conv = conv[:, ::-1, :]
        cwt = conv.reshape(2, 3, 86, 128).transpose(0, 3, 2, 1).reshape(2, 128, 86 * 3)
        per_s.append({
            "cwt": np.ascontiguousarray(cwt),
            "rope0": _rope_tables(perm, 32),
            "rope1": _rope_tables(perm[:NQ], 16),
            "biasT": _bias_tables(np.asarray(inp["na_rpb"][0], np.float32), s),
        })
    return sh, per_s


def prep_core(inp, sh, per_s, core):
    b, s = core // 2, core % 2
    perm = _perm(s)
    x = np.asarray(inp["x"], np.float32)
    ctx = np.asarray(inp["ctx"], np.float32)
    cc = np.empty((128, 16, 2), np.float32)
    cc[:, :, 0] = np.asarray(inp["c"], np.float32)[b].reshape(16, 128).T
    cc[:, :, 1] = np.asarray(inp["c_ctx"], np.float32).reshape(16, 128).T
    m = dict(sh)
    m.update(per_s[s])
    m["xl"] = np.ascontiguousarray(x[b][perm])
    m["cl"] = np.ascontiguousarray(ctx[b] if s == 0 else ctx[b][::-1])
    m["cc"] = cc.reshape(128, 32)
    return m


A_INPUTS = ("ident", "smalls", "ffn_w_in", "ffn_w_out", "cwt", "xl", "cl", "cc", "ada_w", "ada_b", "even_w_in", "even_w_out",
            "rope0", "rope1", "biasT", "mla_w_down")
B_INPUTS = ("ident", "smalls", "ffn_w_in", "ffn_w_out", "cwt", "mla_w_uq", "mla_w_ukv", "mla_w_out", "rope1")


def _stage_inputs(m, names, layer):
    d = {}
    for k in names:
        v = m[k]
        if k in ("ffn_w_in", "ffn_w_out", "cwt"):
            v = v[layer:layer + 1]
        d[k] = v
    return d


def run_stage_A(inputs, cores, sh=None, per_s=None):
    if sh is None:
        sh, per_s = prep_shared(inputs)
    nc = build_program("A")
    in_maps = [_stage_inputs(prep_core(inputs, sh, per_s, c), A_INPUTS, 0) for c in cores]
    res = run_bass_kernel_spmd(nc, in_maps, core_ids=list(range(len(cores))))
    return res.results


def run_stage_B(inputs, cores, resA, sh=None, per_s=None):
    if sh is None:
        sh, per_s = prep_shared(inputs)
    nc = build_program("B")
    in_maps = []
    for c in cores:
        m = dict(sh)
        m.update(per_s[c % 2])
        d = _stage_inputs(m, B_INPUTS, 1)
        ra, rp = resA[c], resA[c ^ 1]
        d["modS"] = ra["modS"]
        d["x1q"] = ra["x1q"]
        d["qlatT"] = ra["qlatT"]
        d["kvx_all"] = np.ascontiguousarray(np.concatenate([ra["kvx_own"], rp["kvx_own"], ra["kvx_ctx"]], axis=1))
        in_maps.append(d)
    res = run_bass_kernel_spmd(nc, in_maps, core_ids=list(range(len(cores))))
    return res.results


FUSED = True
AB_INPUTS = A_INPUTS + ("mla_w_uq", "mla_w_ukv", "mla_w_out")


def kernel_fused(inputs):
    sh, per_s = prep_shared(inputs)
    cores = list(range(8))
    nc = build_program("AB")
    in_maps = []
    for c in cores:
        m = prep_core(inputs, sh, per_s, c)
        in_maps.append({k: m[k] for k in AB_INPUTS})
    res = run_bass_kernel_spmd(nc, in_maps, core_ids=cores)
    out = np.empty((4, SEQ, D), np.float32)
    for c in cores:
        b, s = c // 2, c % 2
        out[b][_perm(s)[:NOWN]] = res.results[c]["out"]
    return out


def kernel(**inputs):
    inputs = {k: np.asarray(v) for k, v in inputs.items()}
    if FUSED:
        return kernel_fused(inputs)
    sh, per_s = prep_shared(inputs)
    cores = list(range(8))
    ra = run_stage_A(inputs, cores, sh, per_s)
    keep = ("modS", "x1q", "qlatT", "kvx_own", "kvx_ctx")
    resA = {c: {k: ra[c][k] for k in keep} for c in cores}
    del ra
    rb = run_stage_B(inputs, cores, resA, sh, per_s)
    out = np.empty((4, SEQ, D), np.float32)
    for c in cores:
        b, s = c // 2, c % 2
        out[b][_perm(s)[:NOWN]] = rb[c]["out"]
    return out
```
